# Optimizing a Trainium2 kernel written in Bass

```python
import math
import jax, jax.numpy as jnp
from jax import lax
import numpy as np

D_MODEL = 1024
BATCH = 8
SEQ = 4096
DEPTH = 1
DEC_BATCH = 32
DEC_SEQ = 1
PAST_LEN = 16384
PAGE_SIZE = 128

D_MIX = D_MODEL
W_A = D_MIX // 2
W_B = D_MIX - W_A
DH_A = 64
E_A = 2 * DH_A
H_A = W_A // E_A
G_B = 4
C_B = W_B // G_B
CHUNK = 128
H_M = 4
DH_M = D_MODEL // H_M
N_MEM = 256
D_FF = ((8 * D_MODEL // 3 + 127) // 128) * 128
NB = 32
MAX_DIST = 128
BLOCK_Q = 128
MIX_IN = 3 * W_A + 2 * W_B
LN_EPS = 1e-5
NEG_INF = -1e30
ALPHA = (2 * DEPTH) ** 0.25
BETA = (8 * DEPTH) ** -0.25

kernel_name = 'hymba_diffattn_sgu_macaron_deepnorm_step'


def layer_norm(x, g, b):
    xf = x.astype(jnp.float32)
    mu = jnp.mean(xf, -1, keepdims=True)
    var = jnp.mean(jnp.square(xf - mu), -1, keepdims=True)
    y = (xf - mu) * lax.rsqrt(var + LN_EPS) * g.astype(jnp.float32) + b.astype(jnp.float32)
    return y.astype(x.dtype)


def swiglu(x, w_in, w_out):
    a, b = jnp.split(x @ w_in, 2, axis=-1)
    return (jax.nn.silu(a) * b) @ w_out


def rel_bucket(q_pos, k_pos):
    n = jnp.maximum(q_pos[:, None] - k_pos[None, :], 0)
    max_exact = NB // 2
    nf = jnp.maximum(n, 1).astype(jnp.float32)
    large = max_exact + (jnp.log(nf / max_exact) / math.log(MAX_DIST / max_exact)
                         * (NB - max_exact)).astype(jnp.int32)
    large = jnp.minimum(large, NB - 1)
    return jnp.where(n < max_exact, n, large)


def rel_bias_for(table, q_pos, k_pos):
    return jnp.transpose(table.astype(jnp.float32)[rel_bucket(q_pos, k_pos)], (2, 0, 1))


def diff_core(q, k, v, bias, mask, lam):
    s = jnp.einsum('bqhcd,bkhcd->bhcqk', q, k).astype(jnp.float32) * (DH_A ** -0.5) + bias[None, :, None]
    s = jnp.where(mask, s, NEG_INF)
    p = jax.nn.softmax(s, axis=-1)
    w = p[:, :, 0] - lam * p[:, :, 1]
    return jnp.einsum('bhqk,bkhe->bqhe', w.astype(v.dtype), v)


def diff_lambda(lq1, lk1, lq2, lk2, lambda_init):
    f = jnp.float32
    return (jnp.exp(jnp.sum(lq1.astype(f) * lk1.astype(f)))
            - jnp.exp(jnp.sum(lq2.astype(f) * lk2.astype(f))) + lambda_init)


def diff_head_out(o, g, lambda_init):
    of = o.astype(jnp.float32)
    y = of * lax.rsqrt(jnp.mean(of * of, -1, keepdims=True) + LN_EPS) * g.astype(jnp.float32) * (1.0 - lambda_init)
    B, T = o.shape[:2]
    return y.astype(o.dtype).reshape(B, T, W_A)


def prompt_attend(table):
    def attend(q, k, v, lam):
        B, S = q.shape[:2]
        nb = S // BLOCK_Q
        qb = jnp.swapaxes(q.reshape(B, nb, BLOCK_Q, H_A, 2, DH_A), 0, 1)
        k_pos = jnp.arange(S)

        def one(args):
            q_blk, i = args
            q_pos = i * BLOCK_Q + jnp.arange(BLOCK_Q)
            mask = q_pos[:, None] >= k_pos[None, :]
            return diff_core(q_blk, k, v, rel_bias_for(table, q_pos, k_pos), mask, lam)

        o = lax.map(one, (qb, jnp.arange(nb)))
        return jnp.swapaxes(o, 0, 1).reshape(B, S, H_A, E_A)
    return attend


def sample_attend(ck, cv, page_table, table):
    def attend(q, k, v, lam):
        Bd, Tn = q.shape[:2]
        k_past = ck[page_table].reshape(Bd, -1, H_A, 2, DH_A)
        v_past = cv[page_table].reshape(Bd, -1, H_A, E_A)
        past = k_past.shape[1]
        k_all = jnp.concatenate([k_past, k], axis=1)
        v_all = jnp.concatenate([v_past, v], axis=1)
        q_pos = past + jnp.arange(Tn)
        k_pos = jnp.arange(past + Tn)
        mask = k_pos[None, :] <= q_pos[:, None]
        return diff_core(q, k_all, v_all, rel_bias_for(table, q_pos, k_pos), mask, lam)
    return attend


def causal_sgu(u, v, ln_g, ln_b, w, b):
    u = jax.nn.gelu(u, approximate=False)
    v = layer_norm(jax.nn.gelu(v, approximate=False), ln_g, ln_b)
    B, T = v.shape[:2]
    Tp = -(-T // CHUNK) * CHUNK
    vp = jnp.pad(v, ((0, 0), (0, Tp - T), (0, 0))).reshape(B, Tp // CHUNK, CHUNK, G_B, C_B)
    mixed = (jnp.einsum('gts,bnsgc->bntgc', jnp.tril(w), vp)
             + jnp.swapaxes(b, 0, 1)[None, None, :, :, None])
    return u * mixed.reshape(B, Tp, W_B)[:, :T], v


def mem_kv(mem, w):
    B, N = mem.shape[:2]
    kv = (mem @ w).reshape(B, N, 2, H_M, DH_M)
    return kv[:, :, 0], kv[:, :, 1]


def cross_attn(x, mk, mv, wq, wo):
    B, T = x.shape[:2]
    q = (x @ wq).reshape(B, T, H_M, DH_M)
    s = jnp.einsum('bqhd,bmhd->bhqm', q, mk).astype(jnp.float32) * (DH_M ** -0.5)
    p = jax.nn.softmax(s, axis=-1)
    o = jnp.einsum('bhqm,bmhd->bqhd', p.astype(mv.dtype), mv).reshape(B, T, D_MODEL)
    return o @ wo


def trunk_layer(x, mk, mv, attend, lam, lambda_init, ln_g, ln_b, f1_in, f1_out, w_mix_in, w_mix_out,
                subln_g, sgu_ln_g, sgu_ln_b, sgu_w, sgu_b, xq_w, xo_w, f2_in, f2_out):
    x = layer_norm(ALPHA * x + 0.5 * swiglu(x, f1_in, f1_out), ln_g[0], ln_b[0])
    h = x @ w_mix_in
    B, T = h.shape[:2]
    q = h[..., :W_A].reshape(B, T, H_A, 2, DH_A)
    k = h[..., W_A:2 * W_A].reshape(B, T, H_A, 2, DH_A)
    v = h[..., 2 * W_A:3 * W_A].reshape(B, T, H_A, E_A)
    u = h[..., 3 * W_A:3 * W_A + W_B]
    vb = h[..., 3 * W_A + W_B:]
    oa = diff_head_out(attend(q, k, v, lam), subln_g, lambda_init)
    ob, vn = causal_sgu(u, vb, sgu_ln_g, sgu_ln_b, sgu_w, sgu_b)
    x = layer_norm(ALPHA * x + jnp.concatenate([oa, ob], axis=-1) @ w_mix_out, ln_g[1], ln_b[1])
    x = layer_norm(ALPHA * x + cross_attn(x, mk, mv, xq_w, xo_w), ln_g[2], ln_b[2])
    x = layer_norm(ALPHA * x + 0.5 * swiglu(x, f2_in, f2_out), ln_g[3], ln_b[3])
    return x, k.reshape(B, T, H_A, E_A), v, vn


def setup_inputs(seed: int = 0) -> dict:
    key = jax.random.key(seed)
    ks = jax.random.split(key, 32)
    f32 = jnp.float32
    n_pages = PAST_LEN // PAGE_SIZE
    n_used = DEC_BATCH * n_pages
    n_phys = (5 * n_used + 3) // 4
    sd = D_MODEL ** -0.5

    def nrm(k, shape, scale=1.0):
        return jax.random.normal(k, shape, f32) * scale

    page_table = jax.random.permutation(ks[7], n_phys)[:n_used].reshape(DEC_BATCH, n_pages).astype(jnp.int32)
    return {
        'x_prompt': nrm(ks[0], (BATCH, SEQ, D_MODEL)),
        'x_sample': nrm(ks[1], (DEC_BATCH, DEC_SEQ, D_MODEL)),
        'mem_prompt': nrm(ks[2], (BATCH, N_MEM, D_MODEL)),
        'cache_k': nrm(ks[3], (DEPTH, n_phys, PAGE_SIZE, H_A, E_A)),
        'cache_v': nrm(ks[4], (DEPTH, n_phys, PAGE_SIZE, H_A, E_A)),
        'cache_mem_k': nrm(ks[5], (DEPTH, DEC_BATCH, N_MEM, H_M, DH_M)),
        'cache_mem_v': nrm(ks[6], (DEPTH, DEC_BATCH, N_MEM, H_M, DH_M)),
        'page_table': page_table,
        'rel_bias': nrm(ks[8], (NB, H_A), 0.5),
        'ln_g': 1.0 + nrm(ks[9], (DEPTH, 4, D_MODEL), 0.02),
        'ln_b': nrm(ks[10], (DEPTH, 4, D_MODEL), 0.02),
        'ffn1_w_in': nrm(ks[11], (DEPTH, D_MODEL, 2 * D_FF), sd),
        'ffn1_w_out': nrm(ks[12], (DEPTH, D_FF, D_MODEL), D_FF ** -0.5 * BETA),
        'w_mix_in': jnp.concatenate([nrm(ks[15], (DEPTH, D_MODEL, 2 * W_A), sd),
                                     nrm(ks[16], (DEPTH, D_MODEL, W_A), sd * BETA),
                                     nrm(ks[17], (DEPTH, D_MODEL, 2 * W_B), sd)], axis=-1),
        'w_mix_out': nrm(ks[18], (DEPTH, D_MIX, D_MODEL), D_MIX ** -0.5 * BETA),
        'lambda_q1': nrm(ks[19], (DEPTH, DH_A), 0.1),
        'lambda_k1': nrm(ks[20], (DEPTH, DH_A), 0.1),
        'lambda_q2': nrm(ks[21], (DEPTH, DH_A), 0.1),
        'lambda_k2': nrm(ks[22], (DEPTH, DH_A), 0.1),
        'subln_g': 1.0 + nrm(ks[23], (DEPTH, E_A), 0.02),
        'sgu_ln_g': 1.0 + nrm(ks[24], (DEPTH, W_B), 0.02),
        'sgu_ln_b': nrm(ks[25], (DEPTH, W_B), 0.02),
        'sgu_w': nrm(ks[26], (DEPTH, G_B, CHUNK, CHUNK), CHUNK ** -0.5),
        'sgu_b': 1.0 + nrm(ks[27], (DEPTH, G_B, CHUNK), 0.1),
        'xq_w': nrm(ks[28], (DEPTH, D_MODEL, D_MODEL), sd),
        'xkv_w': jnp.concatenate([nrm(ks[29], (DEPTH, D_MODEL, D_MODEL), sd),
                                  nrm(ks[30], (DEPTH, D_MODEL, D_MODEL), sd * BETA)], axis=-1),
        'xo_w': nrm(ks[31], (DEPTH, D_MODEL, D_MODEL), sd * BETA),
        'ffn2_w_in': nrm(ks[13], (DEPTH, D_MODEL, 2 * D_FF), sd),
        'ffn2_w_out': nrm(ks[14], (DEPTH, D_FF, D_MODEL), D_FF ** -0.5 * BETA),
    }


def reference(x_prompt, x_sample, mem_prompt, cache_k, cache_v, cache_mem_k, cache_mem_v, page_table,
              rel_bias, ln_g, ln_b, ffn1_w_in, ffn1_w_out, w_mix_in, w_mix_out,
              lambda_q1, lambda_k1, lambda_q2, lambda_k2, subln_g, sgu_ln_g, sgu_ln_b, sgu_w, sgu_b,
              xq_w, xkv_w, xo_w, ffn2_w_in, ffn2_w_out):
    yp, ys = x_prompt, x_sample
    k_p, v_p, mk_p, mv_p, k_s, v_s, g_s = [], [], [], [], [], [], []
    for l in range(DEPTH):
        lambda_init = 0.8 - 0.6 * math.exp(-0.3 * l)
        lam = diff_lambda(lambda_q1[l], lambda_k1[l], lambda_q2[l], lambda_k2[l], lambda_init)
        w = (ln_g[l], ln_b[l], ffn1_w_in[l], ffn1_w_out[l], w_mix_in[l], w_mix_out[l],
             subln_g[l], sgu_ln_g[l], sgu_ln_b[l], sgu_w[l], sgu_b[l], xq_w[l], xo_w[l],
             ffn2_w_in[l], ffn2_w_out[l])
        mk, mv = mem_kv(mem_prompt, xkv_w[l])
        yp, kp, vp, _ = trunk_layer(yp, mk, mv, prompt_attend(rel_bias), lam, lambda_init, *w)
        k_p.append(kp); v_p.append(vp); mk_p.append(mk); mv_p.append(mv)
        ys, ksl, vsl, gsl = trunk_layer(ys, cache_mem_k[l], cache_mem_v[l],
                                        sample_attend(cache_k[l], cache_v[l], page_table, rel_bias),
                                        lam, lambda_init, *w)
        k_s.append(ksl); v_s.append(vsl); g_s.append(gsl)
    return (yp, ys, jnp.stack(k_p), jnp.stack(v_p), jnp.stack(mk_p), jnp.stack(mv_p),
            jnp.stack(k_s), jnp.stack(v_s), jnp.stack(g_s))
```

```python
import contextlib
import math
import numpy as np
import concourse.bass as bass
import concourse.mybir as mybir
from concourse.bass_utils import run_bass_kernel_spmd

F32 = mybir.dt.float32
BF16 = mybir.dt.bfloat16
I32 = mybir.dt.int32
AF = mybir.ActivationFunctionType
ALU = mybir.AluOpType
AX = mybir.AxisListType

D = 1024
SEQ = 4096
TT = 512
NT = SEQ // TT
DFF = 2816
NJ = DFF // 128
NPHYS = 5120
ALPHA = 2.0 ** 0.25
LN_EPS = 1e-5
EPS_EFF = LN_EPS / (ALPHA * ALPHA)
LAMBDA_INIT = 0.8 - 0.6 * math.exp(0.0)
NEG = -30000.0
NS = 4

ENGS = ("pe", "act", "dve", "pool", "sp")
SAME_ENGINE_SYNC = {"pe": False, "act": True, "dve": True, "pool": True, "sp": False}


class Buf:
    __slots__ = ("name", "w", "rs")

    def __init__(self, name=""):
        self.name = name
        self.w = None
        self.rs = {}


class Op:
    __slots__ = ("eng", "idx", "fn", "waits", "dma", "need_inc", "count")

    def __init__(self, eng, idx, fn):
        self.eng = eng
        self.idx = idx
        self.fn = fn
        self.waits = []
        self.dma = None
        self.need_inc = False
        self.count = None


class Prog:
    def __init__(self):
        self.streams = {e: [] for e in ENGS}
        self.dma_cnt = {}
        self.waited = {e: {} for e in ENGS}

    def op(self, eng, fn, reads=(), writes=(), dma_sem=None):
        st = self.streams[eng]
        o = Op(eng, len(st), fn)
        need = []
        for b in reads:
            if b.w is not None:
                need.append(b.w)
        for b in writes:
            if b.w is not None:
                need.append(b.w)
            need.extend(b.rs.values())
        wd = self.waited[eng]
        best = {}
        for t in need:
            if t[0] == "op":
                p = t[1]
                if p.eng == eng and not SAME_ENGINE_SYNC[eng]:
                    continue
                key = "e_" + p.eng
                if wd.get(key, -1) >= p.idx:
                    continue
                if key not in best or best[key][1].idx < p.idx:
                    best[key] = t
            else:
                _, s, v = t
                if wd.get(s, 0) >= v:
                    continue
                if s not in best or best[s][2] < v:
                    best[s] = t
        for key, t in best.items():
            wd[key] = t[1].idx if t[0] == "op" else t[2]
            if t[0] == "op":
                t[1].need_inc = True
        o.waits = list(best.values())
        if dma_sem is not None:
            self.dma_cnt[dma_sem] = self.dma_cnt.get(dma_sem, 0) + 16
            o.dma = (dma_sem, self.dma_cnt[dma_sem])
            tok = ("dma", dma_sem, self.dma_cnt[dma_sem])
            rkey = dma_sem
        else:
            tok = ("op", o)
            rkey = "e_" + eng
        if fn is not None:
            for b in reads:
                b.rs[rkey] = tok
            for b in writes:
                b.w = tok
                b.rs = {}
        st.append(o)
        return tok

    def barrier(self):
        toks = []
        for e in ENGS:
            for p in reversed(self.streams[e]):
                if p.dma is None and p.fn is not None:
                    toks.append(("op", p))
                    break
        for s, v in self.dma_cnt.items():
            toks.append(("dma", s, v))
        for e in ENGS:
            for t in toks:
                if t[0] == "op" and t[1].eng == e:
                    continue
                b2 = Buf()
                b2.w = t
                self.op(e, None, reads=[b2])

    def emit(self, nc, final_wait_eng="sp"):
        for s, v in list(self.dma_cnt.items()):
            b2 = Buf()
            b2.w = ("dma", s, v)
            self.op(final_wait_eng, None, reads=[b2])
        for e in ENGS:
            c = 0
            for o in self.streams[e]:
                if o.need_inc:
                    c += 1
                    o.count = c
        with contextlib.ExitStack() as es:
            sems = {}
            for e in ENGS:
                sems["e_" + e] = es.enter_context(nc.semaphore("e_" + e))
            for s in self.dma_cnt:
                sems[s] = es.enter_context(nc.semaphore(s))
            block = es.enter_context(nc.Block())

            def run(eng_name):
                def body(eng):
                    for o in self.streams[eng_name]:
                        for t in o.waits:
                            if t[0] == "op":
                                eng.wait_ge(sems["e_" + t[1].eng], t[1].count)
                            else:
                                eng.wait_ge(sems[t[1]], t[2])
                        if o.fn is None:
                            continue
                        ins = o.fn(eng)
                        if o.dma is not None:
                            ins.then_inc(sems[o.dma[0]], 16)
                        elif o.need_inc:
                            ins.then_inc(sems["e_" + eng_name], 1)
                return body

            block.tensor(run("pe"))
            block.scalar(run("act"))
            block.vector(run("dve"))
            block.gpsimd(run("pool"))
            block.sync(run("sp"))


def _bucket_np(n):
    n = np.asarray(n, dtype=np.int64)
    nf = np.maximum(n, 1).astype(np.float32)
    large = 16 + (np.log(nf / np.float32(16)) / np.float32(math.log(8.0)) * np.float32(16)).astype(np.int32)
    large = np.minimum(large, 31)
    return np.where(n < 16, n, large)


def _host_consts():
    ident = np.eye(128, dtype=np.float32)
    tril = np.tril(np.ones((128, 128), dtype=np.float32))
    ohm = np.zeros((32, 383), dtype=np.float32)
    for m in range(383):
        n = m - 127
        if n >= 0:
            ohm[int(_bucket_np(n)), m] = 1.0
    maskrow = np.zeros((4, 383), dtype=np.float32)
    maskrow[:, :127] = NEG
    iota = np.arange(128, dtype=np.float32).reshape(128, 1)
    return {"c_ident": ident, "c_tril": tril, "c_ohm": ohm, "c_maskrow": maskrow, "c_iota": iota}


class Ctx:
    pass


def build_nc():
    nc = bass.Bass("TRN2", target_bir_lowering=False)
    P = Prog()

    def din(name, shape, dt=F32):
        return nc.dram_tensor(name, list(shape), dt, kind="ExternalInput").ap()

    def dout(name, shape, dt=F32):
        return nc.dram_tensor(name, list(shape), dt, kind="ExternalOutput").ap()

    def dscr(name, shape, dt=F32):
        return nc.dram_tensor(name, list(shape), dt, kind="Internal").ap()

    xp = din("xp", [SEQ, D])
    mem = din("mem", [256, D])
    rel_bias = din("rel_bias", [32, 4])
    ln_g = din("ln_g", [4, D])
    ln_b = din("ln_b", [4, D])
    w_f1i = din("ffn1_w_in", [D, 2 * DFF])
    w_f1o = din("ffn1_w_out", [DFF, D])
    w_mi = din("w_mix_in", [D, 2560])
    w_mo = din("w_mix_out", [D, D])
    lam_in = din("lam_in", [4, 64])
    subln_g = din("subln_g", [128, 1])
    sgu_ln_g = din("sgu_ln_g", [1, 512])
    sgu_ln_b = din("sgu_ln_b", [1, 512])
    sgu_w = din("sgu_w", [4, 128, 128])
    sgu_b = din("sgu_b", [1, 512])
    w_xq = din("xq_w", [D, D])
    w_xkv = din("xkv_w", [D, 2 * D])
    w_xo = din("xo_w", [D, D])
    w_f2i = din("ffn2_w_in", [D, 2 * DFF])
    w_f2o = din("ffn2_w_out", [DFF, D])
    c_ident = din("c_ident", [128, 128])
    c_tril = din("c_tril", [128, 128])
    c_ohm = din("c_ohm", [32, 383])
    c_maskrow = din("c_maskrow", [4, 383])
    c_iota = din("c_iota", [128, 1])
    xs = din("xs", [NS, D])
    cache_kv = din("cache_kv", [NPHYS * 128, 1024])
    cmk = din("cmk", [NS, 256, D])
    cmv = din("cmv", [NS, 256, D])
    ptab = din("ptab", [NS, 128], I32)

    y_p = dout("y_p", [SEQ, D])
    k_p = dout("k_p", [SEQ, 512])
    v_p = dout("v_p", [SEQ, 512])
    mk_p = dout("mk_p", [256, D])
    mv_p = dout("mv_p", [256, D])
    y_s = dout("y_s", [NS, D])
    k_s = dout("k_s", [NS, 512])
    v_s = dout("v_s", [NS, 512])
    g_s = dout("g_s", [NS, 512])

    NPIECE = 64
    wscr = dscr("wscr", [NPIECE, 128, 4096], BF16)
    dscr_d = dscr("dscr_d", [4, 383])
    dscr_f = dscr("dscr_f", [4, 128 * 383])
    Bwscr = [Buf("wscr%d" % i) for i in range(NPIECE)]

    with contextlib.ExitStack() as es:
        def sb(name, shape, dt):
            return es.enter_context(nc.sbuf_tensor(name, list(shape), dt))

        ident_f = sb("ident_f", [128, 128], F32)
        ident_b = sb("ident_b", [128, 128], BF16)
        ones_b = sb("ones_b", [128, 128], BF16)
        ones_f = sb("ones_f", [128, 128], F32)
        lng = sb("lng", [128, 4, 8], F32)
        lnb = sb("lnb", [128, 4, 8], F32)
        BT = sb("BT", [128, 4, 256], BF16)
        BTf = sb("BTf", [128, 4, 256], F32)
        far = sb("far", [128, 4], F32)
        neglam = sb("neglam", [128, 1], F32)
        sublg = sb("sublg", [128, 1], F32)
        sgug = sb("sgug", [128, 512], F32)
        sgub_ln = sb("sgub_ln", [128, 512], F32)
        sgub_row = sb("sgub_row", [1, 512], BF16)
        sgub_rowf = sb("sgub_rowf", [1, 512], F32)
        trilWT = sb("trilWT", [128, 4, 128], BF16)
        memKT = sb("memKT", [128, 8, 256], BF16)
        memV = sb("memV", [128, 2, 1024], BF16)
        KT2 = sb("KT", [128, 32 * 512], BF16)
        Vt2 = sb("Vt", [128, 32 * 512], BF16)
        KT = KT2[:, :].rearrange("p (t h k) -> p t h k", t=32, h=4)
        Vt = Vt2[:, :].rearrange("p (t f) -> p t f", t=32)
        BKT = [Buf("KT%d" % i) for i in range(32)]
        BVt = [Buf("Vt%d" % i) for i in range(32)]
        x = sb("x", [128, 8, TT], F32)
        xb = sb("xb", [128, 8, TT], BF16)
        hbuf2 = sb("hbuf", [128, NJ * TT], BF16)
        hbuf = hbuf2[:, :].rearrange("p (k t) -> p k t", k=NJ)
        sq = hbuf2[:, 0:8 * TT].rearrange("p (k t) -> p k t", k=8)
        Bx = [Buf("x%d" % i) for i in range(8)]
        Bxb = [Buf("xb%d" % i) for i in range(8)]
        Bh = [Buf("h%d" % i) for i in range(NJ)]
        Bsq = Bh[0:8]
        NSLOT = 3
        wslot = [sb("wslot%d" % i, [128, 4096], BF16) for i in range(NSLOT)]
        Bws = [Buf("ws%d" % i) for i in range(NSLOT)]
        wstage = [KT2[:, 8192:16384].bitcast(F32), Vt2[:, 8192:16384].bitcast(F32)]
        Bwst = [Buf("wst%d" % i) for i in range(2)]
        tmpf = [sb("tmpf%d" % i, [128, TT], F32) for i in range(4)]
        Btmp = [Buf("tmpf%d" % i) for i in range(4)]
        stg = [sb("stg%d" % i, [128, 1024], F32) for i in range(3)]
        Bstg = [Buf("stg%d" % i) for i in range(3)]
        qT = hbuf2[:, 16 * TT:20 * TT].rearrange("p (k t) -> p k t", k=4)
        BqT = Bh[16:20]
        uT = hbuf2[:, 8 * TT:16 * TT].bitcast(F32).rearrange("p (k t) -> p k t", k=4)
        BuT2 = [[Bh[8 + 2 * g], Bh[9 + 2 * g]] for g in range(4)]
        BuT = Bh[8:16]
        vn = sb("vn", [128, 4, 512], BF16)
        Bvn = [Buf("vn%d" % i) for i in range(4)]
        catT = sq
        Bcat = Bh[0:8]
        ET = [sb("ET%d" % i, [128, TT], BF16) for i in range(4)]
        BET = [Buf("ET%d" % i) for i in range(4)]
        oacc = sb("oacc", [128, TT], F32)
        Boacc = Buf("oacc")
        oacc2 = sb("oacc2", [128, TT], F32)
        Boacc2 = Buf("oacc2")
        col = [sb("col%d" % i, [128, 8], F32) for i in range(4)]
        Bcol = [Buf("col%d" % i) for i in range(4)]
        psum = [es.enter_context(nc.psum_tensor("ps%d" % i, [128, 512], F32)) for i in range(8)]
        Bps = [Buf("ps%d" % i) for i in range(8)]

        rr = {"mm": 0, "sc": 0, "pv": 0, "tmp": 0, "stg": 0, "et": 0, "ws": 0, "wst": 0, "col": 0}
        POOLS = {"mm": [0, 1, 2, 3, 4, 5], "sc": [2, 3], "pv": [4, 5, 6, 7]}

        def set_dense():
            POOLS["mm"] = [0, 1, 2, 3, 4, 5]

        def set_attn():
            POOLS["mm"] = [0, 1]

        def bank(pool):
            lst = POOLS[pool]
            i = lst[rr[pool] % len(lst)]
            rr[pool] += 1
            return psum[i], Bps[i]

        def rot(key, arrs, bufs):
            i = rr[key] % len(arrs)
            rr[key] += 1
            return arrs[i], bufs[i]

        def mm(out, lhsT, rhs, start, stop, reads, wbuf, skip=False):
            if skip:
                P.op("pe", lambda e: e.matmul(out, lhsT=lhsT, rhs=rhs, start=start, stop=stop, skip_group_check=True),
                     reads=reads, writes=[wbuf])
            else:
                P.op("pe", lambda e: e.matmul(out, lhsT=lhsT, rhs=rhs, start=start, stop=stop),
                     reads=reads, writes=[wbuf])

        def tr(out, in_, reads, wbuf):
            P.op("pe", lambda e: e.transpose(out, in_, ident_f[:]), reads=reads + [Bconst], writes=[wbuf])

        def act(out, in_, func, reads, writes, bias=None, scale=1.0):
            if bias is None:
                P.op("act", lambda e: e.activation(out=out, in_=in_, func=func, scale=scale),
                     reads=reads, writes=writes)
            else:
                P.op("act", lambda e: e.activation(out=out, in_=in_, func=func, bias=bias, scale=scale),
                     reads=reads, writes=writes)

        def tt(eng, out, in0, in1, op, reads, writes):
            P.op(eng, lambda e: e.tensor_tensor(out=out, in0=in0, in1=in1, op=op), reads=reads, writes=writes)

        def ts(eng, out, in0, s1, s2, op0, op1, reads, writes):
            if s2 is None:
                P.op(eng, lambda e: e.tensor_scalar(out=out, in0=in0, scalar1=s1, scalar2=None, op0=op0),
                     reads=reads, writes=writes)
            else:
                P.op(eng, lambda e: e.tensor_scalar(out=out, in0=in0, scalar1=s1, scalar2=s2, op0=op0, op1=op1),
                     reads=reads, writes=writes)

        def stt(eng, out, in0, scalar, in1, op0, op1, reads, writes):
            P.op(eng, lambda e: e.scalar_tensor_tensor(out=out, in0=in0, scalar=scalar, in1=in1, op0=op0, op1=op1),
                 reads=reads, writes=writes)

        def cp(eng, out, in_, reads, writes):
            if eng == "act":
                P.op("act", lambda e: e.copy(out=out, in_=in_), reads=reads, writes=writes)
            else:
                P.op(eng, lambda e: e.tensor_copy(out=out, in_=in_), reads=reads, writes=writes)

        def dma(eng, out, in_, reads, writes, sem, nonc=False):
            if nonc:
                P.op(eng, lambda e: e.dma_start(out=out, in_=in_, allow_slow_non_contiguous=True),
                     reads=reads, writes=writes, dma_sem=sem)
            else:
                P.op(eng, lambda e: e.dma_start(out=out, in_=in_), reads=reads, writes=writes, dma_sem=sem)

        Bconst = Buf("const")

        dma("sp", ident_f[:], c_ident, [], [Bconst], "d_c1")
        cp("dve", ident_b[:], ident_f[:], [Bconst], [Bconst])
        P.op("pool", lambda e: e.memset(ones_b[:], 1.0), writes=[Bconst])
        P.op("pool", lambda e: e.memset(ones_f[:], 1.0), writes=[Bconst])
        dma("sp", lng[:], ln_g.rearrange("i (k p) -> p i k", p=128), [], [Bconst], "d_c2", nonc=True)
        dma("sp", lnb[:], ln_b.rearrange("i (k p) -> p i k", p=128), [], [Bconst], "d_c3", nonc=True)
        dma("sp", sublg[:], subln_g, [], [Bconst], "d_c4", nonc=True)
        ts("dve", sublg[:], sublg[:], 1.0 - LAMBDA_INIT, None, ALU.mult, None, [Bconst], [Bconst])
        dma("sp", sgug[:], sgu_ln_g.to_broadcast([128, 512]), [], [Bconst], "d_c5", nonc=True)
        dma("sp", sgub_ln[:], sgu_ln_b.to_broadcast([128, 512]), [], [Bconst], "d_c6", nonc=True)
        dma("sp", sgub_rowf[:], sgu_b, [], [Bconst], "d_c7")
        cp("dve", sgub_row[:], sgub_rowf[:], [Bconst], [Bconst])
        lam_t = sb("lam_t", [128, 4, 64], F32)
        lam_s = sb("lam_s", [128, 4], F32)
        dma("sp", lam_t[:], bass.AP(lam_in.tensor, 0, [[0, 128], [64, 4], [1, 64]]), [], [Bconst], "d_c8", nonc=True)
        tt("dve", lam_t[:, 0, :], lam_t[:, 0, :], lam_t[:, 1, :], ALU.mult, [Bconst], [Bconst])
        tt("dve", lam_t[:, 2, :], lam_t[:, 2, :], lam_t[:, 3, :], ALU.mult, [Bconst], [Bconst])
        P.op("dve", lambda e: e.tensor_reduce(out=lam_s[:, 0:1], in_=lam_t[:, 0, :], axis=AX.X, op=ALU.add),
             reads=[Bconst], writes=[Bconst])
        P.op("dve", lambda e: e.tensor_reduce(out=lam_s[:, 1:2], in_=lam_t[:, 2, :], axis=AX.X, op=ALU.add),
             reads=[Bconst], writes=[Bconst])
        act(lam_s[:, 2:4], lam_s[:, 0:2], AF.Exp, [Bconst], [Bconst])
        tt("dve", neglam[:], lam_s[:, 3:4], lam_s[:, 2:3], ALU.subtract, [Bconst], [Bconst])
        ts("dve", neglam[:], neglam[:], -LAMBDA_INIT, None, ALU.add, None, [Bconst], [Bconst])

        if STOP == 1:
            P.emit(nc); return nc
        tab = sb("tab", [32, 4], F32)
        ohm = sb("ohm", [32, 383], F32)
        mrow = sb("mrow", [4, 383], F32)
        dvec = sb("dvec", [4, 383], F32)
        dma("sp", tab[:], rel_bias, [], [Bconst], "d_c9")
        dma("sp", ohm[:], c_ohm, [], [Bconst], "d_c10")
        dma("sp", mrow[:], c_maskrow, [], [Bconst], "d_c11")
        pb, pbuf = bank("mm")
        mm(pb[0:4, 0:383], tab[:], ohm[:], True, True, [Bconst], pbuf)
        cp("dve", dvec[:], pb[0:4, 0:383], [pbuf], [Bconst])
        ts("dve", dvec[:], dvec[:], dvec[:, 382:383], None, ALU.subtract, None, [Bconst], [Bconst])
        tt("dve", dvec[:], dvec[:], mrow[:], ALU.add, [Bconst], [Bconst])
        Bdd = Buf("dscr_d")
        Bdf = Buf("dscr_f")
        dma("sp", dscr_d, dvec[:], [Bconst], [Bdd], "d_c12")
        for h in range(4):
            dma("sp", bass.AP(dscr_f.tensor, h * 128 * 383, [[383, 128], [1, 383]]),
                bass.AP(dscr_d.tensor, h * 383, [[0, 128], [1, 383]]), [Bdd], [Bdf], "d_c13", nonc=True)
        for h in range(4):
            dma("sp", BTf[:, h, :], bass.AP(dscr_f.tensor, h * 128 * 383 + 127, [[382, 128], [1, 256]]),
                [Bdf], [Bconst], "d_c14", nonc=True)
        cp("dve", BT[:], BTf[:], [Bconst], [Bconst])
        if STOP == 2:
            P.emit(nc); return nc
        dma("sp", far[:], bass.AP(rel_bias.tensor, 31 * 4, [[0, 128], [1, 4]]), [], [Bconst], "d_c15", nonc=True)
        wtmp = sb("wtmp", [128, 4, 128], F32)
        trm = sb("trm", [128, 128], F32)
        dma("sp", wtmp[:], sgu_w.rearrange("g t s -> t g s"), [], [Bconst], "d_c16", nonc=True)
        dma("sp", trm[:], c_tril, [], [Bconst], "d_c17")
        for g in range(4):
            tt("dve", wtmp[:, g, :], wtmp[:, g, :], trm[:], ALU.mult, [Bconst], [Bconst])
            pb, pbuf = bank("mm")
            tr(pb[:, 0:128], wtmp[:, g, :], [Bconst], pbuf)
            cp("dve", trilWT[:, g, :], pb[:, 0:128], [pbuf], [Bconst])

        if STOP == 3:
            P.emit(nc); return nc
        piece_id = {}

        def get_piece(key, blocks, first):
            if key not in piece_id:
                piece_id[key] = len(piece_id)
            pid = piece_id[key]
            slot, bslot = rot("ws", wslot, Bws)
            tot = sum(nk * ncols for (_, _, nk, _, ncols, _) in blocks)
            if first:
                st, bst = rot("wst", wstage, Bwst)
                for (W, r0, nk, c0, ncols, off) in blocks:
                    src = W[r0:r0 + nk * 128, c0:c0 + ncols].rearrange("(k p) c -> p k c", p=128)
                    dst = st[:, off:off + nk * ncols].rearrange("p (k c) -> p k c", c=ncols)
                    dma("sp", dst, src, [], [bst], "d_wst%d" % (rr["wst"] % 2))
                ceng = ("act", "dve", "pool")[pid % 3]
                cp(ceng, slot[:, 0:tot], st[:, 0:tot], [bst], [bslot])
                dma("sp", wscr[pid][:, 0:tot], slot[:, 0:tot], [bslot], [Bwscr[pid]], "d_wsc%d" % (rr["ws"] % NSLOT))
            else:
                dma("sp", slot[:, 0:tot], wscr[pid][:, 0:tot], [Bwscr[pid]], [bslot], "d_ws%d" % (rr["ws"] % NSLOT))
            return slot, bslot

        ln_state = {}

        def ln_begin():
            ln_state["pend"] = None
            ln_state["cnt"] = 0

        def ln_stats(k, N, last):
            sqs, bsq_ = ln_state["sq%d" % k]
            mm(psum[6][:, 0:N], ones_b[:], xb[:, k, 0:N], k == 0, last, [Bconst, Bxb[k]], Bps[6])
            mm(psum[7][:, 0:N], ones_b[:], sqs[:, 0:N], k == 0, last, [Bconst, bsq_], Bps[7])

        def ln_pre(k, N):
            cp("pool", xb[:, k, 0:N], x[:, k, 0:N], [Bx[k]], [Bxb[k]])
            sqs, bsq_ = rot("et", ET, BET)
            act(sqs[:, 0:N], x[:, k, 0:N], AF.Square, [Bx[k]], [bsq_])
            ln_state["sq%d" % k] = (sqs, bsq_)
            if k >= 1:
                ln_stats(k - 1, N, False)

        def layer_norm(N, li):
            ln_stats(7, N, True)
            s1, b1 = psum[6], Bps[6]
            s2, b2 = psum[7], Bps[7]
            mean, bm = rot("tmp", tmpf, Btmp)
            msq, bq = rot("tmp", tmpf, Btmp)
            rstd, br = rot("tmp", tmpf, Btmp)
            ts("dve", mean[:, 0:N], s1[:, 0:N], 1.0 / D, None, ALU.mult, None, [b1], [bm])
            tt("dve", msq[:, 0:N], mean[:, 0:N], mean[:, 0:N], ALU.mult, [bm], [bq])
            stt("dve", rstd[:, 0:N], s2[:, 0:N], 1.0 / D, msq[:, 0:N], ALU.mult, ALU.subtract, [b2, bq], [br])
            ts("dve", rstd[:, 0:N], rstd[:, 0:N], EPS_EFF, None, ALU.add, None, [br], [br])
            act(rstd[:, 0:N], rstd[:, 0:N], AF.Sqrt, [br], [br])
            P.op("dve", lambda e: e.reciprocal(out=rstd[:, 0:N], in_=rstd[:, 0:N]), reads=[br], writes=[br])
            for k in range(8):
                eng = "pool" if k in (2, 5, 7) else "dve"
                tt(eng, x[:, k, 0:N], x[:, k, 0:N], mean[:, 0:N], ALU.subtract, [Bx[k], bm], [Bx[k]])
                tt(eng, x[:, k, 0:N], x[:, k, 0:N], rstd[:, 0:N], ALU.mult, [Bx[k], br], [Bx[k]])
                act(xb[:, k, 0:N], x[:, k, 0:N], AF.Identity, [Bx[k], Bconst], [Bxb[k]],
                    bias=lnb[:, li, k:k + 1], scale=lng[:, li, k:k + 1])
                ts("pool", x[:, k, 0:N], x[:, k, 0:N], lng[:, li, k:k + 1], lnb[:, li, k:k + 1], ALU.mult, ALU.add,
                   [Bx[k], Bconst], [Bx[k]])

        def ffn(N, Win, Wout, li, first, tag):
            for pi in range(11):
                slot, bs = get_piece((tag, "in", pi),
                                     [(Win, 0, 8, 256 * pi, 256, 0), (Win, 0, 8, DFF + 256 * pi, 256, 2048)], first)
                sv = slot[:, :].rearrange("p (a k c) -> p a k c", a=2, k=8)
                for jj in range(2):
                    j = 2 * pi + jj
                    A, bA = bank("mm")
                    for k in range(8):
                        mm(A[:, 0:N], sv[:, 0, k, jj * 128:(jj + 1) * 128], xb[:, k, 0:N], k == 0, k == 7,
                           [bs, Bxb[k]], bA)
                    Bm, bB = bank("mm")
                    for k in range(8):
                        mm(Bm[:, 0:N], sv[:, 1, k, jj * 128:(jj + 1) * 128], xb[:, k, 0:N], k == 0, k == 7,
                           [bs, Bxb[k]], bB)
                    t, bt = rot("tmp", tmpf, Btmp)
                    act(t[:, 0:N], A[:, 0:N], AF.Silu, [bA], [bt])
                    tt("dve", hbuf[:, j, 0:N], t[:, 0:N], Bm[:, 0:N], ALU.mult, [bt, bB], [Bh[j]])
            for c in range(8):
                slot, bs = get_piece((tag, "out", c), [(Wout, 0, NJ, 128 * c, 128, 0)], first)
                sv = slot[:, 0:NJ * 128].rearrange("p (k c) -> p k c", k=NJ)
                Y, bY = bank("mm")
                for k in range(NJ):
                    mm(Y[:, 0:N], sv[:, k, :], hbuf[:, k, 0:N], k == 0, k == NJ - 1, [bs, Bh[k]], bY)
                stt("dve", x[:, c, 0:N], Y[:, 0:N], 0.5 / ALPHA, x[:, c, 0:N], ALU.mult, ALU.add,
                    [bY, Bx[c]], [Bx[c]])
                ln_pre(c, N)
            layer_norm(N, li)

        def proj_res(N, W, src, Bsrc, tag, first):
            for pi in range(2):
                slot, bs = get_piece((tag, pi), [(W, 0, 8, 512 * pi, 512, 0)], first)
                sv = slot[:, :].rearrange("p (k c) -> p k c", k=8)
                for cc in range(4):
                    c = 4 * pi + cc
                    Y, bY = bank("mm")
                    for k in range(8):
                        mm(Y[:, 0:N], sv[:, k, cc * 128:(cc + 1) * 128], src[:, k, 0:N], k == 0, k == 7,
                           [bs, Bsrc[k]], bY)
                    stt("dve", x[:, c, 0:N], Y[:, 0:N], 1.0 / ALPHA, x[:, c, 0:N], ALU.mult, ALU.add,
                        [bY, Bx[c]], [Bx[c]])
                    ln_pre(c, N)

        def mem_phase():
            for s in range(2):
                st_, bst_ = rot("stg", stg, Bstg)
                dma("sp", st_[:, :], mem[s * 128:(s + 1) * 128, :], [], [bst_], "d_stg%d" % (rr["stg"] % 3))
                for kk in range(2):
                    pb, pbuf = bank("mm")
                    for q4 in range(4):
                        k = kk * 4 + q4
                        tr(pb[:, q4 * 128:(q4 + 1) * 128], st_[:, k * 128:(k + 1) * 128], [bst_], pbuf)
                    for q4 in range(4):
                        k = kk * 4 + q4
                        cp("dve" if kk else "act", hbuf[:, k, s * 128:(s + 1) * 128],
                           pb[:, q4 * 128:(q4 + 1) * 128], [pbuf], [Bh[k]])
            if STOP == 5:
                return
            for pi in range(4 if STOP != 6 else 1):
                slot, bs = rot("ws", wslot, Bws)
                src = w_xkv[:, 512 * pi:512 * (pi + 1)].rearrange("(k p) c -> p k c", p=128)
                P.op("pool", lambda e, slot=slot, src=src: e.dma_start(
                    out=slot[:, :].rearrange("p (k c) -> p k c", k=8), in_=src),
                    writes=[bs], dma_sem="d_ws%d" % (rr["ws"] % NSLOT))
                sv = slot[:, :].rearrange("p (k c) -> p k c", k=8)
                if pi < 2:
                    for cc in range(4):
                        j = 4 * pi + cc
                        pb, pbuf = bank("mm")
                        for k in range(8):
                            mm(pb[:, 0:256], sv[:, k, cc * 128:(cc + 1) * 128], hbuf[:, k, 0:256], k == 0, k == 7,
                               [bs, Bh[k]], pbuf)
                        cp("act", memKT[:, j, :], pb[:, 0:256], [pbuf], [Bconst])
                for s in range(2):
                    pb, pbuf = bank("mm")
                    for k in range(8):
                        mm(pb[:, :], hbuf[:, k, s * 128:(s + 1) * 128], sv[:, k, :], k == 0, k == 7,
                           [bs, Bh[k]], pbuf)
                    st_, bst_ = rot("stg", stg, Bstg)
                    cp("dve", st_[:, 0:512], pb[:, :], [pbuf], [bst_])
                    if pi >= 2:
                        cp("pool", memV[:, s, 512 * (pi - 2):512 * (pi - 1)], st_[:, 0:512], [bst_], [Bconst])
                        dma("sp", mv_p[s * 128:(s + 1) * 128, 512 * (pi - 2):512 * (pi - 1)], st_[:, 0:512],
                            [bst_], [], "d_stg%d" % (rr["stg"] % 3))
                    else:
                        dma("sp", mk_p[s * 128:(s + 1) * 128, 512 * pi:512 * (pi + 1)], st_[:, 0:512],
                            [bst_], [], "d_stg%d" % (rr["stg"] % 3))

        def cross_core(c0, c1):
            for hm in range(4):
                ets = []
                for mc in range(2):
                    S, bS = bank("sc")
                    for dc in range(2):
                        mm(S[:, c0:c1], memKT[:, hm * 2 + dc, mc * 128:(mc + 1) * 128], hbuf[:, hm * 2 + dc, c0:c1],
                           dc == 0, dc == 1, [Bconst, Bh[hm * 2 + dc]], bS)
                    et, bet = rot("et", ET, BET)
                    act(et[:, c0:c1], S[:, c0:c1], AF.Exp, [bS], [bet])
                    ets.append((et, bet))
                SM, bSM = bank("pv")
                for mc in range(2):
                    mm(SM[:, c0:c1], ones_b[:], ets[mc][0][:, c0:c1], mc == 0, mc == 1, [Bconst, ets[mc][1]], bSM)
                rc, brc = rot("tmp", tmpf, Btmp)
                P.op("dve", lambda e, rc=rc, SM=SM: e.reciprocal(out=rc[:, c0:c1], in_=SM[:, c0:c1]), reads=[bSM], writes=[brc])
                for dc in range(2):
                    O, bO = bank("pv")
                    for mc in range(2):
                        mm(O[:, c0:c1], memV[:, mc, hm * 256 + dc * 128:hm * 256 + (dc + 1) * 128], ets[mc][0][:, c0:c1],
                           mc == 0, mc == 1, [Bconst, ets[mc][1]], bO)
                    tt("dve", hbuf[:, 8 + hm * 2 + dc, c0:c1], O[:, c0:c1], rc[:, c0:c1], ALU.mult, [bO, brc], [Bh[8 + hm * 2 + dc]])


        srow = sb("srow", [4, 3, 512], F32)
        Bsrow = [Buf("srow%d" % i) for i in range(3)]
        vrow_b = sb("vrow_b", [128, 512], BF16)
        vnrow_b = sb("vnrow_b", [4, 512], BF16)
        dW = sb("dW", [4, 4, 4], BF16)
        w00 = sb("w00", [4, 4], F32)
        dnew = sb("dnew", [128, 4], F32)
        MBall = sb("MBall", [128, 4, 8], BF16)
        MBf = sb("MBf", [128, 4, 8], F32)
        t1c = sb("t1c", [128, 1], F32)
        BSl = sb("BSl", [128, 4, 2], BF16)
        iota_f = sb("iota_f", [128, 1], F32)
        ptb = sb("ptb", [128, 128], I32)
        idx_all = sb("idx_all", [128, 128], I32)
        Bidx = Buf("idx")
        Bsm = Buf("smisc")

        def sample_phase():
            N = NS
            first = True
            dma("sp", iota_f[:], c_iota, [], [Bsm], "d_s1")
            dma("sp", w00[:].unsqueeze(2), bass.AP(sgu_w.tensor, 0, [[0, 4], [128 * 128, 4], [1, 1]]), [], [Bsm], "d_s2", nonc=True)
            dma("sp", dnew[:].unsqueeze(2), bass.AP(dscr_d.tensor, 127, [[0, 128], [383, 4], [1, 1]]), [Bdd], [Bsm], "d_s3", nonc=True)
            for g in range(4):
                ts("dve", dW[:, g, :], ident_f[0:4, 0:4], w00[:, g:g + 1], None, ALU.mult, None, [Bsm, Bconst], [Bsm])
            for b in range(4):
                ts("dve", t1c[:, 0:1], ident_f[:, b:b + 1], -NEG, NEG, ALU.mult, ALU.add, [Bconst, Bsm], [Bsm])
                for c in range(2):
                    ts("dve", MBf[:, b, :].rearrange("p (h c) -> p h c", c=2)[:, :, c], dnew[:, :], ident_f[:, b:b + 1],
                       t1c[:, 0:1], ALU.mult, ALU.add, [Bsm, Bconst], [Bsm])
            cp("dve", MBall[:, :, :], MBf[:, :, :], [Bsm], [Bsm])
            for c in range(2):
                cp("dve", BSl[:, :, c:c + 1], BT[:, :, 128:129], [Bconst], [Bsm])
            if STOP == 10:
                return
            st_, bst_ = rot("stg", stg, Bstg)
            dma("sp", st_[0:4, :], xs, [], [bst_], "d_stg%d" % (rr["stg"] % 3))
            pb, pbuf = bank("mm")
            for k in range(8):
                P.op("pe", lambda e, pb=pb, st_=st_, k=k: e.transpose(pb[:, k * 4:(k + 1) * 4], st_[0:4, k * 128:(k + 1) * 128],
                                                                  ident_f[0:4, 0:4]), reads=[bst_, Bconst], writes=[pbuf])
            P.op("dve", lambda e, pb=pb: e.tensor_copy(out=x[:, :, 0:4], in_=pb[:, 0:32].rearrange("p (k b) -> p k b", k=8)),
                 reads=[pbuf], writes=Bx)
            P.op("pool", lambda e: e.tensor_copy(out=xb[:, :, 0:4], in_=x[:, :, 0:4]), reads=Bx, writes=Bxb)
            ffn(N, w_f1i, w_f1o, 0, first, "f1")
            if STOP == 11:
                return
            slot, bs = get_piece(("mi", 0), [(w_mi, 0, 8, 0, 512, 0)], first)
            sv = slot[:, :].rearrange("p (k c) -> p k c", k=8)
            for h in range(4):
                pb, pbuf = bank("mm")
                for k in range(8):
                    mm(pb[:, 0:N], sv[:, k, h * 128:(h + 1) * 128], xb[:, k, 0:N], k == 0, k == 7, [bs, Bxb[k]], pbuf)
                act(qT[:, h, 0:N], pb[:, 0:N], AF.Copy, [pbuf], [BqT[h]], scale=0.125)
            slot, bs = get_piece(("mi", 1), [(w_mi, 0, 8, 512, 512, 0)], first)
            sv = slot[:, :].rearrange("p (k c) -> p k c", k=8)
            P.op("pool", lambda e: e.memset(KT[:, 15, :, :].rearrange("p h k -> p (h k)"), 0.0), writes=[BKT[15]])
            P.op("pool", lambda e: e.memset(vrow_b[:, :], 0.0), writes=[Bsm])
            for h in range(4):
                pb, pbuf = bank("mm")
                for k in range(8):
                    mm(pb[:, 0:N], sv[:, k, h * 128:(h + 1) * 128], xb[:, k, 0:N], k == 0, k == 7, [bs, Bxb[k]], pbuf)
                cp("dve", KT[:, 15, h, 0:4], pb[:, 0:N], [pbuf], [BKT[15]])
            pb, pbuf = bank("mm")
            for k in range(8):
                mm(pb[0:4, :], xb[:, k, 0:4], sv[:, k, :], k == 0, k == 7, [bs, Bxb[k]], pbuf)
            cp("act", srow[:, 0, :], pb[0:4, :], [pbuf], [Bsrow[0], Bsm])
            dma("sp", k_s, srow[:, 0, :], [Bsrow[0]], [], "d_s4")
            slot, bs = get_piece(("mi", 2), [(w_mi, 0, 8, 1024, 512, 0)], first)
            sv = slot[:, :].rearrange("p (k c) -> p k c", k=8)
            pb, pbuf = bank("mm")
            for k in range(8):
                mm(pb[0:4, :], xb[:, k, 0:4], sv[:, k, :], k == 0, k == 7, [bs, Bxb[k]], pbuf)
            cp("act", srow[:, 1, :], pb[0:4, :], [pbuf], [Bsrow[1]])
            cp("pool", vrow_b[0:4, :], srow[:, 1, :], [Bsrow[1]], [Bsm])
            dma("sp", v_s, srow[:, 1, :], [Bsrow[1]], [], "d_s5")
            slot, bs = get_piece(("mi", 3), [(w_mi, 0, 8, 1536, 512, 0)], first)
            sv = slot[:, :].rearrange("p (k c) -> p k c", k=8)
            for g in range(4):
                pb, pbuf = bank("mm")
                for k in range(8):
                    mm(pb[:, 0:N], sv[:, k, g * 128:(g + 1) * 128], xb[:, k, 0:N], k == 0, k == 7, [bs, Bxb[k]], pbuf)
                act(uT[:, g, 0:N], pb[:, 0:N], AF.Gelu, [pbuf], BuT2[g])
            slot, bs = get_piece(("mi", 4), [(w_mi, 0, 8, 2048, 512, 0)], first)
            sv = slot[:, :].rearrange("p (k c) -> p k c", k=8)
            pb, pbuf = bank("mm")
            for k in range(8):
                mm(pb[0:4, :], xb[:, k, 0:4], sv[:, k, :], k == 0, k == 7, [bs, Bxb[k]], pbuf)
            gv = srow[:, 2, :]
            bgv = Bsrow[2]
            cl, bcl = rot("col", col, Bcol)
            act(gv, pb[0:4, :], AF.Gelu, [pbuf], [bgv])
            P.op("dve", lambda e, cl=cl: e.tensor_reduce(out=cl[0:4, 0:1], in_=srow[:, 2, :], axis=AX.X, op=ALU.add),
                 reads=[bgv], writes=[bcl])
            ts("dve", cl[0:4, 1:2], cl[0:4, 0:1], -1.0 / 512, None, ALU.mult, None, [bcl], [bcl])
            ts("dve", gv, gv, cl[0:4, 1:2], None, ALU.add, None, [bgv, bcl], [bgv])
            junk, bj = rot("tmp", tmpf, Btmp)
            tt("pool", junk[0:4, :], gv, gv, ALU.mult, [bgv], [bj])
            P.op("dve", lambda e, junk=junk, cl=cl: e.tensor_reduce(out=cl[0:4, 2:3], in_=junk[0:4, :], axis=AX.X, op=ALU.add),
                 reads=[bj], writes=[bcl])
            ts("dve", cl[0:4, 3:4], cl[0:4, 2:3], 1.0 / 512, LN_EPS, ALU.mult, ALU.add, [bcl], [bcl])
            act(cl[0:4, 3:4], cl[0:4, 3:4], AF.Sqrt, [bcl], [bcl])
            P.op("dve", lambda e, cl=cl: e.reciprocal(out=cl[0:4, 3:4], in_=cl[0:4, 3:4]), reads=[bcl], writes=[bcl])
            stt("dve", gv, gv, cl[0:4, 3:4], sgug[0:4, :], ALU.mult, ALU.mult, [bgv, bcl, Bconst], [bgv])
            tt("dve", gv, gv, sgub_ln[0:4, :], ALU.add, [bgv, Bconst], [bgv])
            cp("pool", vnrow_b[:, :], gv, [bgv], [Bsm])
            dma("sp", g_s, gv, [bgv], [], "d_s6")
            pb, pbuf = bank("mm")
            for g in range(4):
                mm(pb[:, g * 4:(g + 1) * 4], vnrow_b[0:4, g * 128:(g + 1) * 128], dW[0:4, g, :], True, False, [Bsm], pbuf)
                mm(pb[:, g * 4:(g + 1) * 4], ones_b[0:1, :], sgub_row[0:1, g * 128:g * 128 + 1].to_broadcast([1, 4]),
                   False, True, [Bconst], pbuf)
            P.op("dve", lambda e, pb=pb: e.tensor_tensor(out=catT[:, 4:8, 0:4], in0=pb[:, 0:16].rearrange("p (g t) -> p g t", g=4),
                                                       in1=uT[:, :, 0:4], op=ALU.mult), reads=[pbuf] + BuT, writes=Bcat[4:8])
            if STOP == 12:
                return
            set_attn()
            for b in range(4):
                dma("sp", ptb[:, :], bass.AP(ptab.tensor, b * 128, [[0, 128], [1, 128]]), [], [Bidx], "d_s7", nonc=True)
                ts("dve", idx_all[:, :], ptb[:, :], 128.0, iota_f[:, 0:1], ALU.mult, ALU.add, [Bidx, Bsm], [Bidx])
                O, bO = bank("pv")
                SM, bSM = bank("pv")
                jl = list(range(129))
                if STOP in (14, 15, 16):
                    jl = list(range(8))
                if STOP == 17:
                    jl = list(range(8)) + [127, 128]
                if STOP == 18:
                    jl = list(range(8)) + [127]
                if STOP == 19:
                    jl = list(range(8)) + [128]
                jlast = jl[-1]
                for j in jl:
                    sl = j % 8
                    if j < 128:
                        kst, bkst = rot("stg", stg, Bstg)
                        semn = "d_stg%d" % (rr["stg"] % 3)
                        P.op("pool", lambda e, kst=kst, j=j: e.indirect_dma_start(
                            out=kst[:, :], out_offset=None, in_=cache_kv,
                            in_offset=bass.IndirectOffsetOnAxis(ap=idx_all[:, j:j + 1], axis=0)),
                            reads=[Bidx], writes=[bkst], dma_sem=semn)
                        cp("dve" if j % 2 else "act", Vt[:, sl, :], kst[:, 512:1024], [bkst], [BVt[sl]])
                        pbk, pbkb = bank("mm")
                        for h in range(4):
                            tr(pbk[:, h * 128:(h + 1) * 128], kst[:, h * 128:(h + 1) * 128], [bkst], pbkb)
                        cp("act" if j % 2 else "dve", KT[:, sl, :, :].rearrange("p h k -> p (h k)"), pbk[:, :], [pbkb], [BKT[sl]])
                        ktile, bkt, nkeys, vt_l = sl, BKT[sl], 128, None
                        if STOP in (14, 15):
                            continue
                    else:
                        ktile, bkt, nkeys = 15, BKT[15], 128
                    S, bS = bank("sc")
                    for h in range(4):
                        for c in range(2):
                            hc = h * 2 + c
                            special = (j >= 127)
                            mm(S[0:nkeys, hc:hc + 1], KT[c * 64:(c + 1) * 64, ktile, h, 0:nkeys],
                               qT[c * 64:(c + 1) * 64, h, b:b + 1], hc == 0, (hc == 7) and not special,
                               [bkt, BqT[h]], bS, skip=True)
                    if j == 127:
                        mm(S[:, 0:8], ident_b[:], BSl[:, :, :].rearrange("p h c -> p (h c)"), False, True, [Bconst, Bsm], bS, skip=True)
                    if j == 128:
                        mm(S[:, 0:8], ident_b[:], MBall[:, b, :], False, True, [Bconst, Bsm], bS, skip=True)
                    et, bet = rot("et", ET, BET)
                    act(et[0:nkeys, 0:8], S[0:nkeys, 0:8], AF.Exp, [bS], [bet])
                    for h in range(4):
                        if j < 128:
                            lhs = Vt[:, sl, h * 128:(h + 1) * 128]
                            rd = [BVt[sl], bet]
                        else:
                            lhs = vrow_b[:, h * 128:(h + 1) * 128]
                            rd = [Bsm, bet]
                        mm(O[:, 2 * h:2 * h + 2], lhs, et[0:nkeys, 2 * h:2 * h + 2], (j == 0 and h == 0), (j == jlast and h == 3),
                           rd, bO, skip=True)
                    mm(SM[:, 0:8], ones_b[0:nkeys, :], et[0:nkeys, 0:8], j == 0, j == jlast, [Bconst, bet], bSM, skip=True)
                if STOP in (14, 15):
                    continue
                rc, brc = rot("tmp", tmpf, Btmp)
                P.op("dve", lambda e, rc=rc, SM=SM: e.reciprocal(out=rc[:, 0:8], in_=SM[:, 0:8]), reads=[bSM], writes=[brc])
                tt("dve", rc[:, 0:8], O[:, 0:8], rc[:, 0:8], ALU.mult, [bO, brc], [brc])
                rv = rc[:, 0:8].rearrange("p (h c) -> p h c", c=2)
                stt("dve", rc[:, 8:12], rv[:, :, 1], neglam[:, 0:1], rv[:, :, 0], ALU.mult, ALU.add, [brc, Bconst], [brc])
                sqh, bsqh = rot("et", ET, BET)
                act(sqh[:, 0:4], rc[:, 8:12], AF.Square, [brc], [bsqh])
                pb, pbuf = bank("mm")
                mm(pb[:, 0:4], ones_b[:], sqh[:, 0:4], True, True, [Bconst, bsqh], pbuf)
                ts("dve", rc[:, 16:20], pb[:, 0:4], 1.0 / 128, LN_EPS, ALU.mult, ALU.add, [pbuf, brc], [brc])
                act(rc[:, 16:20], rc[:, 16:20], AF.Sqrt, [brc], [brc])
                P.op("dve", lambda e, rc=rc: e.reciprocal(out=rc[:, 16:20], in_=rc[:, 16:20]), reads=[brc], writes=[brc])
                tt("dve", rc[:, 16:20], rc[:, 16:20], rc[:, 8:12], ALU.mult, [brc], [brc])
                P.op("act", lambda e, rc=rc, b=b: e.activation(out=catT[:, 0:4, b:b + 1], in_=rc[:, 16:20].unsqueeze(2),
                                                              func=AF.Copy, scale=sublg[:, 0:1]),
                     reads=[brc, Bconst], writes=Bcat[0:4])
            if STOP in (13, 14, 15, 16, 17, 18, 19):
                return
            set_dense()
            proj_res(N, w_mo, catT, Bcat, "mo", first)
            layer_norm(N, 1)
            for pi in range(2):
                slot, bs = get_piece(("xq", pi), [(w_xq, 0, 8, 512 * pi, 512, 0)], first)
                sv = slot[:, :].rearrange("p (k c) -> p k c", k=8)
                for cc in range(4):
                    j = 4 * pi + cc
                    pb, pbuf = bank("mm")
                    for k in range(8):
                        mm(pb[:, 0:N], sv[:, k, cc * 128:(cc + 1) * 128], xb[:, k, 0:N], k == 0, k == 7,
                           [bs, Bxb[k]], pbuf)
                    act(hbuf[:, j, 0:N], pb[:, 0:N], AF.Copy, [pbuf], [Bh[j]], scale=1.0 / 16)
            set_attn()
            for b in range(4):
                for mc in range(2):
                    st_, bst_ = rot("stg", stg, Bstg)
                    dma("sp", st_[:, :], cmk[b, mc * 128:(mc + 1) * 128, :], [], [bst_], "d_stg%d" % (rr["stg"] % 3))
                    for kk in range(2):
                        pb, pbuf = bank("mm")
                        for q4 in range(4):
                            k = kk * 4 + q4
                            tr(pb[:, q4 * 128:(q4 + 1) * 128], st_[:, k * 128:(k + 1) * 128], [bst_], pbuf)
                        P.op("dve" if kk else "act", (lambda e, pb=pb, kk=kk, mc=mc: e.tensor_copy(
                            out=memKT[:, kk * 4:(kk + 1) * 4, mc * 128:(mc + 1) * 128],
                            in_=pb[:, :].rearrange("p (q m) -> p q m", q=4))) if kk else
                            (lambda e, pb=pb, kk=kk, mc=mc: e.copy(
                                out=memKT[:, kk * 4:(kk + 1) * 4, mc * 128:(mc + 1) * 128],
                                in_=pb[:, :].rearrange("p (q m) -> p q m", q=4))),
                            reads=[pbuf], writes=[Bconst])
                    st2, bst2 = rot("stg", stg, Bstg)
                    dma("sp", st2[:, :], cmv[b, mc * 128:(mc + 1) * 128, :], [], [bst2], "d_stg%d" % (rr["stg"] % 3))
                    cp("pool", memV[:, mc, :], st2[:, :], [bst2], [Bconst])
                if STOP != 22:
                    cross_core(b, b + 1)
            if STOP in (20, 22):
                return
            set_dense()
            proj_res(N, w_xo, hbuf[:, 8:16, :], Bh[8:16], "xo", first)
            layer_norm(N, 2)
            ffn(N, w_f2i, w_f2o, 3, first, "f2")
            if STOP == 21:
                return
            st_, bst_ = rot("stg", stg, Bstg)
            for kk in range(2):
                pb, pbuf = bank("mm")
                for q4 in range(4):
                    k = kk * 4 + q4
                    mm(pb[0:4, q4 * 128:(q4 + 1) * 128], x[:, k, 0:4], ident_f[:, :], True, True, [Bx[k], Bconst], pbuf)
                cp("dve", st_[0:4, kk * 512:(kk + 1) * 512], pb[0:4, :], [pbuf], [bst_])
            dma("sp", y_s, st_[0:4, :], [bst_], [], "d_stg%d" % (rr["stg"] % 3))

        def prompt_tile(ti, first):
            N = TT
            t0 = ti * TT
            if ti == 4:
                for eng_ in ("act", "dve", "pool", "pe"):
                    P.op(eng_, None, writes=[Bwst[0], Bwst[1]])
            for s in range(4):
                st_, bst_ = rot("stg", stg, Bstg)
                dma("sp", st_[:, :], xp[t0 + s * 128:t0 + (s + 1) * 128, :], [], [bst_], "d_stg%d" % (rr["stg"] % 3))
                for kk in range(2):
                    pb, pbuf = bank("mm")
                    for q4 in range(4):
                        k = kk * 4 + q4
                        tr(pb[:, q4 * 128:(q4 + 1) * 128], st_[:, k * 128:(k + 1) * 128], [bst_], pbuf)
                    for q4 in range(4):
                        k = kk * 4 + q4
                        e1 = "dve" if kk else "act"
                        cp(e1, x[:, k, s * 128:(s + 1) * 128], pb[:, q4 * 128:(q4 + 1) * 128], [pbuf], [Bx[k]])
            for k in range(8):
                cp(("pool", "dve", "act", "pool")[k % 4], xb[:, k, :], x[:, k, :], [Bx[k]], [Bxb[k]])
            ffn(N, w_f1i, w_f1o, 0, first, "f1")
            slot, bs = get_piece(("mi", 0), [(w_mi, 0, 8, 0, 512, 0)], first)
            sv = slot[:, :].rearrange("p (k c) -> p k c", k=8)
            for h in range(4):
                pb, pbuf = bank("mm")
                for k in range(8):
                    mm(pb[:, 0:N], sv[:, k, h * 128:(h + 1) * 128], xb[:, k, 0:N], k == 0, k == 7, [bs, Bxb[k]], pbuf)
                act(qT[:, h, :], pb[:, 0:N], AF.Copy, [pbuf], [BqT[h]], scale=0.125)
            slot, bs = get_piece(("mi", 1), [(w_mi, 0, 8, 512, 512, 0)], first)
            sv = slot[:, :].rearrange("p (k c) -> p k c", k=8)
            for h in range(4):
                pb, pbuf = bank("mm")
                for k in range(8):
                    mm(pb[:, 0:N], sv[:, k, h * 128:(h + 1) * 128], xb[:, k, 0:N], k == 0, k == 7, [bs, Bxb[k]], pbuf)
                for s in range(4):
                    kt = ti * 4 + s
                    cp("act" if h % 2 else "dve", KT[:, kt, h, :], pb[:, s * 128:(s + 1) * 128], [pbuf], [BKT[kt]])
            for s in range(4):
                pb, pbuf = bank("mm")
                for k in range(8):
                    mm(pb[:, :], xb[:, k, s * 128:(s + 1) * 128], sv[:, k, :], k == 0, k == 7, [bs, Bxb[k]], pbuf)
                st_, bst_ = rot("stg", stg, Bstg)
                cp("act", st_[:, 0:512], pb[:, :], [pbuf], [bst_])
                dma("sp", k_p[t0 + s * 128:t0 + (s + 1) * 128, :], st_[:, 0:512], [bst_], [], "d_stg%d" % (rr["stg"] % 3))
            slot, bs = get_piece(("mi", 2), [(w_mi, 0, 8, 1024, 512, 0)], first)
            sv = slot[:, :].rearrange("p (k c) -> p k c", k=8)
            for s in range(4):
                kt = ti * 4 + s
                pb, pbuf = bank("mm")
                for k in range(8):
                    mm(pb[:, :], xb[:, k, s * 128:(s + 1) * 128], sv[:, k, :], k == 0, k == 7, [bs, Bxb[k]], pbuf)
                st_, bst_ = rot("stg", stg, Bstg)
                cp("act", st_[:, 0:512], pb[:, :], [pbuf], [bst_])
                cp("pool", Vt[:, kt, :], st_[:, 0:512], [bst_], [BVt[kt]])
                dma("sp", v_p[t0 + s * 128:t0 + (s + 1) * 128, :], st_[:, 0:512], [bst_], [], "d_stg%d" % (rr["stg"] % 3))
            slot, bs = get_piece(("mi", 3), [(w_mi, 0, 8, 1536, 512, 0)], first)
            sv = slot[:, :].rearrange("p (k c) -> p k c", k=8)
            for g in range(4):
                pb, pbuf = bank("mm")
                for k in range(8):
                    mm(pb[:, 0:N], sv[:, k, g * 128:(g + 1) * 128], xb[:, k, 0:N], k == 0, k == 7, [bs, Bxb[k]], pbuf)
                act(uT[:, g, :], pb[:, 0:N], AF.Gelu, [pbuf], BuT2[g])
            slot, bs = get_piece(("mi", 4), [(w_mi, 0, 8, 2048, 512, 0)], first)
            sv = slot[:, :].rearrange("p (k c) -> p k c", k=8)
            vbs = []
            for s in range(4):
                pb, pbuf = bank("mm")
                for k in range(8):
                    mm(pb[:, :], xb[:, k, s * 128:(s + 1) * 128], sv[:, k, :], k == 0, k == 7, [bs, Bxb[k]], pbuf)
                gv, bgv = tmpf[s], Btmp[s]
                cl, bcl = col[s], Bcol[s]
                act(gv[:, :], pb[:, :], AF.Gelu, [pbuf], [bgv])
                vbs.append((gv, bgv, cl, bcl))
            for s in range(4):
                gv, bgv, cl, bcl = vbs[s]
                P.op("dve", lambda e, gv=gv, cl=cl: e.tensor_reduce(out=cl[:, 0:1], in_=gv[:, :], axis=AX.X, op=ALU.add),
                     reads=[bgv], writes=[bcl])
                ts("dve", cl[:, 1:2], cl[:, 0:1], -1.0 / 512, None, ALU.mult, None, [bcl], [bcl])
                ts("dve", gv[:, :], gv[:, :], cl[:, 1:2], None, ALU.add, None, [bgv, bcl], [bgv])
                tt("pool", oacc[:, :], gv[:, :], gv[:, :], ALU.mult, [bgv], [Boacc])
                P.op("dve", lambda e, cl=cl: e.tensor_reduce(out=cl[:, 2:3], in_=oacc[:, :], axis=AX.X, op=ALU.add),
                     reads=[Boacc], writes=[bcl])
                ts("dve", cl[:, 3:4], cl[:, 2:3], 1.0 / 512, LN_EPS, ALU.mult, ALU.add, [bcl], [bcl])
            for s in range(4):
                gv, bgv, cl, bcl = vbs[s]
                act(cl[:, 3:4], cl[:, 3:4], AF.Sqrt, [bcl], [bcl])
            for s in range(4):
                gv, bgv, cl, bcl = vbs[s]
                P.op("dve", lambda e, cl=cl: e.reciprocal(out=cl[:, 3:4], in_=cl[:, 3:4]), reads=[bcl], writes=[bcl])
                stt("dve", gv[:, :], gv[:, :], cl[:, 3:4], sgug[:, :], ALU.mult, ALU.mult, [bgv, bcl, Bconst], [bgv])
                tt("pool", vn[:, s, :], gv[:, :], sgub_ln[:, :], ALU.add, [bgv, Bconst], [Bvn[s]])
            set_attn()
            nk = 4 * (ti + 1)
            def rms_head(h):
                oa, boa = (oacc, Boacc) if h % 2 == 0 else (oacc2, Boacc2)
                sqh, bsqh = rot("et", ET, BET)
                act(sqh[:, :], oa[:, :], AF.Square, [boa], [bsqh])
                pb, pbuf = bank("mm")
                mm(pb[:, :], ones_b[:], sqh[:, :], True, True, [Bconst, bsqh], pbuf)
                rs_, brs = rot("tmp", tmpf, Btmp)
                ts("dve", rs_[:, :], pb[:, :], 1.0 / 128, LN_EPS, ALU.mult, ALU.add, [pbuf], [brs])
                act(rs_[:, :], rs_[:, :], AF.Sqrt, [brs], [brs])
                P.op("dve", lambda e, rs_=rs_: e.reciprocal(out=rs_[:, :], in_=rs_[:, :]), reads=[brs], writes=[brs])
                tt("dve", rs_[:, :], rs_[:, :], oa[:, :], ALU.mult, [brs, boa], [brs])
                act(catT[:, h, :], rs_[:, :], AF.Copy, [brs, Bconst], [Bcat[h]], scale=sublg[:, 0:1])
            for h in range(4):
                oa, boa = (oacc, Boacc) if h % 2 == 0 else (oacc2, Boacc2)
                for c in range(2):
                    if c == 1 and h >= 1:
                        rms_head(h - 1)
                    O, bO = bank("pv")
                    SM, bSM = bank("pv")
                    for kt in range(nk):
                        r = kt - 4 * ti
                        q0 = max(r, 0) * 128
                        S, bS = bank("sc")
                        spec = None
                        if kt == 4 * ti - 1:
                            spec = (0, 128, 128)
                        elif r >= 0:
                            ln_ = min(256, 512 - q0)
                            spec = (q0, ln_, 0)
                        mm(S[:, q0:512], KT[c * 64:(c + 1) * 64, kt, h, :], qT[c * 64:(c + 1) * 64, h, q0:512],
                           True, spec is None, [BKT[kt], BqT[h]], bS, skip=True)
                        if spec is not None:
                            cs, ln_, bo = spec
                            mm(S[:, cs:cs + ln_], ident_b[:], BT[:, h, bo:bo + ln_], False, True, [Bconst], bS, skip=True)
                        et, bet = rot("et", ET, BET)
                        act(et[:, q0:512], S[:, q0:512], AF.Exp, [bS, Bconst], [bet], bias=far[:, h:h + 1], scale=1.0)
                        mm(O[:, q0:512], Vt[:, kt, h * 128:(h + 1) * 128], et[:, q0:512], kt == 0, kt == nk - 1,
                           [BVt[kt], bet], bO, skip=True)
                        mm(SM[:, q0:512], ones_b[:], et[:, q0:512], kt == 0, kt == nk - 1, [Bconst, bet], bSM, skip=True)
                    rc, brc = rot("tmp", tmpf, Btmp)
                    P.op("dve", lambda e, rc=rc, SM=SM: e.reciprocal(out=rc[:, :], in_=SM[:, :]), reads=[bSM], writes=[brc])
                    if c == 0:
                        tt("dve", oa[:, :], O[:, :], rc[:, :], ALU.mult, [bO, brc], [boa])
                    else:
                        tt("dve", rc[:, :], O[:, :], rc[:, :], ALU.mult, [bO, brc], [brc])
                        stt("dve", oa[:, :], rc[:, :], neglam[:, 0:1], oa[:, :], ALU.mult, ALU.add,
                            [brc, boa, Bconst], [boa])
            rms_head(3)
            for s in range(4):
                pb, pbuf = bank("mm")
                for g in range(4):
                    mm(pb[:, g * 128:(g + 1) * 128], vn[:, s, g * 128:(g + 1) * 128], trilWT[:, g, :], True, False,
                       [Bvn[s], Bconst], pbuf)
                    mm(pb[:, g * 128:(g + 1) * 128], ones_b[0:1, :], sgub_row[0:1, g * 128:(g + 1) * 128], False, True,
                       [Bconst], pbuf)
                P.op("dve", lambda e, pb=pb, s=s: e.tensor_tensor(
                    out=catT[:, 4:8, s * 128:(s + 1) * 128],
                    in0=pb[:, :].rearrange("p (g t) -> p g t", g=4),
                    in1=uT[:, :, s * 128:(s + 1) * 128], op=ALU.mult),
                    reads=[pbuf] + BuT, writes=Bcat[4:8])
            set_dense()
            proj_res(N, w_mo, catT, Bcat, "mo", first)
            layer_norm(N, 1)
            for pi in range(2):
                slot, bs = get_piece(("xq", pi), [(w_xq, 0, 8, 512 * pi, 512, 0)], first)
                sv = slot[:, :].rearrange("p (k c) -> p k c", k=8)
                for cc in range(4):
                    j = 4 * pi + cc
                    pb, pbuf = bank("mm")
                    for k in range(8):
                        mm(pb[:, 0:N], sv[:, k, cc * 128:(cc + 1) * 128], xb[:, k, 0:N], k == 0, k == 7,
                           [bs, Bxb[k]], pbuf)
                    act(hbuf[:, j, :], pb[:, 0:N], AF.Copy, [pbuf], [Bh[j]], scale=1.0 / 16)
            set_attn()
            cross_core(0, N)
            set_dense()
            proj_res(N, w_xo, hbuf[:, 8:16, :], Bh[8:16], "xo", first)
            layer_norm(N, 2)
            ffn(N, w_f2i, w_f2o, 3, first, "f2")
            for s in range(4):
                st_, bst_ = rot("stg", stg, Bstg)
                for kk in range(2):
                    pb, pbuf = bank("mm")
                    for q4 in range(4):
                        k = kk * 4 + q4
                        tr(pb[:, q4 * 128:(q4 + 1) * 128], x[:, k, s * 128:(s + 1) * 128], [Bx[k]], pbuf)
                    cp("act" if kk else "dve", st_[:, kk * 512:(kk + 1) * 512], pb[:, :], [pbuf], [bst_])
                dma("sp", y_p[t0 + s * 128:t0 + (s + 1) * 128, :], st_[:, :], [bst_], [], "d_stg%d" % (rr["stg"] % 3))


        P.barrier()
        sample_phase()
        if STOP >= 10:
            P.emit(nc); return nc
        P.barrier()
        mem_phase()
        if STOP in (5, 6):
            P.emit(nc); return nc
        for ti in range(NT_RUN):
            prompt_tile(ti, False)
        P.emit(nc)
    return nc


NT_RUN = NT
STOP = 0


def kernel(**inp):
    f32 = np.float32
    consts = _host_consts()
    nc = build_nc()
    lam_in = np.stack([inp["lambda_q1"][0], inp["lambda_k1"][0], inp["lambda_q2"][0], inp["lambda_k2"][0]]).astype(f32)
    shared = {
        "rel_bias": np.ascontiguousarray(inp["rel_bias"], dtype=f32),
        "ln_g": np.ascontiguousarray(inp["ln_g"][0]),
        "ln_b": np.ascontiguousarray(inp["ln_b"][0]),
        "ffn1_w_in": np.ascontiguousarray(inp["ffn1_w_in"][0]),
        "ffn1_w_out": np.ascontiguousarray(inp["ffn1_w_out"][0]),
        "w_mix_in": np.ascontiguousarray(inp["w_mix_in"][0]),
        "w_mix_out": np.ascontiguousarray(inp["w_mix_out"][0]),
        "lam_in": lam_in,
        "subln_g": np.ascontiguousarray(inp["subln_g"][0].reshape(128, 1)),
        "sgu_ln_g": np.ascontiguousarray(inp["sgu_ln_g"][0].reshape(1, 512)),
        "sgu_ln_b": np.ascontiguousarray(inp["sgu_ln_b"][0].reshape(1, 512)),
        "sgu_w": np.ascontiguousarray(inp["sgu_w"][0]),
        "sgu_b": np.ascontiguousarray(inp["sgu_b"][0].reshape(1, 512)),
        "xq_w": np.ascontiguousarray(inp["xq_w"][0]),
        "xkv_w": np.ascontiguousarray(inp["xkv_w"][0]),
        "xo_w": np.ascontiguousarray(inp["xo_w"][0]),
        "ffn2_w_in": np.ascontiguousarray(inp["ffn2_w_in"][0]),
        "ffn2_w_out": np.ascontiguousarray(inp["ffn2_w_out"][0]),
    }
    shared.update(consts)
    kv = np.empty((NPHYS * 128, 1024), dtype=f32)
    kv[:, 0:512] = np.asarray(inp["cache_k"]).reshape(NPHYS * 128, 512)
    kv[:, 512:1024] = np.asarray(inp["cache_v"]).reshape(NPHYS * 128, 512)
    shared["cache_kv"] = kv
    in_maps = []
    for c in range(8):
        m = dict(shared)
        m["xp"] = np.ascontiguousarray(inp["x_prompt"][c])
        m["mem"] = np.ascontiguousarray(inp["mem_prompt"][c])
        m["xs"] = np.ascontiguousarray(inp["x_sample"][NS * c:NS * (c + 1), 0, :])
        m["cmk"] = np.ascontiguousarray(inp["cache_mem_k"][0, NS * c:NS * (c + 1)]).reshape(NS, 256, D)
        m["cmv"] = np.ascontiguousarray(inp["cache_mem_v"][0, NS * c:NS * (c + 1)]).reshape(NS, 256, D)
        m["ptab"] = np.ascontiguousarray(inp["page_table"][NS * c:NS * (c + 1)]).astype(np.int32)
        in_maps.append(m)
    res = run_bass_kernel_spmd(nc, in_maps, core_ids=list(range(8)))
    R = res.results
    y_p = np.stack([R[c]["y_p"] for c in range(8)])
    k_p = np.stack([R[c]["k_p"] for c in range(8)]).reshape(1, 8, SEQ, 4, 128)
    v_p = np.stack([R[c]["v_p"] for c in range(8)]).reshape(1, 8, SEQ, 4, 128)
    mk_p = np.stack([R[c]["mk_p"] for c in range(8)]).reshape(1, 8, 256, 4, 256)
    mv_p = np.stack([R[c]["mv_p"] for c in range(8)]).reshape(1, 8, 256, 4, 256)
    y_s = np.concatenate([R[c]["y_s"] for c in range(8)]).reshape(32, 1, D)
    k_s = np.concatenate([R[c]["k_s"] for c in range(8)]).reshape(1, 32, 1, 4, 128)
    v_s = np.concatenate([R[c]["v_s"] for c in range(8)]).reshape(1, 32, 1, 4, 128)
    g_s = np.concatenate([R[c]["g_s"] for c in range(8)]).reshape(1, 32, 1, 512)
    return (y_p, y_s, k_p, v_p, mk_p, mv_p, k_s, v_s, g_s)
```

```python
import contextlib
import math
import numpy as np
import concourse.bass as bass
import concourse.mybir as mybir
from concourse.bass_utils import run_bass_kernel_spmd

F32 = mybir.dt.float32
BF16 = mybir.dt.bfloat16
I32 = mybir.dt.int32
AF = mybir.ActivationFunctionType
ALU = mybir.AluOpType
AX = mybir.AxisListType

D = 1024
SEQ = 4096
TT = 512
NT = SEQ // TT
DFF = 2816
NJ = DFF // 128
NPHYS = 5120
ALPHA = 2.0 ** 0.25
LN_EPS = 1e-5
EPS_EFF = LN_EPS / (ALPHA * ALPHA)
LAMBDA_INIT = 0.8 - 0.6 * math.exp(0.0)
NEG = -30000.0
NS = 4

ENGS = ("pe", "act", "dve", "pool", "sp")
SAME_ENGINE_SYNC = {"pe": False, "act": True, "dve": True, "pool": True, "sp": False}


class Buf:
    __slots__ = ("name", "w", "rs")

    def __init__(self, name=""):
        self.name = name
        self.w = None
        self.rs = {}


class Op:
    __slots__ = ("eng", "idx", "fn", "waits", "dma", "need_inc", "count")

    def __init__(self, eng, idx, fn):
        self.eng = eng
        self.idx = idx
        self.fn = fn
        self.waits = []
        self.dma = None
        self.need_inc = False
        self.count = None


class Prog:
    def __init__(self):
        self.streams = {e: [] for e in ENGS}
        self.dma_cnt = {}
        self.waited = {e: {} for e in ENGS}

    def op(self, eng, fn, reads=(), writes=(), dma_sem=None):
        st = self.streams[eng]
        o = Op(eng, len(st), fn)
        need = []
        for b in reads:
            if b.w is not None:
                need.append(b.w)
        for b in writes:
            if b.w is not None:
                need.append(b.w)
            need.extend(b.rs.values())
        wd = self.waited[eng]
        best = {}
        for t in need:
            if t[0] == "op":
                p = t[1]
                if p.eng == eng and not SAME_ENGINE_SYNC[eng]:
                    continue
                key = "e_" + p.eng
                if wd.get(key, -1) >= p.idx:
                    continue
                if key not in best or best[key][1].idx < p.idx:
                    best[key] = t
            else:
                _, s, v = t
                if wd.get(s, 0) >= v:
                    continue
                if s not in best or best[s][2] < v:
                    best[s] = t
        for key, t in best.items():
            wd[key] = t[1].idx if t[0] == "op" else t[2]
            if t[0] == "op":
                t[1].need_inc = True
        o.waits = list(best.values())
        if dma_sem is not None:
            self.dma_cnt[dma_sem] = self.dma_cnt.get(dma_sem, 0) + 16
            o.dma = (dma_sem, self.dma_cnt[dma_sem])
            tok = ("dma", dma_sem, self.dma_cnt[dma_sem])
            rkey = dma_sem
        else:
            tok = ("op", o)
            rkey = "e_" + eng
        if fn is not None:
            for b in reads:
                b.rs[rkey] = tok
            for b in writes:
                b.w = tok
                b.rs = {}
        st.append(o)
        return tok

    def barrier(self):
        toks = []
        for e in ENGS:
            for p in reversed(self.streams[e]):
                if p.dma is None and p.fn is not None:
                    toks.append(("op", p))
                    break
        for s, v in self.dma_cnt.items():
            toks.append(("dma", s, v))
        for e in ENGS:
            for t in toks:
                if t[0] == "op" and t[1].eng == e:
                    continue
                b2 = Buf()
                b2.w = t
                self.op(e, None, reads=[b2])

    def emit(self, nc, final_wait_eng="sp"):
        for s, v in list(self.dma_cnt.items()):
            b2 = Buf()
            b2.w = ("dma", s, v)
            self.op(final_wait_eng, None, reads=[b2])
        for e in ENGS:
            c = 0
            for o in self.streams[e]:
                if o.need_inc:
                    c += 1
                    o.count = c
        with contextlib.ExitStack() as es:
            sems = {}
            for e in ENGS:
                sems["e_" + e] = es.enter_context(nc.semaphore("e_" + e))
            for s in self.dma_cnt:
                sems[s] = es.enter_context(nc.semaphore(s))
            block = es.enter_context(nc.Block())

            def run(eng_name):
                def body(eng):
                    for o in self.streams[eng_name]:
                        for t in o.waits:
                            if t[0] == "op":
                                eng.wait_ge(sems["e_" + t[1].eng], t[1].count)
                            else:
                                eng.wait_ge(sems[t[1]], t[2])
                        if o.fn is None:
                            continue
                        ins = o.fn(eng)
                        if o.dma is not None:
                            ins.then_inc(sems[o.dma[0]], 16)
                        elif o.need_inc:
                            ins.then_inc(sems["e_" + eng_name], 1)
                return body

            block.tensor(run("pe"))
            block.scalar(run("act"))
            block.vector(run("dve"))
            block.gpsimd(run("pool"))
            block.sync(run("sp"))


def _bucket_np(n):
    n = np.asarray(n, dtype=np.int64)
    nf = np.maximum(n, 1).astype(np.float32)
    large = 16 + (np.log(nf / np.float32(16)) / np.float32(math.log(8.0)) * np.float32(16)).astype(np.int32)
    large = np.minimum(large, 31)
    return np.where(n < 16, n, large)


def _host_consts():
    ident = np.eye(128, dtype=np.float32)
    tril = np.tril(np.ones((128, 128), dtype=np.float32))
    ohm = np.zeros((32, 383), dtype=np.float32)
    for m in range(383):
        n = m - 127
        if n >= 0:
            ohm[int(_bucket_np(n)), m] = 1.0
    maskrow = np.zeros((4, 383), dtype=np.float32)
    maskrow[:, :127] = NEG
    iota = np.arange(128, dtype=np.float32).reshape(128, 1)
    return {"c_ident": ident, "c_tril": tril, "c_ohm": ohm, "c_maskrow": maskrow, "c_iota": iota}


class Ctx:
    pass


def build_nc():
    nc = bass.Bass("TRN2", target_bir_lowering=False)
    P = Prog()

    def din(name, shape, dt=F32):
        return nc.dram_tensor(name, list(shape), dt, kind="ExternalInput").ap()

    def dout(name, shape, dt=F32):
        return nc.dram_tensor(name, list(shape), dt, kind="ExternalOutput").ap()

    def dscr(name, shape, dt=F32):
        return nc.dram_tensor(name, list(shape), dt, kind="Internal").ap()

    xp = din("xp", [SEQ, D])
    mem = din("mem", [256, D])
    rel_bias = din("rel_bias", [32, 4])
    ln_g = din("ln_g", [4, D])
    ln_b = din("ln_b", [4, D])
    w_f1i = din("ffn1_w_in", [D, 2 * DFF])
    w_f1o = din("ffn1_w_out", [DFF, D])
    w_mi = din("w_mix_in", [D, 2560])
    w_mo = din("w_mix_out", [D, D])
    lam_in = din("lam_in", [4, 64])
    subln_g = din("subln_g", [128, 1])
    sgu_ln_g = din("sgu_ln_g", [1, 512])
    sgu_ln_b = din("sgu_ln_b", [1, 512])
    sgu_w = din("sgu_w", [4, 128, 128])
    sgu_b = din("sgu_b", [1, 512])
    w_xq = din("xq_w", [D, D])
    w_xkv = din("xkv_w", [D, 2 * D])
    w_xo = din("xo_w", [D, D])
    w_f2i = din("ffn2_w_in", [D, 2 * DFF])
    w_f2o = din("ffn2_w_out", [DFF, D])
    c_ident = din("c_ident", [128, 128])
    c_tril = din("c_tril", [128, 128])
    c_ohm = din("c_ohm", [32, 383])
    c_maskrow = din("c_maskrow", [4, 383])
    c_iota = din("c_iota", [128, 1])
    xs = din("xs", [NS, D])
    cache_kv = din("cache_kv", [NPHYS * 128, 1024])
    cmk = din("cmk", [NS, 256, D])
    cmv = din("cmv", [NS, 256, D])
    ptab = din("ptab", [NS, 128], I32)

    y_p = dout("y_p", [SEQ, D])
    k_p = dout("k_p", [SEQ, 512])
    v_p = dout("v_p", [SEQ, 512])
    mk_p = dout("mk_p", [256, D])
    mv_p = dout("mv_p", [256, D])
    y_s = dout("y_s", [NS, D])
    k_s = dout("k_s", [NS, 512])
    v_s = dout("v_s", [NS, 512])
    g_s = dout("g_s", [NS, 512])

    NPIECE = 64
    wscr = dscr("wscr", [NPIECE, 128, 4096], BF16)
    dscr_d = dscr("dscr_d", [4, 383])
    dscr_f = dscr("dscr_f", [4, 128 * 383])
    Bwscr = [Buf("wscr%d" % i) for i in range(NPIECE)]

    with contextlib.ExitStack() as es:
        def sb(name, shape, dt):
            return es.enter_context(nc.sbuf_tensor(name, list(shape), dt))

        ident_f = sb("ident_f", [128, 128], F32)
        ident_b = sb("ident_b", [128, 128], BF16)
        ones_b = sb("ones_b", [128, 128], BF16)
        ones_f = sb("ones_f", [128, 128], F32)
        lng = sb("lng", [128, 4, 8], F32)
        lnb = sb("lnb", [128, 4, 8], F32)
        BT = sb("BT", [128, 4, 256], BF16)
        BTf = sb("BTf", [128, 4, 256], F32)
        far = sb("far", [128, 4], F32)
        neglam = sb("neglam", [128, 1], F32)
        sublg = sb("sublg", [128, 1], F32)
        sgug = sb("sgug", [128, 512], F32)
        sgub_ln = sb("sgub_ln", [128, 512], F32)
        sgub_row = sb("sgub_row", [1, 512], BF16)
        sgub_rowf = sb("sgub_rowf", [1, 512], F32)
        trilWT = sb("trilWT", [128, 4, 128], BF16)
        memKT = sb("memKT", [128, 8, 256], BF16)
        memV = sb("memV", [128, 2, 1024], BF16)
        KT2 = sb("KT", [128, 32 * 512], BF16)
        Vt2 = sb("Vt", [128, 32 * 512], BF16)
        KT = KT2[:, :].rearrange("p (t h k) -> p t h k", t=32, h=4)
        Vt = Vt2[:, :].rearrange("p (t f) -> p t f", t=32)
        BKT = [Buf("KT%d" % i) for i in range(32)]
        BVt = [Buf("Vt%d" % i) for i in range(32)]
        x = sb("x", [128, 8, TT], F32)
        xb = sb("xb", [128, 8, TT], BF16)
        hbuf2 = sb("hbuf", [128, NJ * TT], BF16)
        hbuf = hbuf2[:, :].rearrange("p (k t) -> p k t", k=NJ)
        sq = hbuf2[:, 0:8 * TT].rearrange("p (k t) -> p k t", k=8)
        Bx = [Buf("x%d" % i) for i in range(8)]
        Bxb = [Buf("xb%d" % i) for i in range(8)]
        Bh = [Buf("h%d" % i) for i in range(NJ)]
        Bsq = Bh[0:8]
        NSLOT = 3
        wslot = [sb("wslot%d" % i, [128, 4096], BF16) for i in range(NSLOT)]
        Bws = [Buf("ws%d" % i) for i in range(NSLOT)]
        wstage = [KT2[:, 8192:16384].bitcast(F32), Vt2[:, 8192:16384].bitcast(F32)]
        Bwst = [Buf("wst%d" % i) for i in range(2)]
        tmpf = [sb("tmpf%d" % i, [128, TT], F32) for i in range(4)]
        Btmp = [Buf("tmpf%d" % i) for i in range(4)]
        stg = [sb("stg%d" % i, [128, 1024], F32) for i in range(3)]
        Bstg = [Buf("stg%d" % i) for i in range(3)]
        qT = hbuf2[:, 16 * TT:20 * TT].rearrange("p (k t) -> p k t", k=4)
        BqT = Bh[16:20]
        uT = hbuf2[:, 8 * TT:16 * TT].bitcast(F32).rearrange("p (k t) -> p k t", k=4)
        BuT2 = [[Bh[8 + 2 * g], Bh[9 + 2 * g]] for g in range(4)]
        BuT = Bh[8:16]
        vn = sb("vn", [128, 4, 512], BF16)
        Bvn = [Buf("vn%d" % i) for i in range(4)]
        catT = sq
        Bcat = Bh[0:8]
        ET = [sb("ET%d" % i, [128, TT], BF16) for i in range(4)]
        BET = [Buf("ET%d" % i) for i in range(4)]
        oacc = sb("oacc", [128, TT], F32)
        Boacc = Buf("oacc")
        oacc2 = sb("oacc2", [128, TT], F32)
        Boacc2 = Buf("oacc2")
        col = [sb("col%d" % i, [128, 8], F32) for i in range(4)]
        Bcol = [Buf("col%d" % i) for i in range(4)]
        psum = [es.enter_context(nc.psum_tensor("ps%d" % i, [128, 512], F32)) for i in range(8)]
        Bps = [Buf("ps%d" % i) for i in range(8)]

        rr = {"mm": 0, "sc": 0, "pv": 0, "tmp": 0, "stg": 0, "et": 0, "ws": 0, "wst": 0, "col": 0}
        POOLS = {"mm": [0, 1, 2, 3, 4, 5], "sc": [2, 3], "pv": [4, 5, 6, 7]}

        def set_dense():
            POOLS["mm"] = [0, 1, 2, 3, 4, 5]

        def set_attn():
            POOLS["mm"] = [0, 1]

        def bank(pool):
            lst = POOLS[pool]
            i = lst[rr[pool] % len(lst)]
            rr[pool] += 1
            return psum[i], Bps[i]

        def rot(key, arrs, bufs):
            i = rr[key] % len(arrs)
            rr[key] += 1
            return arrs[i], bufs[i]

        def mm(out, lhsT, rhs, start, stop, reads, wbuf, skip=False):
            if skip:
                P.op("pe", lambda e: e.matmul(out, lhsT=lhsT, rhs=rhs, start=start, stop=stop, skip_group_check=True),
                     reads=reads, writes=[wbuf])
            else:
                P.op("pe", lambda e: e.matmul(out, lhsT=lhsT, rhs=rhs, start=start, stop=stop),
                     reads=reads, writes=[wbuf])

        def tr(out, in_, reads, wbuf):
            P.op("pe", lambda e: e.transpose(out, in_, ident_f[:]), reads=reads + [Bconst], writes=[wbuf])

        def act(out, in_, func, reads, writes, bias=None, scale=1.0):
            if bias is None:
                P.op("act", lambda e: e.activation(out=out, in_=in_, func=func, scale=scale),
                     reads=reads, writes=writes)
            else:
                P.op("act", lambda e: e.activation(out=out, in_=in_, func=func, bias=bias, scale=scale),
                     reads=reads, writes=writes)

        def tt(eng, out, in0, in1, op, reads, writes):
            P.op(eng, lambda e: e.tensor_tensor(out=out, in0=in0, in1=in1, op=op), reads=reads, writes=writes)

        def ts(eng, out, in0, s1, s2, op0, op1, reads, writes):
            if s2 is None:
                P.op(eng, lambda e: e.tensor_scalar(out=out, in0=in0, scalar1=s1, scalar2=None, op0=op0),
                     reads=reads, writes=writes)
            else:
                P.op(eng, lambda e: e.tensor_scalar(out=out, in0=in0, scalar1=s1, scalar2=s2, op0=op0, op1=op1),
                     reads=reads, writes=writes)

        def stt(eng, out, in0, scalar, in1, op0, op1, reads, writes):
            P.op(eng, lambda e: e.scalar_tensor_tensor(out=out, in0=in0, scalar=scalar, in1=in1, op0=op0, op1=op1),
                 reads=reads, writes=writes)

        def cp(eng, out, in_, reads, writes):
            if eng == "act":
                P.op("act", lambda e: e.copy(out=out, in_=in_), reads=reads, writes=writes)
            else:
                P.op(eng, lambda e: e.tensor_copy(out=out, in_=in_), reads=reads, writes=writes)

        def dma(eng, out, in_, reads, writes, sem, nonc=False):
            if nonc:
                P.op(eng, lambda e: e.dma_start(out=out, in_=in_, allow_slow_non_contiguous=True),
                     reads=reads, writes=writes, dma_sem=sem)
            else:
                P.op(eng, lambda e: e.dma_start(out=out, in_=in_), reads=reads, writes=writes, dma_sem=sem)

        Bconst = Buf("const")

        dma("sp", ident_f[:], c_ident, [], [Bconst], "d_c1")
        cp("dve", ident_b[:], ident_f[:], [Bconst], [Bconst])
        P.op("pool", lambda e: e.memset(ones_b[:], 1.0), writes=[Bconst])
        P.op("pool", lambda e: e.memset(ones_f[:], 1.0), writes=[Bconst])
        dma("sp", lng[:], ln_g.rearrange("i (k p) -> p i k", p=128), [], [Bconst], "d_c2", nonc=True)
        dma("sp", lnb[:], ln_b.rearrange("i (k p) -> p i k", p=128), [], [Bconst], "d_c3", nonc=True)
        dma("sp", sublg[:], subln_g, [], [Bconst], "d_c4", nonc=True)
        ts("dve", sublg[:], sublg[:], 1.0 - LAMBDA_INIT, None, ALU.mult, None, [Bconst], [Bconst])
        dma("sp", sgug[:], sgu_ln_g.to_broadcast([128, 512]), [], [Bconst], "d_c5", nonc=True)
        dma("sp", sgub_ln[:], sgu_ln_b.to_broadcast([128, 512]), [], [Bconst], "d_c6", nonc=True)
        dma("sp", sgub_rowf[:], sgu_b, [], [Bconst], "d_c7")
        cp("dve", sgub_row[:], sgub_rowf[:], [Bconst], [Bconst])
        lam_t = sb("lam_t", [128, 4, 64], F32)
        lam_s = sb("lam_s", [128, 4], F32)
        dma("sp", lam_t[:], bass.AP(lam_in.tensor, 0, [[0, 128], [64, 4], [1, 64]]), [], [Bconst], "d_c8", nonc=True)
        tt("dve", lam_t[:, 0, :], lam_t[:, 0, :], lam_t[:, 1, :], ALU.mult, [Bconst], [Bconst])
        tt("dve", lam_t[:, 2, :], lam_t[:, 2, :], lam_t[:, 3, :], ALU.mult, [Bconst], [Bconst])
        P.op("dve", lambda e: e.tensor_reduce(out=lam_s[:, 0:1], in_=lam_t[:, 0, :], axis=AX.X, op=ALU.add),
             reads=[Bconst], writes=[Bconst])
        P.op("dve", lambda e: e.tensor_reduce(out=lam_s[:, 1:2], in_=lam_t[:, 2, :], axis=AX.X, op=ALU.add),
             reads=[Bconst], writes=[Bconst])
        act(lam_s[:, 2:4], lam_s[:, 0:2], AF.Exp, [Bconst], [Bconst])
        tt("dve", neglam[:], lam_s[:, 3:4], lam_s[:, 2:3], ALU.subtract, [Bconst], [Bconst])
        ts("dve", neglam[:], neglam[:], -LAMBDA_INIT, None, ALU.add, None, [Bconst], [Bconst])

        if STOP == 1:
            P.emit(nc); return nc
        tab = sb("tab", [32, 4], F32)
        ohm = sb("ohm", [32, 383], F32)
        mrow = sb("mrow", [4, 383], F32)
        dvec = sb("dvec", [4, 383], F32)
        dma("sp", tab[:], rel_bias, [], [Bconst], "d_c9")
        dma("sp", ohm[:], c_ohm, [], [Bconst], "d_c10")
        dma("sp", mrow[:], c_maskrow, [], [Bconst], "d_c11")
        pb, pbuf = bank("mm")
        mm(pb[0:4, 0:383], tab[:], ohm[:], True, True, [Bconst], pbuf)
        cp("dve", dvec[:], pb[0:4, 0:383], [pbuf], [Bconst])
        ts("dve", dvec[:], dvec[:], dvec[:, 382:383], None, ALU.subtract, None, [Bconst], [Bconst])
        tt("dve", dvec[:], dvec[:], mrow[:], ALU.add, [Bconst], [Bconst])
        Bdd = Buf("dscr_d")
        Bdf = Buf("dscr_f")
        dma("sp", dscr_d, dvec[:], [Bconst], [Bdd], "d_c12")
        for h in range(4):
            dma("sp", bass.AP(dscr_f.tensor, h * 128 * 383, [[383, 128], [1, 383]]),
                bass.AP(dscr_d.tensor, h * 383, [[0, 128], [1, 383]]), [Bdd], [Bdf], "d_c13", nonc=True)
        for h in range(4):
            dma("sp", BTf[:, h, :], bass.AP(dscr_f.tensor, h * 128 * 383 + 127, [[382, 128], [1, 256]]),
                [Bdf], [Bconst], "d_c14", nonc=True)
        cp("dve", BT[:], BTf[:], [Bconst], [Bconst])
        if STOP == 2:
            P.emit(nc); return nc
        dma("sp", far[:], bass.AP(rel_bias.tensor, 31 * 4, [[0, 128], [1, 4]]), [], [Bconst], "d_c15", nonc=True)
        wtmp = sb("wtmp", [128, 4, 128], F32)
        trm = sb("trm", [128, 128], F32)
        dma("sp", wtmp[:], sgu_w.rearrange("g t s -> t g s"), [], [Bconst], "d_c16", nonc=True)
        dma("sp", trm[:], c_tril, [], [Bconst], "d_c17")
        for g in range(4):
            tt("dve", wtmp[:, g, :], wtmp[:, g, :], trm[:], ALU.mult, [Bconst], [Bconst])
            pb, pbuf = bank("mm")
            tr(pb[:, 0:128], wtmp[:, g, :], [Bconst], pbuf)
            cp("dve", trilWT[:, g, :], pb[:, 0:128], [pbuf], [Bconst])

        if STOP == 3:
            P.emit(nc); return nc
        piece_id = {}

        def get_piece(key, blocks, first):
            if key not in piece_id:
                piece_id[key] = len(piece_id)
            pid = piece_id[key]
            slot, bslot = rot("ws", wslot, Bws)
            tot = sum(nk * ncols for (_, _, nk, _, ncols, _) in blocks)
            if first:
                st, bst = rot("wst", wstage, Bwst)
                for (W, r0, nk, c0, ncols, off) in blocks:
                    src = W[r0:r0 + nk * 128, c0:c0 + ncols].rearrange("(k p) c -> p k c", p=128)
                    dst = st[:, off:off + nk * ncols].rearrange("p (k c) -> p k c", c=ncols)
                    dma("sp", dst, src, [], [bst], "d_wst%d" % (rr["wst"] % 2))
                ceng = ("act", "dve", "pool")[pid % 3]
                cp(ceng, slot[:, 0:tot], st[:, 0:tot], [bst], [bslot])
                dma("sp", wscr[pid][:, 0:tot], slot[:, 0:tot], [bslot], [Bwscr[pid]], "d_wsc%d" % (rr["ws"] % NSLOT))
            else:
                dma("sp", slot[:, 0:tot], wscr[pid][:, 0:tot], [Bwscr[pid]], [bslot], "d_ws%d" % (rr["ws"] % NSLOT))
            return slot, bslot

        ln_state = {}

        def ln_begin():
            ln_state["pend"] = None
            ln_state["cnt"] = 0

        def ln_stats(k, N, last):
            sqs, bsq_ = ln_state["sq%d" % k]
            mm(psum[6][:, 0:N], ones_b[:], xb[:, k, 0:N], k == 0, last, [Bconst, Bxb[k]], Bps[6])
            mm(psum[7][:, 0:N], ones_b[:], sqs[:, 0:N], k == 0, last, [Bconst, bsq_], Bps[7])

        def ln_pre(k, N):
            cp("pool", xb[:, k, 0:N], x[:, k, 0:N], [Bx[k]], [Bxb[k]])
            sqs, bsq_ = rot("et", ET, BET)
            act(sqs[:, 0:N], x[:, k, 0:N], AF.Square, [Bx[k]], [bsq_])
            ln_state["sq%d" % k] = (sqs, bsq_)
            if k >= 1:
                ln_stats(k - 1, N, False)

        def layer_norm(N, li):
            ln_stats(7, N, True)
            s1, b1 = psum[6], Bps[6]
            s2, b2 = psum[7], Bps[7]
            mean, bm = rot("tmp", tmpf, Btmp)
            msq, bq = rot("tmp", tmpf, Btmp)
            rstd, br = rot("tmp", tmpf, Btmp)
            ts("dve", mean[:, 0:N], s1[:, 0:N], 1.0 / D, None, ALU.mult, None, [b1], [bm])
            tt("dve", msq[:, 0:N], mean[:, 0:N], mean[:, 0:N], ALU.mult, [bm], [bq])
            stt("dve", rstd[:, 0:N], s2[:, 0:N], 1.0 / D, msq[:, 0:N], ALU.mult, ALU.subtract, [b2, bq], [br])
            ts("dve", rstd[:, 0:N], rstd[:, 0:N], EPS_EFF, None, ALU.add, None, [br], [br])
            act(rstd[:, 0:N], rstd[:, 0:N], AF.Sqrt, [br], [br])
            P.op("dve", lambda e: e.reciprocal(out=rstd[:, 0:N], in_=rstd[:, 0:N]), reads=[br], writes=[br])
            for k in range(8):
                eng = "dve"
                tt(eng, x[:, k, 0:N], x[:, k, 0:N], mean[:, 0:N], ALU.subtract, [Bx[k], bm], [Bx[k]])
                tt(eng, x[:, k, 0:N], x[:, k, 0:N], rstd[:, 0:N], ALU.mult, [Bx[k], br], [Bx[k]])
                act(xb[:, k, 0:N], x[:, k, 0:N], AF.Identity, [Bx[k], Bconst], [Bxb[k]],
                    bias=lnb[:, li, k:k + 1], scale=lng[:, li, k:k + 1])
                ts("pool", x[:, k, 0:N], x[:, k, 0:N], lng[:, li, k:k + 1], lnb[:, li, k:k + 1], ALU.mult, ALU.add,
                   [Bx[k], Bconst], [Bx[k]])

        def ffn(N, Win, Wout, li, first, tag):
            for pi in range(11):
                slot, bs = get_piece((tag, "in", pi),
                                     [(Win, 0, 8, 256 * pi, 256, 0), (Win, 0, 8, DFF + 256 * pi, 256, 2048)], first)
                sv = slot[:, :].rearrange("p (a k c) -> p a k c", a=2, k=8)
                for jj in range(2):
                    j = 2 * pi + jj
                    A, bA = bank("mm")
                    for k in range(8):
                        mm(A[:, 0:N], sv[:, 0, k, jj * 128:(jj + 1) * 128], xb[:, k, 0:N], k == 0, k == 7,
                           [bs, Bxb[k]], bA)
                    Bm, bB = bank("mm")
                    for k in range(8):
                        mm(Bm[:, 0:N], sv[:, 1, k, jj * 128:(jj + 1) * 128], xb[:, k, 0:N], k == 0, k == 7,
                           [bs, Bxb[k]], bB)
                    t, bt = rot("tmp", tmpf, Btmp)
                    act(t[:, 0:N], A[:, 0:N], AF.Silu, [bA], [bt])
                    tt("dve", hbuf[:, j, 0:N], t[:, 0:N], Bm[:, 0:N], ALU.mult, [bt, bB], [Bh[j]])
            for c in range(8):
                slot, bs = get_piece((tag, "out", c), [(Wout, 0, NJ, 128 * c, 128, 0)], first)
                sv = slot[:, 0:NJ * 128].rearrange("p (k c) -> p k c", k=NJ)
                Y, bY = bank("mm")
                for k in range(NJ):
                    mm(Y[:, 0:N], sv[:, k, :], hbuf[:, k, 0:N], k == 0, k == NJ - 1, [bs, Bh[k]], bY)
                stt("dve", x[:, c, 0:N], Y[:, 0:N], 0.5 / ALPHA, x[:, c, 0:N], ALU.mult, ALU.add,
                    [bY, Bx[c]], [Bx[c]])
                ln_pre(c, N)
            layer_norm(N, li)

        def proj_res(N, W, src, Bsrc, tag, first):
            for pi in range(2):
                slot, bs = get_piece((tag, pi), [(W, 0, 8, 512 * pi, 512, 0)], first)
                sv = slot[:, :].rearrange("p (k c) -> p k c", k=8)
                for cc in range(4):
                    c = 4 * pi + cc
                    Y, bY = bank("mm")
                    for k in range(8):
                        mm(Y[:, 0:N], sv[:, k, cc * 128:(cc + 1) * 128], src[:, k, 0:N], k == 0, k == 7,
                           [bs, Bsrc[k]], bY)
                    stt("dve", x[:, c, 0:N], Y[:, 0:N], 1.0 / ALPHA, x[:, c, 0:N], ALU.mult, ALU.add,
                        [bY, Bx[c]], [Bx[c]])
                    ln_pre(c, N)

        def mem_phase():
            for s in range(2):
                st_, bst_ = rot("stg", stg, Bstg)
                dma("sp", st_[:, :], mem[s * 128:(s + 1) * 128, :], [], [bst_], "d_stg%d" % (rr["stg"] % 3))
                for kk in range(2):
                    pb, pbuf = bank("mm")
                    for q4 in range(4):
                        k = kk * 4 + q4
                        tr(pb[:, q4 * 128:(q4 + 1) * 128], st_[:, k * 128:(k + 1) * 128], [bst_], pbuf)
                    for q4 in range(4):
                        k = kk * 4 + q4
                        cp("dve" if kk else "act", hbuf[:, k, s * 128:(s + 1) * 128],
                           pb[:, q4 * 128:(q4 + 1) * 128], [pbuf], [Bh[k]])
            if STOP == 5:
                return
            for pi in range(4 if STOP != 6 else 1):
                slot, bs = rot("ws", wslot, Bws)
                src = w_xkv[:, 512 * pi:512 * (pi + 1)].rearrange("(k p) c -> p k c", p=128)
                P.op("pool", lambda e, slot=slot, src=src: e.dma_start(
                    out=slot[:, :].rearrange("p (k c) -> p k c", k=8), in_=src),
                    writes=[bs], dma_sem="d_ws%d" % (rr["ws"] % NSLOT))
                sv = slot[:, :].rearrange("p (k c) -> p k c", k=8)
                if pi < 2:
                    for cc in range(4):
                        j = 4 * pi + cc
                        pb, pbuf = bank("mm")
                        for k in range(8):
                            mm(pb[:, 0:256], sv[:, k, cc * 128:(cc + 1) * 128], hbuf[:, k, 0:256], k == 0, k == 7,
                               [bs, Bh[k]], pbuf)
                        cp("act", memKT[:, j, :], pb[:, 0:256], [pbuf], [Bconst])
                for s in range(2):
                    pb, pbuf = bank("mm")
                    for k in range(8):
                        mm(pb[:, :], hbuf[:, k, s * 128:(s + 1) * 128], sv[:, k, :], k == 0, k == 7,
                           [bs, Bh[k]], pbuf)
                    st_, bst_ = rot("stg", stg, Bstg)
                    cp("dve", st_[:, 0:512], pb[:, :], [pbuf], [bst_])
                    if pi >= 2:
                        cp("pool", memV[:, s, 512 * (pi - 2):512 * (pi - 1)], st_[:, 0:512], [bst_], [Bconst])
                        dma("sp", mv_p[s * 128:(s + 1) * 128, 512 * (pi - 2):512 * (pi - 1)], st_[:, 0:512],
                            [bst_], [], "d_stg%d" % (rr["stg"] % 3))
                    else:
                        dma("sp", mk_p[s * 128:(s + 1) * 128, 512 * pi:512 * (pi + 1)], st_[:, 0:512],
                            [bst_], [], "d_stg%d" % (rr["stg"] % 3))

        def cross_core(c0, c1):
            for hm in range(4):
                ets = []
                for mc in range(2):
                    S, bS = bank("sc")
                    for dc in range(2):
                        mm(S[:, c0:c1], memKT[:, hm * 2 + dc, mc * 128:(mc + 1) * 128], hbuf[:, hm * 2 + dc, c0:c1],
                           dc == 0, dc == 1, [Bconst, Bh[hm * 2 + dc]], bS)
                    et, bet = rot("et", ET, BET)
                    act(et[:, c0:c1], S[:, c0:c1], AF.Exp, [bS], [bet])
                    ets.append((et, bet))
                SM, bSM = bank("pv")
                for mc in range(2):
                    mm(SM[:, c0:c1], ones_b[:], ets[mc][0][:, c0:c1], mc == 0, mc == 1, [Bconst, ets[mc][1]], bSM)
                rc, brc = rot("tmp", tmpf, Btmp)
                P.op("dve", lambda e, rc=rc, SM=SM: e.reciprocal(out=rc[:, c0:c1], in_=SM[:, c0:c1]), reads=[bSM], writes=[brc])
                for dc in range(2):
                    O, bO = bank("pv")
                    for mc in range(2):
                        mm(O[:, c0:c1], memV[:, mc, hm * 256 + dc * 128:hm * 256 + (dc + 1) * 128], ets[mc][0][:, c0:c1],
                           mc == 0, mc == 1, [Bconst, ets[mc][1]], bO)
                    tt("dve", hbuf[:, 8 + hm * 2 + dc, c0:c1], O[:, c0:c1], rc[:, c0:c1], ALU.mult, [bO, brc], [Bh[8 + hm * 2 + dc]])


        srow = sb("srow", [4, 3, 512], F32)
        Bsrow = [Buf("srow%d" % i) for i in range(3)]
        vrow_b = sb("vrow_b", [128, 512], BF16)
        vnrow_b = sb("vnrow_b", [4, 512], BF16)
        dW = sb("dW", [4, 4, 4], BF16)
        w00 = sb("w00", [4, 4], F32)
        dnew = sb("dnew", [128, 4], F32)
        MBall = sb("MBall", [128, 4, 8], BF16)
        MBf = sb("MBf", [128, 4, 8], F32)
        t1c = sb("t1c", [128, 1], F32)
        BSl = sb("BSl", [128, 4, 2], BF16)
        iota_f = sb("iota_f", [128, 1], F32)
        ptb = sb("ptb", [128, 128], I32)
        idx_all = sb("idx_all", [128, 128], I32)
        Bidx = Buf("idx")
        Bsm = Buf("smisc")

        def sample_phase():
            N = NS
            first = True
            dma("sp", iota_f[:], c_iota, [], [Bsm], "d_s1")
            dma("sp", w00[:].unsqueeze(2), bass.AP(sgu_w.tensor, 0, [[0, 4], [128 * 128, 4], [1, 1]]), [], [Bsm], "d_s2", nonc=True)
            dma("sp", dnew[:].unsqueeze(2), bass.AP(dscr_d.tensor, 127, [[0, 128], [383, 4], [1, 1]]), [Bdd], [Bsm], "d_s3", nonc=True)
            for g in range(4):
                ts("dve", dW[:, g, :], ident_f[0:4, 0:4], w00[:, g:g + 1], None, ALU.mult, None, [Bsm, Bconst], [Bsm])
            for b in range(4):
                ts("dve", t1c[:, 0:1], ident_f[:, b:b + 1], -NEG, NEG, ALU.mult, ALU.add, [Bconst, Bsm], [Bsm])
                for c in range(2):
                    ts("dve", MBf[:, b, :].rearrange("p (h c) -> p h c", c=2)[:, :, c], dnew[:, :], ident_f[:, b:b + 1],
                       t1c[:, 0:1], ALU.mult, ALU.add, [Bsm, Bconst], [Bsm])
            cp("dve", MBall[:, :, :], MBf[:, :, :], [Bsm], [Bsm])
            for c in range(2):
                cp("dve", BSl[:, :, c:c + 1], BT[:, :, 128:129], [Bconst], [Bsm])
            if STOP == 10:
                return
            st_, bst_ = rot("stg", stg, Bstg)
            dma("sp", st_[0:4, :], xs, [], [bst_], "d_stg%d" % (rr["stg"] % 3))
            pb, pbuf = bank("mm")
            for k in range(8):
                P.op("pe", lambda e, pb=pb, st_=st_, k=k: e.transpose(pb[:, k * 4:(k + 1) * 4], st_[0:4, k * 128:(k + 1) * 128],
                                                                  ident_f[0:4, 0:4]), reads=[bst_, Bconst], writes=[pbuf])
            P.op("dve", lambda e, pb=pb: e.tensor_copy(out=x[:, :, 0:4], in_=pb[:, 0:32].rearrange("p (k b) -> p k b", k=8)),
                 reads=[pbuf], writes=Bx)
            P.op("pool", lambda e: e.tensor_copy(out=xb[:, :, 0:4], in_=x[:, :, 0:4]), reads=Bx, writes=Bxb)
            ffn(N, w_f1i, w_f1o, 0, first, "f1")
            if STOP == 11:
                return
            slot, bs = get_piece(("mi", 0), [(w_mi, 0, 8, 0, 512, 0)], first)
            sv = slot[:, :].rearrange("p (k c) -> p k c", k=8)
            for h in range(4):
                pb, pbuf = bank("mm")
                for k in range(8):
                    mm(pb[:, 0:N], sv[:, k, h * 128:(h + 1) * 128], xb[:, k, 0:N], k == 0, k == 7, [bs, Bxb[k]], pbuf)
                act(qT[:, h, 0:N], pb[:, 0:N], AF.Copy, [pbuf], [BqT[h]], scale=0.125)
            slot, bs = get_piece(("mi", 1), [(w_mi, 0, 8, 512, 512, 0)], first)
            sv = slot[:, :].rearrange("p (k c) -> p k c", k=8)
            P.op("pool", lambda e: e.memset(KT[:, 15, :, :].rearrange("p h k -> p (h k)"), 0.0), writes=[BKT[15]])
            P.op("pool", lambda e: e.memset(vrow_b[:, :], 0.0), writes=[Bsm])
            for h in range(4):
                pb, pbuf = bank("mm")
                for k in range(8):
                    mm(pb[:, 0:N], sv[:, k, h * 128:(h + 1) * 128], xb[:, k, 0:N], k == 0, k == 7, [bs, Bxb[k]], pbuf)
                cp("dve", KT[:, 15, h, 0:4], pb[:, 0:N], [pbuf], [BKT[15]])
            pb, pbuf = bank("mm")
            for k in range(8):
                mm(pb[0:4, :], xb[:, k, 0:4], sv[:, k, :], k == 0, k == 7, [bs, Bxb[k]], pbuf)
            cp("act", srow[:, 0, :], pb[0:4, :], [pbuf], [Bsrow[0], Bsm])
            dma("sp", k_s, srow[:, 0, :], [Bsrow[0]], [], "d_s4")
            slot, bs = get_piece(("mi", 2), [(w_mi, 0, 8, 1024, 512, 0)], first)
            sv = slot[:, :].rearrange("p (k c) -> p k c", k=8)
            pb, pbuf = bank("mm")
            for k in range(8):
                mm(pb[0:4, :], xb[:, k, 0:4], sv[:, k, :], k == 0, k == 7, [bs, Bxb[k]], pbuf)
            cp("act", srow[:, 1, :], pb[0:4, :], [pbuf], [Bsrow[1]])
            cp("pool", vrow_b[0:4, :], srow[:, 1, :], [Bsrow[1]], [Bsm])
            dma("sp", v_s, srow[:, 1, :], [Bsrow[1]], [], "d_s5")
            slot, bs = get_piece(("mi", 3), [(w_mi, 0, 8, 1536, 512, 0)], first)
            sv = slot[:, :].rearrange("p (k c) -> p k c", k=8)
            for g in range(4):
                pb, pbuf = bank("mm")
                for k in range(8):
                    mm(pb[:, 0:N], sv[:, k, g * 128:(g + 1) * 128], xb[:, k, 0:N], k == 0, k == 7, [bs, Bxb[k]], pbuf)
                act(uT[:, g, 0:N], pb[:, 0:N], AF.Gelu, [pbuf], BuT2[g])
            slot, bs = get_piece(("mi", 4), [(w_mi, 0, 8, 2048, 512, 0)], first)
            sv = slot[:, :].rearrange("p (k c) -> p k c", k=8)
            pb, pbuf = bank("mm")
            for k in range(8):
                mm(pb[0:4, :], xb[:, k, 0:4], sv[:, k, :], k == 0, k == 7, [bs, Bxb[k]], pbuf)
            gv = srow[:, 2, :]
            bgv = Bsrow[2]
            cl, bcl = rot("col", col, Bcol)
            act(gv, pb[0:4, :], AF.Gelu, [pbuf], [bgv])
            P.op("dve", lambda e, cl=cl: e.tensor_reduce(out=cl[0:4, 0:1], in_=srow[:, 2, :], axis=AX.X, op=ALU.add),
                 reads=[bgv], writes=[bcl])
            ts("dve", cl[0:4, 1:2], cl[0:4, 0:1], -1.0 / 512, None, ALU.mult, None, [bcl], [bcl])
            ts("dve", gv, gv, cl[0:4, 1:2], None, ALU.add, None, [bgv, bcl], [bgv])
            junk, bj = rot("tmp", tmpf, Btmp)
            tt("pool", junk[0:4, :], gv, gv, ALU.mult, [bgv], [bj])
            P.op("dve", lambda e, junk=junk, cl=cl: e.tensor_reduce(out=cl[0:4, 2:3], in_=junk[0:4, :], axis=AX.X, op=ALU.add),
                 reads=[bj], writes=[bcl])
            ts("dve", cl[0:4, 3:4], cl[0:4, 2:3], 1.0 / 512, LN_EPS, ALU.mult, ALU.add, [bcl], [bcl])
            act(cl[0:4, 3:4], cl[0:4, 3:4], AF.Sqrt, [bcl], [bcl])
            P.op("dve", lambda e, cl=cl: e.reciprocal(out=cl[0:4, 3:4], in_=cl[0:4, 3:4]), reads=[bcl], writes=[bcl])
            stt("dve", gv, gv, cl[0:4, 3:4], sgug[0:4, :], ALU.mult, ALU.mult, [bgv, bcl, Bconst], [bgv])
            tt("dve", gv, gv, sgub_ln[0:4, :], ALU.add, [bgv, Bconst], [bgv])
            cp("pool", vnrow_b[:, :], gv, [bgv], [Bsm])
            dma("sp", g_s, gv, [bgv], [], "d_s6")
            pb, pbuf = bank("mm")
            for g in range(4):
                mm(pb[:, g * 4:(g + 1) * 4], vnrow_b[0:4, g * 128:(g + 1) * 128], dW[0:4, g, :], True, False, [Bsm], pbuf)
                mm(pb[:, g * 4:(g + 1) * 4], ones_b[0:1, :], sgub_row[0:1, g * 128:g * 128 + 1].to_broadcast([1, 4]),
                   False, True, [Bconst], pbuf)
            P.op("dve", lambda e, pb=pb: e.tensor_tensor(out=catT[:, 4:8, 0:4], in0=pb[:, 0:16].rearrange("p (g t) -> p g t", g=4),
                                                       in1=uT[:, :, 0:4], op=ALU.mult), reads=[pbuf] + BuT, writes=Bcat[4:8])
            if STOP == 12:
                return
            set_attn()
            for b in range(4):
                dma("sp", ptb[:, :], bass.AP(ptab.tensor, b * 128, [[0, 128], [1, 128]]), [], [Bidx], "d_s7", nonc=True)
                ts("dve", idx_all[:, :], ptb[:, :], 128.0, iota_f[:, 0:1], ALU.mult, ALU.add, [Bidx, Bsm], [Bidx])
                O, bO = bank("pv")
                SM, bSM = bank("pv")
                jl = list(range(129))
                if STOP in (14, 15, 16):
                    jl = list(range(8))
                if STOP == 17:
                    jl = list(range(8)) + [127, 128]
                if STOP == 18:
                    jl = list(range(8)) + [127]
                if STOP == 19:
                    jl = list(range(8)) + [128]
                jlast = jl[-1]
                for j in jl:
                    sl = j % 8
                    if j < 128:
                        kst, bkst = rot("stg", stg, Bstg)
                        semn = "d_stg%d" % (rr["stg"] % 3)
                        P.op("pool", lambda e, kst=kst, j=j: e.indirect_dma_start(
                            out=kst[:, :], out_offset=None, in_=cache_kv,
                            in_offset=bass.IndirectOffsetOnAxis(ap=idx_all[:, j:j + 1], axis=0)),
                            reads=[Bidx], writes=[bkst], dma_sem=semn)
                        cp("dve" if j % 2 else "act", Vt[:, sl, :], kst[:, 512:1024], [bkst], [BVt[sl]])
                        pbk, pbkb = bank("mm")
                        for h in range(4):
                            tr(pbk[:, h * 128:(h + 1) * 128], kst[:, h * 128:(h + 1) * 128], [bkst], pbkb)
                        cp("act" if j % 2 else "dve", KT[:, sl, :, :].rearrange("p h k -> p (h k)"), pbk[:, :], [pbkb], [BKT[sl]])
                        ktile, bkt, nkeys, vt_l = sl, BKT[sl], 128, None
                        if STOP in (14, 15):
                            continue
                    else:
                        ktile, bkt, nkeys = 15, BKT[15], 128
                    S, bS = bank("sc")
                    for h in range(4):
                        for c in range(2):
                            hc = h * 2 + c
                            special = (j >= 127)
                            mm(S[0:nkeys, hc:hc + 1], KT[c * 64:(c + 1) * 64, ktile, h, 0:nkeys],
                               qT[c * 64:(c + 1) * 64, h, b:b + 1], hc == 0, (hc == 7) and not special,
                               [bkt, BqT[h]], bS, skip=True)
                    if j == 127:
                        mm(S[:, 0:8], ident_b[:], BSl[:, :, :].rearrange("p h c -> p (h c)"), False, True, [Bconst, Bsm], bS, skip=True)
                    if j == 128:
                        mm(S[:, 0:8], ident_b[:], MBall[:, b, :], False, True, [Bconst, Bsm], bS, skip=True)
                    et, bet = rot("et", ET, BET)
                    act(et[0:nkeys, 0:8], S[0:nkeys, 0:8], AF.Exp, [bS], [bet])
                    for h in range(4):
                        if j < 128:
                            lhs = Vt[:, sl, h * 128:(h + 1) * 128]
                            rd = [BVt[sl], bet]
                        else:
                            lhs = vrow_b[:, h * 128:(h + 1) * 128]
                            rd = [Bsm, bet]
                        mm(O[:, 2 * h:2 * h + 2], lhs, et[0:nkeys, 2 * h:2 * h + 2], (j == 0 and h == 0), (j == jlast and h == 3),
                           rd, bO, skip=True)
                    mm(SM[:, 0:8], ones_b[0:nkeys, :], et[0:nkeys, 0:8], j == 0, j == jlast, [Bconst, bet], bSM, skip=True)
                if STOP in (14, 15):
                    continue
                rc, brc = rot("tmp", tmpf, Btmp)
                P.op("dve", lambda e, rc=rc, SM=SM: e.reciprocal(out=rc[:, 0:8], in_=SM[:, 0:8]), reads=[bSM], writes=[brc])
                tt("dve", rc[:, 0:8], O[:, 0:8], rc[:, 0:8], ALU.mult, [bO, brc], [brc])
                rv = rc[:, 0:8].rearrange("p (h c) -> p h c", c=2)
                stt("dve", rc[:, 8:12], rv[:, :, 1], neglam[:, 0:1], rv[:, :, 0], ALU.mult, ALU.add, [brc, Bconst], [brc])
                sqh, bsqh = rot("et", ET, BET)
                act(sqh[:, 0:4], rc[:, 8:12], AF.Square, [brc], [bsqh])
                pb, pbuf = bank("mm")
                mm(pb[:, 0:4], ones_b[:], sqh[:, 0:4], True, True, [Bconst, bsqh], pbuf)
                ts("dve", rc[:, 16:20], pb[:, 0:4], 1.0 / 128, LN_EPS, ALU.mult, ALU.add, [pbuf, brc], [brc])
                act(rc[:, 16:20], rc[:, 16:20], AF.Sqrt, [brc], [brc])
                P.op("dve", lambda e, rc=rc: e.reciprocal(out=rc[:, 16:20], in_=rc[:, 16:20]), reads=[brc], writes=[brc])
                tt("dve", rc[:, 16:20], rc[:, 16:20], rc[:, 8:12], ALU.mult, [brc], [brc])
                P.op("act", lambda e, rc=rc, b=b: e.activation(out=catT[:, 0:4, b:b + 1], in_=rc[:, 16:20].unsqueeze(2),
                                                              func=AF.Copy, scale=sublg[:, 0:1]),
                     reads=[brc, Bconst], writes=Bcat[0:4])
            if STOP in (13, 14, 15, 16, 17, 18, 19):
                return
            set_dense()
            proj_res(N, w_mo, catT, Bcat, "mo", first)
            layer_norm(N, 1)
            for pi in range(2):
                slot, bs = get_piece(("xq", pi), [(w_xq, 0, 8, 512 * pi, 512, 0)], first)
                sv = slot[:, :].rearrange("p (k c) -> p k c", k=8)
                for cc in range(4):
                    j = 4 * pi + cc
                    pb, pbuf = bank("mm")
                    for k in range(8):
                        mm(pb[:, 0:N], sv[:, k, cc * 128:(cc + 1) * 128], xb[:, k, 0:N], k == 0, k == 7,
                           [bs, Bxb[k]], pbuf)
                    act(hbuf[:, j, 0:N], pb[:, 0:N], AF.Copy, [pbuf], [Bh[j]], scale=1.0 / 16)
            set_attn()
            for b in range(4):
                for mc in range(2):
                    st_, bst_ = rot("stg", stg, Bstg)
                    dma("sp", st_[:, :], cmk[b, mc * 128:(mc + 1) * 128, :], [], [bst_], "d_stg%d" % (rr["stg"] % 3))
                    for kk in range(2):
                        pb, pbuf = bank("mm")
                        for q4 in range(4):
                            k = kk * 4 + q4
                            tr(pb[:, q4 * 128:(q4 + 1) * 128], st_[:, k * 128:(k + 1) * 128], [bst_], pbuf)
                        P.op("dve" if kk else "act", (lambda e, pb=pb, kk=kk, mc=mc: e.tensor_copy(
                            out=memKT[:, kk * 4:(kk + 1) * 4, mc * 128:(mc + 1) * 128],
                            in_=pb[:, :].rearrange("p (q m) -> p q m", q=4))) if kk else
                            (lambda e, pb=pb, kk=kk, mc=mc: e.copy(
                                out=memKT[:, kk * 4:(kk + 1) * 4, mc * 128:(mc + 1) * 128],
                                in_=pb[:, :].rearrange("p (q m) -> p q m", q=4))),
                            reads=[pbuf], writes=[Bconst])
                    st2, bst2 = rot("stg", stg, Bstg)
                    dma("sp", st2[:, :], cmv[b, mc * 128:(mc + 1) * 128, :], [], [bst2], "d_stg%d" % (rr["stg"] % 3))
                    cp("pool", memV[:, mc, :], st2[:, :], [bst2], [Bconst])
                if STOP != 22:
                    cross_core(b, b + 1)
            if STOP in (20, 22):
                return
            set_dense()
            proj_res(N, w_xo, hbuf[:, 8:16, :], Bh[8:16], "xo", first)
            layer_norm(N, 2)
            ffn(N, w_f2i, w_f2o, 3, first, "f2")
            if STOP == 21:
                return
            st_, bst_ = rot("stg", stg, Bstg)
            for kk in range(2):
                pb, pbuf = bank("mm")
                for q4 in range(4):
                    k = kk * 4 + q4
                    mm(pb[0:4, q4 * 128:(q4 + 1) * 128], x[:, k, 0:4], ident_f[:, :], True, True, [Bx[k], Bconst], pbuf)
                cp("dve", st_[0:4, kk * 512:(kk + 1) * 512], pb[0:4, :], [pbuf], [bst_])
            dma("sp", y_s, st_[0:4, :], [bst_], [], "d_stg%d" % (rr["stg"] % 3))

        def prompt_tile(ti, first):
            N = TT
            t0 = ti * TT
            if ti == 4:
                for eng_ in ("act", "dve", "pool", "pe"):
                    P.op(eng_, None, writes=[Bwst[0], Bwst[1]])
            for s in range(4):
                st_, bst_ = rot("stg", stg, Bstg)
                dma("sp", st_[:, :], xp[t0 + s * 128:t0 + (s + 1) * 128, :], [], [bst_], "d_stg%d" % (rr["stg"] % 3))
                for kk in range(2):
                    pb, pbuf = bank("mm")
                    for q4 in range(4):
                        k = kk * 4 + q4
                        tr(pb[:, q4 * 128:(q4 + 1) * 128], st_[:, k * 128:(k + 1) * 128], [bst_], pbuf)
                    for q4 in range(4):
                        k = kk * 4 + q4
                        e1 = "dve" if kk else "act"
                        cp(e1, x[:, k, s * 128:(s + 1) * 128], pb[:, q4 * 128:(q4 + 1) * 128], [pbuf], [Bx[k]])
            for k in range(8):
                cp(("pool", "dve", "act", "pool")[k % 4], xb[:, k, :], x[:, k, :], [Bx[k]], [Bxb[k]])
            ffn(N, w_f1i, w_f1o, 0, first, "f1")
            slot, bs = get_piece(("mi", 0), [(w_mi, 0, 8, 0, 512, 0)], first)
            sv = slot[:, :].rearrange("p (k c) -> p k c", k=8)
            for h in range(4):
                pb, pbuf = bank("mm")
                for k in range(8):
                    mm(pb[:, 0:N], sv[:, k, h * 128:(h + 1) * 128], xb[:, k, 0:N], k == 0, k == 7, [bs, Bxb[k]], pbuf)
                act(qT[:, h, :], pb[:, 0:N], AF.Copy, [pbuf], [BqT[h]], scale=0.125)
            slot, bs = get_piece(("mi", 1), [(w_mi, 0, 8, 512, 512, 0)], first)
            sv = slot[:, :].rearrange("p (k c) -> p k c", k=8)
            for h in range(4):
                pb, pbuf = bank("mm")
                for k in range(8):
                    mm(pb[:, 0:N], sv[:, k, h * 128:(h + 1) * 128], xb[:, k, 0:N], k == 0, k == 7, [bs, Bxb[k]], pbuf)
                for s in range(4):
                    kt = ti * 4 + s
                    cp("act" if h % 2 else "dve", KT[:, kt, h, :], pb[:, s * 128:(s + 1) * 128], [pbuf], [BKT[kt]])
            for s in range(4):
                pb, pbuf = bank("mm")
                for k in range(8):
                    mm(pb[:, :], xb[:, k, s * 128:(s + 1) * 128], sv[:, k, :], k == 0, k == 7, [bs, Bxb[k]], pbuf)
                st_, bst_ = rot("stg", stg, Bstg)
                cp("act", st_[:, 0:512], pb[:, :], [pbuf], [bst_])
                dma("sp", k_p[t0 + s * 128:t0 + (s + 1) * 128, :], st_[:, 0:512], [bst_], [], "d_stg%d" % (rr["stg"] % 3))
            slot, bs = get_piece(("mi", 2), [(w_mi, 0, 8, 1024, 512, 0)], first)
            sv = slot[:, :].rearrange("p (k c) -> p k c", k=8)
            for s in range(4):
                kt = ti * 4 + s
                pb, pbuf = bank("mm")
                for k in range(8):
                    mm(pb[:, :], xb[:, k, s * 128:(s + 1) * 128], sv[:, k, :], k == 0, k == 7, [bs, Bxb[k]], pbuf)
                st_, bst_ = rot("stg", stg, Bstg)
                cp("act", st_[:, 0:512], pb[:, :], [pbuf], [bst_])
                cp("pool", Vt[:, kt, :], st_[:, 0:512], [bst_], [BVt[kt]])
                dma("sp", v_p[t0 + s * 128:t0 + (s + 1) * 128, :], st_[:, 0:512], [bst_], [], "d_stg%d" % (rr["stg"] % 3))
            slot, bs = get_piece(("mi", 3), [(w_mi, 0, 8, 1536, 512, 0)], first)
            sv = slot[:, :].rearrange("p (k c) -> p k c", k=8)
            for g in range(4):
                pb, pbuf = bank("mm")
                for k in range(8):
                    mm(pb[:, 0:N], sv[:, k, g * 128:(g + 1) * 128], xb[:, k, 0:N], k == 0, k == 7, [bs, Bxb[k]], pbuf)
                act(uT[:, g, :], pb[:, 0:N], AF.Gelu, [pbuf], BuT2[g])
            slot, bs = get_piece(("mi", 4), [(w_mi, 0, 8, 2048, 512, 0)], first)
            sv = slot[:, :].rearrange("p (k c) -> p k c", k=8)
            vbs = []
            for s in range(4):
                pb, pbuf = bank("mm")
                for k in range(8):
                    mm(pb[:, :], xb[:, k, s * 128:(s + 1) * 128], sv[:, k, :], k == 0, k == 7, [bs, Bxb[k]], pbuf)
                gv, bgv = tmpf[s], Btmp[s]
                cl, bcl = col[s], Bcol[s]
                act(gv[:, :], pb[:, :], AF.Gelu, [pbuf], [bgv])
                vbs.append((gv, bgv, cl, bcl))
            for s in range(4):
                gv, bgv, cl, bcl = vbs[s]
                P.op("dve", lambda e, gv=gv, cl=cl: e.tensor_reduce(out=cl[:, 0:1], in_=gv[:, :], axis=AX.X, op=ALU.add),
                     reads=[bgv], writes=[bcl])
                ts("dve", cl[:, 1:2], cl[:, 0:1], -1.0 / 512, None, ALU.mult, None, [bcl], [bcl])
                ts("dve", gv[:, :], gv[:, :], cl[:, 1:2], None, ALU.add, None, [bgv, bcl], [bgv])
                tt("pool", oacc[:, :], gv[:, :], gv[:, :], ALU.mult, [bgv], [Boacc])
                P.op("dve", lambda e, cl=cl: e.tensor_reduce(out=cl[:, 2:3], in_=oacc[:, :], axis=AX.X, op=ALU.add),
                     reads=[Boacc], writes=[bcl])
                ts("dve", cl[:, 3:4], cl[:, 2:3], 1.0 / 512, LN_EPS, ALU.mult, ALU.add, [bcl], [bcl])
            for s in range(4):
                gv, bgv, cl, bcl = vbs[s]
                act(cl[:, 3:4], cl[:, 3:4], AF.Sqrt, [bcl], [bcl])
            for s in range(4):
                gv, bgv, cl, bcl = vbs[s]
                P.op("dve", lambda e, cl=cl: e.reciprocal(out=cl[:, 3:4], in_=cl[:, 3:4]), reads=[bcl], writes=[bcl])
                stt("dve", gv[:, :], gv[:, :], cl[:, 3:4], sgug[:, :], ALU.mult, ALU.mult, [bgv, bcl, Bconst], [bgv])
                tt("pool", vn[:, s, :], gv[:, :], sgub_ln[:, :], ALU.add, [bgv, Bconst], [Bvn[s]])
            set_attn()
            nk = 4 * (ti + 1)
            def rms_head(h):
                oa, boa = (oacc, Boacc) if h % 2 == 0 else (oacc2, Boacc2)
                sqh, bsqh = rot("et", ET, BET)
                act(sqh[:, :], oa[:, :], AF.Square, [boa], [bsqh])
                pb, pbuf = bank("mm")
                mm(pb[:, :], ones_b[:], sqh[:, :], True, True, [Bconst, bsqh], pbuf)
                rs_, brs = rot("tmp", tmpf, Btmp)
                ts("dve", rs_[:, :], pb[:, :], 1.0 / 128, LN_EPS, ALU.mult, ALU.add, [pbuf], [brs])
                act(rs_[:, :], rs_[:, :], AF.Sqrt, [brs], [brs])
                P.op("dve", lambda e, rs_=rs_: e.reciprocal(out=rs_[:, :], in_=rs_[:, :]), reads=[brs], writes=[brs])
                tt("dve", rs_[:, :], rs_[:, :], oa[:, :], ALU.mult, [brs, boa], [brs])
                act(catT[:, h, :], rs_[:, :], AF.Copy, [brs, Bconst], [Bcat[h]], scale=sublg[:, 0:1])
            for h in range(4):
                oa, boa = (oacc, Boacc) if h % 2 == 0 else (oacc2, Boacc2)
                for c in range(2):
                    if c == 1 and h >= 1:
                        rms_head(h - 1)
                    O, bO = bank("pv")
                    SM, bSM = bank("pv")
                    for kt in range(nk):
                        r = kt - 4 * ti
                        q0 = max(r, 0) * 128
                        S, bS = bank("sc")
                        spec = None
                        if kt == 4 * ti - 1:
                            spec = (0, 128, 128)
                        elif r >= 0:
                            ln_ = min(256, 512 - q0)
                            spec = (q0, ln_, 0)
                        mm(S[:, q0:512], KT[c * 64:(c + 1) * 64, kt, h, :], qT[c * 64:(c + 1) * 64, h, q0:512],
                           True, spec is None, [BKT[kt], BqT[h]], bS, skip=True)
                        if spec is not None:
                            cs, ln_, bo = spec
                            mm(S[:, cs:cs + ln_], ident_b[:], BT[:, h, bo:bo + ln_], False, True, [Bconst], bS, skip=True)
                        et, bet = rot("et", ET, BET)
                        act(et[:, q0:512], S[:, q0:512], AF.Exp, [bS, Bconst], [bet], bias=far[:, h:h + 1], scale=1.0)
                        mm(O[:, q0:512], Vt[:, kt, h * 128:(h + 1) * 128], et[:, q0:512], kt == 0, kt == nk - 1,
                           [BVt[kt], bet], bO, skip=True)
                        mm(SM[:, q0:512], ones_b[:], et[:, q0:512], kt == 0, kt == nk - 1, [Bconst, bet], bSM, skip=True)
                    rc, brc = rot("tmp", tmpf, Btmp)
                    P.op("dve", lambda e, rc=rc, SM=SM: e.reciprocal(out=rc[:, :], in_=SM[:, :]), reads=[bSM], writes=[brc])
                    if c == 0:
                        tt("dve", oa[:, :], O[:, :], rc[:, :], ALU.mult, [bO, brc], [boa])
                    else:
                        tt("dve", rc[:, :], O[:, :], rc[:, :], ALU.mult, [bO, brc], [brc])
                        stt("dve", oa[:, :], rc[:, :], neglam[:, 0:1], oa[:, :], ALU.mult, ALU.add,
                            [brc, boa, Bconst], [boa])
            rms_head(3)
            for s in range(4):
                pb, pbuf = bank("mm")
                for g in range(4):
                    mm(pb[:, g * 128:(g + 1) * 128], vn[:, s, g * 128:(g + 1) * 128], trilWT[:, g, :], True, False,
                       [Bvn[s], Bconst], pbuf)
                    mm(pb[:, g * 128:(g + 1) * 128], ones_b[0:1, :], sgub_row[0:1, g * 128:(g + 1) * 128], False, True,
                       [Bconst], pbuf)
                P.op("dve", lambda e, pb=pb, s=s: e.tensor_tensor(
                    out=catT[:, 4:8, s * 128:(s + 1) * 128],
                    in0=pb[:, :].rearrange("p (g t) -> p g t", g=4),
                    in1=uT[:, :, s * 128:(s + 1) * 128], op=ALU.mult),
                    reads=[pbuf] + BuT, writes=Bcat[4:8])
            set_dense()
            proj_res(N, w_mo, catT, Bcat, "mo", first)
            layer_norm(N, 1)
            for pi in range(2):
                slot, bs = get_piece(("xq", pi), [(w_xq, 0, 8, 512 * pi, 512, 0)], first)
                sv = slot[:, :].rearrange("p (k c) -> p k c", k=8)
                for cc in range(4):
                    j = 4 * pi + cc
                    pb, pbuf = bank("mm")
                    for k in range(8):
                        mm(pb[:, 0:N], sv[:, k, cc * 128:(cc + 1) * 128], xb[:, k, 0:N], k == 0, k == 7,
                           [bs, Bxb[k]], pbuf)
                    act(hbuf[:, j, :], pb[:, 0:N], AF.Copy, [pbuf], [Bh[j]], scale=1.0 / 16)
            set_attn()
            cross_core(0, N)
            set_dense()
            proj_res(N, w_xo, hbuf[:, 8:16, :], Bh[8:16], "xo", first)
            layer_norm(N, 2)
            ffn(N, w_f2i, w_f2o, 3, first, "f2")
            for s in range(4):
                st_, bst_ = rot("stg", stg, Bstg)
                for kk in range(2):
                    pb, pbuf = bank("mm")
                    for q4 in range(4):
                        k = kk * 4 + q4
                        tr(pb[:, q4 * 128:(q4 + 1) * 128], x[:, k, s * 128:(s + 1) * 128], [Bx[k]], pbuf)
                    cp("act" if kk else "dve", st_[:, kk * 512:(kk + 1) * 512], pb[:, :], [pbuf], [bst_])
                dma("sp", y_p[t0 + s * 128:t0 + (s + 1) * 128, :], st_[:, :], [bst_], [], "d_stg%d" % (rr["stg"] % 3))


        P.barrier()
        sample_phase()
        if STOP >= 10:
            P.emit(nc); return nc
        P.barrier()
        mem_phase()
        if STOP in (5, 6):
            P.emit(nc); return nc
        for ti in range(NT_RUN):
            prompt_tile(ti, False)
        P.emit(nc)
    return nc


NT_RUN = NT
STOP = 0


def kernel(**inp):
    f32 = np.float32
    consts = _host_consts()
    nc = build_nc()
    lam_in = np.stack([inp["lambda_q1"][0], inp["lambda_k1"][0], inp["lambda_q2"][0], inp["lambda_k2"][0]]).astype(f32)
    shared = {
        "rel_bias": np.ascontiguousarray(inp["rel_bias"], dtype=f32),
        "ln_g": np.ascontiguousarray(inp["ln_g"][0]),
        "ln_b": np.ascontiguousarray(inp["ln_b"][0]),
        "ffn1_w_in": np.ascontiguousarray(inp["ffn1_w_in"][0]),
        "ffn1_w_out": np.ascontiguousarray(inp["ffn1_w_out"][0]),
        "w_mix_in": np.ascontiguousarray(inp["w_mix_in"][0]),
        "w_mix_out": np.ascontiguousarray(inp["w_mix_out"][0]),
        "lam_in": lam_in,
        "subln_g": np.ascontiguousarray(inp["subln_g"][0].reshape(128, 1)),
        "sgu_ln_g": np.ascontiguousarray(inp["sgu_ln_g"][0].reshape(1, 512)),
        "sgu_ln_b": np.ascontiguousarray(inp["sgu_ln_b"][0].reshape(1, 512)),
        "sgu_w": np.ascontiguousarray(inp["sgu_w"][0]),
        "sgu_b": np.ascontiguousarray(inp["sgu_b"][0].reshape(1, 512)),
        "xq_w": np.ascontiguousarray(inp["xq_w"][0]),
        "xkv_w": np.ascontiguousarray(inp["xkv_w"][0]),
        "xo_w": np.ascontiguousarray(inp["xo_w"][0]),
        "ffn2_w_in": np.ascontiguousarray(inp["ffn2_w_in"][0]),
        "ffn2_w_out": np.ascontiguousarray(inp["ffn2_w_out"][0]),
    }
    shared.update(consts)
    kv = np.empty((NPHYS * 128, 1024), dtype=f32)
    kv[:, 0:512] = np.asarray(inp["cache_k"]).reshape(NPHYS * 128, 512)
    kv[:, 512:1024] = np.asarray(inp["cache_v"]).reshape(NPHYS * 128, 512)
    shared["cache_kv"] = kv
    in_maps = []
    for c in range(8):
        m = dict(shared)
        m["xp"] = np.ascontiguousarray(inp["x_prompt"][c])
        m["mem"] = np.ascontiguousarray(inp["mem_prompt"][c])
        m["xs"] = np.ascontiguousarray(inp["x_sample"][NS * c:NS * (c + 1), 0, :])
        m["cmk"] = np.ascontiguousarray(inp["cache_mem_k"][0, NS * c:NS * (c + 1)]).reshape(NS, 256, D)
        m["cmv"] = np.ascontiguousarray(inp["cache_mem_v"][0, NS * c:NS * (c + 1)]).reshape(NS, 256, D)
        m["ptab"] = np.ascontiguousarray(inp["page_table"][NS * c:NS * (c + 1)]).astype(np.int32)
        in_maps.append(m)
    res = run_bass_kernel_spmd(nc, in_maps, core_ids=list(range(8)))
    R = res.results
    y_p = np.stack([R[c]["y_p"] for c in range(8)])
    k_p = np.stack([R[c]["k_p"] for c in range(8)]).reshape(1, 8, SEQ, 4, 128)
    v_p = np.stack([R[c]["v_p"] for c in range(8)]).reshape(1, 8, SEQ, 4, 128)
    mk_p = np.stack([R[c]["mk_p"] for c in range(8)]).reshape(1, 8, 256, 4, 256)
    mv_p = np.stack([R[c]["mv_p"] for c in range(8)]).reshape(1, 8, 256, 4, 256)
    y_s = np.concatenate([R[c]["y_s"] for c in range(8)]).reshape(32, 1, D)
    k_s = np.concatenate([R[c]["k_s"] for c in range(8)]).reshape(1, 32, 1, 4, 128)
    v_s = np.concatenate([R[c]["v_s"] for c in range(8)]).reshape(1, 32, 1, 4, 128)
    g_s = np.concatenate([R[c]["g_s"] for c in range(8)]).reshape(1, 32, 1, 512)
    return (y_p, y_s, k_p, v_p, mk_p, mv_p, k_s, v_s, g_s)
```

```python
import contextlib
import math
import numpy as np
import concourse.bass as bass
import concourse.mybir as mybir
from concourse.bass_utils import run_bass_kernel_spmd

F32 = mybir.dt.float32
BF16 = mybir.dt.bfloat16
I32 = mybir.dt.int32
AF = mybir.ActivationFunctionType
ALU = mybir.AluOpType
AX = mybir.AxisListType

D = 1024
SEQ = 4096
TT = 512
NT = SEQ // TT
DFF = 2816
NJ = DFF // 128
NPHYS = 5120
ALPHA = 2.0 ** 0.25
LN_EPS = 1e-5
EPS_EFF = LN_EPS / (ALPHA * ALPHA)
LAMBDA_INIT = 0.8 - 0.6 * math.exp(0.0)
NEG = -30000.0
NS = 4

ENGS = ("pe", "act", "dve", "pool", "sp")
SAME_ENGINE_SYNC = {"pe": False, "act": True, "dve": True, "pool": True, "sp": False}


class Buf:
    __slots__ = ("name", "w", "rs")

    def __init__(self, name=""):
        self.name = name
        self.w = None
        self.rs = {}


class Op:
    __slots__ = ("eng", "idx", "fn", "waits", "dma", "need_inc", "count")

    def __init__(self, eng, idx, fn):
        self.eng = eng
        self.idx = idx
        self.fn = fn
        self.waits = []
        self.dma = None
        self.need_inc = False
        self.count = None


class Prog:
    def __init__(self):
        self.streams = {e: [] for e in ENGS}
        self.dma_cnt = {}
        self.waited = {e: {} for e in ENGS}

    def op(self, eng, fn, reads=(), writes=(), dma_sem=None):
        st = self.streams[eng]
        o = Op(eng, len(st), fn)
        need = []
        for b in reads:
            if b.w is not None:
                need.append(b.w)
        for b in writes:
            if b.w is not None:
                need.append(b.w)
            need.extend(b.rs.values())
        wd = self.waited[eng]
        best = {}
        for t in need:
            if t[0] == "op":
                p = t[1]
                if p.eng == eng and not SAME_ENGINE_SYNC[eng]:
                    continue
                key = "e_" + p.eng
                if wd.get(key, -1) >= p.idx:
                    continue
                if key not in best or best[key][1].idx < p.idx:
                    best[key] = t
            else:
                _, s, v = t
                if wd.get(s, 0) >= v:
                    continue
                if s not in best or best[s][2] < v:
                    best[s] = t
        for key, t in best.items():
            wd[key] = t[1].idx if t[0] == "op" else t[2]
            if t[0] == "op":
                t[1].need_inc = True
        o.waits = list(best.values())
        if dma_sem is not None:
            self.dma_cnt[dma_sem] = self.dma_cnt.get(dma_sem, 0) + 16
            o.dma = (dma_sem, self.dma_cnt[dma_sem])
            tok = ("dma", dma_sem, self.dma_cnt[dma_sem])
            rkey = dma_sem
        else:
            tok = ("op", o)
            rkey = "e_" + eng
        if fn is not None:
            for b in reads:
                b.rs[rkey] = tok
            for b in writes:
                b.w = tok
                b.rs = {}
        st.append(o)
        return tok

    def barrier(self):
        toks = []
        for e in ENGS:
            for p in reversed(self.streams[e]):
                if p.dma is None and p.fn is not None:
                    toks.append(("op", p))
                    break
        for s, v in self.dma_cnt.items():
            toks.append(("dma", s, v))
        for e in ENGS:
            for t in toks:
                if t[0] == "op" and t[1].eng == e:
                    continue
                b2 = Buf()
                b2.w = t
                self.op(e, None, reads=[b2])

    def emit(self, nc, final_wait_eng="sp"):
        for s, v in list(self.dma_cnt.items()):
            b2 = Buf()
            b2.w = ("dma", s, v)
            self.op(final_wait_eng, None, reads=[b2])
        for e in ENGS:
            c = 0
            for o in self.streams[e]:
                if o.need_inc:
                    c += 1
                    o.count = c
        with contextlib.ExitStack() as es:
            sems = {}
            for e in ENGS:
                sems["e_" + e] = es.enter_context(nc.semaphore("e_" + e))
            for s in self.dma_cnt:
                sems[s] = es.enter_context(nc.semaphore(s))
            block = es.enter_context(nc.Block())

            def run(eng_name):
                def body(eng):
                    for o in self.streams[eng_name]:
                        for t in o.waits:
                            if t[0] == "op":
                                eng.wait_ge(sems["e_" + t[1].eng], t[1].count)
                            else:
                                eng.wait_ge(sems[t[1]], t[2])
                        if o.fn is None:
                            continue
                        ins = o.fn(eng)
                        if o.dma is not None:
                            ins.then_inc(sems[o.dma[0]], 16)
                        elif o.need_inc:
                            ins.then_inc(sems["e_" + eng_name], 1)
                return body

            block.tensor(run("pe"))
            block.scalar(run("act"))
            block.vector(run("dve"))
            block.gpsimd(run("pool"))
            block.sync(run("sp"))


def _bucket_np(n):
    n = np.asarray(n, dtype=np.int64)
    nf = np.maximum(n, 1).astype(np.float32)
    large = 16 + (np.log(nf / np.float32(16)) / np.float32(math.log(8.0)) * np.float32(16)).astype(np.int32)
    large = np.minimum(large, 31)
    return np.where(n < 16, n, large)


def _host_consts():
    ident = np.eye(128, dtype=np.float32)
    tril = np.tril(np.ones((128, 128), dtype=np.float32))
    ohm = np.zeros((32, 383), dtype=np.float32)
    for m in range(383):
        n = m - 127
        if n >= 0:
            ohm[int(_bucket_np(n)), m] = 1.0
    maskrow = np.zeros((4, 383), dtype=np.float32)
    maskrow[:, :127] = NEG
    iota = np.arange(128, dtype=np.float32).reshape(128, 1)
    return {"c_ident": ident, "c_tril": tril, "c_ohm": ohm, "c_maskrow": maskrow, "c_iota": iota}


class Ctx:
    pass


def build_nc():
    nc = bass.Bass("TRN2", target_bir_lowering=False)
    P = Prog()

    def din(name, shape, dt=F32):
        return nc.dram_tensor(name, list(shape), dt, kind="ExternalInput").ap()

    def dout(name, shape, dt=F32):
        return nc.dram_tensor(name, list(shape), dt, kind="ExternalOutput").ap()

    def dscr(name, shape, dt=F32):
        return nc.dram_tensor(name, list(shape), dt, kind="Internal").ap()

    xp = din("xp", [SEQ, D])
    mem = din("mem", [256, D])
    rel_bias = din("rel_bias", [32, 4])
    ln_g = din("ln_g", [4, D])
    ln_b = din("ln_b", [4, D])
    w_f1i = din("ffn1_w_in", [D, 2 * DFF])
    w_f1o = din("ffn1_w_out", [DFF, D])
    w_mi = din("w_mix_in", [D, 2560])
    w_mo = din("w_mix_out", [D, D])
    lam_in = din("lam_in", [4, 64])
    subln_g = din("subln_g", [128, 1])
    sgu_ln_g = din("sgu_ln_g", [1, 512])
    sgu_ln_b = din("sgu_ln_b", [1, 512])
    sgu_w = din("sgu_w", [4, 128, 128])
    sgu_b = din("sgu_b", [1, 512])
    w_xq = din("xq_w", [D, D])
    w_xkv = din("xkv_w", [D, 2 * D])
    w_xo = din("xo_w", [D, D])
    w_f2i = din("ffn2_w_in", [D, 2 * DFF])
    w_f2o = din("ffn2_w_out", [DFF, D])
    c_ident = din("c_ident", [128, 128])
    c_tril = din("c_tril", [128, 128])
    c_ohm = din("c_ohm", [32, 383])
    c_maskrow = din("c_maskrow", [4, 383])
    c_iota = din("c_iota", [128, 1])
    xs = din("xs", [NS, D])
    cache_kv = din("cache_kv", [NPHYS * 128, 1024])
    cmk = din("cmk", [NS, 256, D])
    cmv = din("cmv", [NS, 256, D])
    ptab = din("ptab", [NS, 128], I32)

    y_p = dout("y_p", [SEQ, D])
    k_p = dout("k_p", [SEQ, 512])
    v_p = dout("v_p", [SEQ, 512])
    mk_p = dout("mk_p", [256, D])
    mv_p = dout("mv_p", [256, D])
    y_s = dout("y_s", [NS, D])
    k_s = dout("k_s", [NS, 512])
    v_s = dout("v_s", [NS, 512])
    g_s = dout("g_s", [NS, 512])

    NPIECE = 64
    wscr = dscr("wscr", [NPIECE, 128, 4096], BF16)
    dscr_d = dscr("dscr_d", [4, 383])
    dscr_f = dscr("dscr_f", [4, 128 * 383])
    Bwscr = [Buf("wscr%d" % i) for i in range(NPIECE)]

    with contextlib.ExitStack() as es:
        def sb(name, shape, dt):
            return es.enter_context(nc.sbuf_tensor(name, list(shape), dt))

        ident_f = sb("ident_f", [128, 128], F32)
        ident_b = sb("ident_b", [128, 128], BF16)
        ones_b = sb("ones_b", [128, 128], BF16)
        ones_f = sb("ones_f", [128, 128], F32)
        lng = sb("lng", [128, 4, 8], F32)
        lnb = sb("lnb", [128, 4, 8], F32)
        BT = sb("BT", [128, 4, 256], BF16)
        BTf = sb("BTf", [128, 4, 256], F32)
        far = sb("far", [128, 4], F32)
        neglam = sb("neglam", [128, 1], F32)
        sublg = sb("sublg", [128, 1], F32)
        sgug = sb("sgug", [128, 512], F32)
        sgub_ln = sb("sgub_ln", [128, 512], F32)
        sgub_row = sb("sgub_row", [1, 512], BF16)
        sgub_rowf = sb("sgub_rowf", [1, 512], F32)
        trilWT = sb("trilWT", [128, 4, 128], BF16)
        memKT = sb("memKT", [128, 8, 256], BF16)
        memV = sb("memV", [128, 2, 1024], BF16)
        KT2 = sb("KT", [128, 32 * 512], BF16)
        Vt2 = sb("Vt", [128, 32 * 512], BF16)
        KT = KT2[:, :].rearrange("p (t h k) -> p t h k", t=32, h=4)
        Vt = Vt2[:, :].rearrange("p (t f) -> p t f", t=32)
        BKT = [Buf("KT%d" % i) for i in range(32)]
        BVt = [Buf("Vt%d" % i) for i in range(32)]
        x = sb("x", [128, 8, TT], F32)
        xb = sb("xb", [128, 8, TT], BF16)
        hbuf2 = sb("hbuf", [128, NJ * TT], BF16)
        hbuf = hbuf2[:, :].rearrange("p (k t) -> p k t", k=NJ)
        sq = hbuf2[:, 0:8 * TT].rearrange("p (k t) -> p k t", k=8)
        Bx = [Buf("x%d" % i) for i in range(8)]
        Bxb = [Buf("xb%d" % i) for i in range(8)]
        Bh = [Buf("h%d" % i) for i in range(NJ)]
        Bsq = Bh[0:8]
        NSLOT = 3
        wslot = [sb("wslot%d" % i, [128, 4096], BF16) for i in range(NSLOT)]
        Bws = [Buf("ws%d" % i) for i in range(NSLOT)]
        wstage = [KT2[:, 8192:16384].bitcast(F32), Vt2[:, 8192:16384].bitcast(F32)]
        Bwst = [Buf("wst%d" % i) for i in range(2)]
        tmpf = [sb("tmpf%d" % i, [128, TT], F32) for i in range(4)]
        Btmp = [Buf("tmpf%d" % i) for i in range(4)]
        stg = [sb("stg%d" % i, [128, 1024], F32) for i in range(3)]
        Bstg = [Buf("stg%d" % i) for i in range(3)]
        qT = hbuf2[:, 16 * TT:20 * TT].rearrange("p (k t) -> p k t", k=4)
        BqT = Bh[16:20]
        uT = hbuf2[:, 8 * TT:16 * TT].bitcast(F32).rearrange("p (k t) -> p k t", k=4)
        BuT2 = [[Bh[8 + 2 * g], Bh[9 + 2 * g]] for g in range(4)]
        BuT = Bh[8:16]
        vn = sb("vn", [128, 4, 512], BF16)
        Bvn = [Buf("vn%d" % i) for i in range(4)]
        catT = sq
        Bcat = Bh[0:8]
        ET = [sb("ET%d" % i, [128, TT], BF16) for i in range(4)]
        BET = [Buf("ET%d" % i) for i in range(4)]
        oacc = sb("oacc", [128, TT], F32)
        Boacc = Buf("oacc")
        oacc2 = sb("oacc2", [128, TT], F32)
        Boacc2 = Buf("oacc2")
        col = [sb("col%d" % i, [128, 8], F32) for i in range(4)]
        Bcol = [Buf("col%d" % i) for i in range(4)]
        psum = [es.enter_context(nc.psum_tensor("ps%d" % i, [128, 512], F32)) for i in range(8)]
        Bps = [Buf("ps%d" % i) for i in range(8)]

        rr = {"mm": 0, "sc": 0, "pv": 0, "tmp": 0, "stg": 0, "et": 0, "ws": 0, "wst": 0, "col": 0}
        POOLS = {"mm": [0, 1, 2, 3, 4, 5], "sc": [2, 3], "pv": [4, 5, 6, 7]}

        def set_dense():
            POOLS["mm"] = [0, 1, 2, 3, 4, 5]

        def set_attn():
            POOLS["mm"] = [0, 1]

        def bank(pool):
            lst = POOLS[pool]
            i = lst[rr[pool] % len(lst)]
            rr[pool] += 1
            return psum[i], Bps[i]

        def rot(key, arrs, bufs):
            i = rr[key] % len(arrs)
            rr[key] += 1
            return arrs[i], bufs[i]

        def mm(out, lhsT, rhs, start, stop, reads, wbuf, skip=False):
            if skip:
                P.op("pe", lambda e: e.matmul(out, lhsT=lhsT, rhs=rhs, start=start, stop=stop, skip_group_check=True),
                     reads=reads, writes=[wbuf])
            else:
                P.op("pe", lambda e: e.matmul(out, lhsT=lhsT, rhs=rhs, start=start, stop=stop),
                     reads=reads, writes=[wbuf])

        def tr(out, in_, reads, wbuf):
            P.op("pe", lambda e: e.transpose(out, in_, ident_f[:]), reads=reads + [Bconst], writes=[wbuf])

        def act(out, in_, func, reads, writes, bias=None, scale=1.0):
            if bias is None:
                P.op("act", lambda e: e.activation(out=out, in_=in_, func=func, scale=scale),
                     reads=reads, writes=writes)
            else:
                P.op("act", lambda e: e.activation(out=out, in_=in_, func=func, bias=bias, scale=scale),
                     reads=reads, writes=writes)

        def tt(eng, out, in0, in1, op, reads, writes):
            P.op(eng, lambda e: e.tensor_tensor(out=out, in0=in0, in1=in1, op=op), reads=reads, writes=writes)

        def ts(eng, out, in0, s1, s2, op0, op1, reads, writes):
            if s2 is None:
                P.op(eng, lambda e: e.tensor_scalar(out=out, in0=in0, scalar1=s1, scalar2=None, op0=op0),
                     reads=reads, writes=writes)
            else:
                P.op(eng, lambda e: e.tensor_scalar(out=out, in0=in0, scalar1=s1, scalar2=s2, op0=op0, op1=op1),
                     reads=reads, writes=writes)

        def stt(eng, out, in0, scalar, in1, op0, op1, reads, writes):
            P.op(eng, lambda e: e.scalar_tensor_tensor(out=out, in0=in0, scalar=scalar, in1=in1, op0=op0, op1=op1),
                 reads=reads, writes=writes)

        def cp(eng, out, in_, reads, writes):
            if eng == "act":
                P.op("act", lambda e: e.copy(out=out, in_=in_), reads=reads, writes=writes)
            else:
                P.op(eng, lambda e: e.tensor_copy(out=out, in_=in_), reads=reads, writes=writes)

        def dma(eng, out, in_, reads, writes, sem, nonc=False):
            if nonc:
                P.op(eng, lambda e: e.dma_start(out=out, in_=in_, allow_slow_non_contiguous=True),
                     reads=reads, writes=writes, dma_sem=sem)
            else:
                P.op(eng, lambda e: e.dma_start(out=out, in_=in_), reads=reads, writes=writes, dma_sem=sem)

        Bconst = Buf("const")

        dma("sp", ident_f[:], c_ident, [], [Bconst], "d_c1")
        cp("dve", ident_b[:], ident_f[:], [Bconst], [Bconst])
        P.op("pool", lambda e: e.memset(ones_b[:], 1.0), writes=[Bconst])
        P.op("pool", lambda e: e.memset(ones_f[:], 1.0), writes=[Bconst])
        dma("sp", lng[:], ln_g.rearrange("i (k p) -> p i k", p=128), [], [Bconst], "d_c2", nonc=True)
        dma("sp", lnb[:], ln_b.rearrange("i (k p) -> p i k", p=128), [], [Bconst], "d_c3", nonc=True)
        dma("sp", sublg[:], subln_g, [], [Bconst], "d_c4", nonc=True)
        ts("dve", sublg[:], sublg[:], 1.0 - LAMBDA_INIT, None, ALU.mult, None, [Bconst], [Bconst])
        dma("sp", sgug[:], sgu_ln_g.to_broadcast([128, 512]), [], [Bconst], "d_c5", nonc=True)
        dma("sp", sgub_ln[:], sgu_ln_b.to_broadcast([128, 512]), [], [Bconst], "d_c6", nonc=True)
        dma("sp", sgub_rowf[:], sgu_b, [], [Bconst], "d_c7")
        cp("dve", sgub_row[:], sgub_rowf[:], [Bconst], [Bconst])
        lam_t = sb("lam_t", [128, 4, 64], F32)
        lam_s = sb("lam_s", [128, 4], F32)
        dma("sp", lam_t[:], bass.AP(lam_in.tensor, 0, [[0, 128], [64, 4], [1, 64]]), [], [Bconst], "d_c8", nonc=True)
        tt("dve", lam_t[:, 0, :], lam_t[:, 0, :], lam_t[:, 1, :], ALU.mult, [Bconst], [Bconst])
        tt("dve", lam_t[:, 2, :], lam_t[:, 2, :], lam_t[:, 3, :], ALU.mult, [Bconst], [Bconst])
        P.op("dve", lambda e: e.tensor_reduce(out=lam_s[:, 0:1], in_=lam_t[:, 0, :], axis=AX.X, op=ALU.add),
             reads=[Bconst], writes=[Bconst])
        P.op("dve", lambda e: e.tensor_reduce(out=lam_s[:, 1:2], in_=lam_t[:, 2, :], axis=AX.X, op=ALU.add),
             reads=[Bconst], writes=[Bconst])
        act(lam_s[:, 2:4], lam_s[:, 0:2], AF.Exp, [Bconst], [Bconst])
        tt("dve", neglam[:], lam_s[:, 3:4], lam_s[:, 2:3], ALU.subtract, [Bconst], [Bconst])
        ts("dve", neglam[:], neglam[:], -LAMBDA_INIT, None, ALU.add, None, [Bconst], [Bconst])

        if STOP == 1:
            P.emit(nc); return nc
        tab = sb("tab", [32, 4], F32)
        ohm = sb("ohm", [32, 383], F32)
        mrow = sb("mrow", [4, 383], F32)
        dvec = sb("dvec", [4, 383], F32)
        dma("sp", tab[:], rel_bias, [], [Bconst], "d_c9")
        dma("sp", ohm[:], c_ohm, [], [Bconst], "d_c10")
        dma("sp", mrow[:], c_maskrow, [], [Bconst], "d_c11")
        pb, pbuf = bank("mm")
        mm(pb[0:4, 0:383], tab[:], ohm[:], True, True, [Bconst], pbuf)
        cp("dve", dvec[:], pb[0:4, 0:383], [pbuf], [Bconst])
        ts("dve", dvec[:], dvec[:], dvec[:, 382:383], None, ALU.subtract, None, [Bconst], [Bconst])
        tt("dve", dvec[:], dvec[:], mrow[:], ALU.add, [Bconst], [Bconst])
        Bdd = Buf("dscr_d")
        Bdf = Buf("dscr_f")
        dma("sp", dscr_d, dvec[:], [Bconst], [Bdd], "d_c12")
        for h in range(4):
            dma("sp", bass.AP(dscr_f.tensor, h * 128 * 383, [[383, 128], [1, 383]]),
                bass.AP(dscr_d.tensor, h * 383, [[0, 128], [1, 383]]), [Bdd], [Bdf], "d_c13", nonc=True)
        for h in range(4):
            dma("sp", BTf[:, h, :], bass.AP(dscr_f.tensor, h * 128 * 383 + 127, [[382, 128], [1, 256]]),
                [Bdf], [Bconst], "d_c14", nonc=True)
        cp("dve", BT[:], BTf[:], [Bconst], [Bconst])
        if STOP == 2:
            P.emit(nc); return nc
        dma("sp", far[:], bass.AP(rel_bias.tensor, 31 * 4, [[0, 128], [1, 4]]), [], [Bconst], "d_c15", nonc=True)
        wtmp = sb("wtmp", [128, 4, 128], F32)
        trm = sb("trm", [128, 128], F32)
        dma("sp", wtmp[:], sgu_w.rearrange("g t s -> t g s"), [], [Bconst], "d_c16", nonc=True)
        dma("sp", trm[:], c_tril, [], [Bconst], "d_c17")
        for g in range(4):
            tt("dve", wtmp[:, g, :], wtmp[:, g, :], trm[:], ALU.mult, [Bconst], [Bconst])
            pb, pbuf = bank("mm")
            tr(pb[:, 0:128], wtmp[:, g, :], [Bconst], pbuf)
            cp("dve", trilWT[:, g, :], pb[:, 0:128], [pbuf], [Bconst])

        if STOP == 3:
            P.emit(nc); return nc
        piece_id = {}

        def get_piece(key, blocks, first):
            if key not in piece_id:
                piece_id[key] = len(piece_id)
            pid = piece_id[key]
            slot, bslot = rot("ws", wslot, Bws)
            tot = sum(nk * ncols for (_, _, nk, _, ncols, _) in blocks)
            if first:
                st, bst = rot("wst", wstage, Bwst)
                for (W, r0, nk, c0, ncols, off) in blocks:
                    src = W[r0:r0 + nk * 128, c0:c0 + ncols].rearrange("(k p) c -> p k c", p=128)
                    dst = st[:, off:off + nk * ncols].rearrange("p (k c) -> p k c", c=ncols)
                    dma("sp", dst, src, [], [bst], "d_wst%d" % (rr["wst"] % 2))
                ceng = ("act", "dve", "pool")[pid % 3]
                cp(ceng, slot[:, 0:tot], st[:, 0:tot], [bst], [bslot])
                dma("sp", wscr[pid][:, 0:tot], slot[:, 0:tot], [bslot], [Bwscr[pid]], "d_wsc%d" % (rr["ws"] % NSLOT))
            else:
                dma("sp", slot[:, 0:tot], wscr[pid][:, 0:tot], [Bwscr[pid]], [bslot], "d_ws%d" % (rr["ws"] % NSLOT))
            return slot, bslot

        ln_state = {}

        def ln_begin():
            ln_state["pend"] = None
            ln_state["cnt"] = 0

        def ln_stats(k, N, last):
            sqs, bsq_ = ln_state["sq%d" % k]
            mm(psum[6][:, 0:N], ones_b[:], xb[:, k, 0:N], k == 0, last, [Bconst, Bxb[k]], Bps[6])
            mm(psum[7][:, 0:N], ones_b[:], sqs[:, 0:N], k == 0, last, [Bconst, bsq_], Bps[7])

        def ln_pre(k, N):
            cp("pool", xb[:, k, 0:N], x[:, k, 0:N], [Bx[k]], [Bxb[k]])
            sqs, bsq_ = rot("et", ET, BET)
            act(sqs[:, 0:N], x[:, k, 0:N], AF.Square, [Bx[k]], [bsq_])
            ln_state["sq%d" % k] = (sqs, bsq_)
            if k >= 1:
                ln_stats(k - 1, N, False)

        def layer_norm(N, li):
            ln_stats(7, N, True)
            s1, b1 = psum[6], Bps[6]
            s2, b2 = psum[7], Bps[7]
            mean, bm = rot("tmp", tmpf, Btmp)
            msq, bq = rot("tmp", tmpf, Btmp)
            rstd, br = rot("tmp", tmpf, Btmp)
            ts("dve", mean[:, 0:N], s1[:, 0:N], 1.0 / D, None, ALU.mult, None, [b1], [bm])
            tt("dve", msq[:, 0:N], mean[:, 0:N], mean[:, 0:N], ALU.mult, [bm], [bq])
            stt("dve", rstd[:, 0:N], s2[:, 0:N], 1.0 / D, msq[:, 0:N], ALU.mult, ALU.subtract, [b2, bq], [br])
            ts("dve", rstd[:, 0:N], rstd[:, 0:N], EPS_EFF, None, ALU.add, None, [br], [br])
            act(rstd[:, 0:N], rstd[:, 0:N], AF.Sqrt, [br], [br])
            P.op("dve", lambda e: e.reciprocal(out=rstd[:, 0:N], in_=rstd[:, 0:N]), reads=[br], writes=[br])
            for k in range(8):
                eng = "dve"
                tt(eng, x[:, k, 0:N], x[:, k, 0:N], mean[:, 0:N], ALU.subtract, [Bx[k], bm], [Bx[k]])
                tt(eng, x[:, k, 0:N], x[:, k, 0:N], rstd[:, 0:N], ALU.mult, [Bx[k], br], [Bx[k]])
                act(xb[:, k, 0:N], x[:, k, 0:N], AF.Identity, [Bx[k], Bconst], [Bxb[k]],
                    bias=lnb[:, li, k:k + 1], scale=lng[:, li, k:k + 1])
                ts("pool", x[:, k, 0:N], x[:, k, 0:N], lng[:, li, k:k + 1], lnb[:, li, k:k + 1], ALU.mult, ALU.add,
                   [Bx[k], Bconst], [Bx[k]])

        def ffn(N, Win, Wout, li, first, tag):
            for pi in range(11):
                slot, bs = get_piece((tag, "in", pi),
                                     [(Win, 0, 8, 256 * pi, 256, 0), (Win, 0, 8, DFF + 256 * pi, 256, 2048)], first)
                sv = slot[:, :].rearrange("p (a k c) -> p a k c", a=2, k=8)
                for jj in range(2):
                    j = 2 * pi + jj
                    A, bA = bank("mm")
                    for k in range(8):
                        mm(A[:, 0:N], sv[:, 0, k, jj * 128:(jj + 1) * 128], xb[:, k, 0:N], k == 0, k == 7,
                           [bs, Bxb[k]], bA)
                    Bm, bB = bank("mm")
                    for k in range(8):
                        mm(Bm[:, 0:N], sv[:, 1, k, jj * 128:(jj + 1) * 128], xb[:, k, 0:N], k == 0, k == 7,
                           [bs, Bxb[k]], bB)
                    t, bt = rot("tmp", tmpf, Btmp)
                    act(t[:, 0:N], A[:, 0:N], AF.Silu, [bA], [bt])
                    tt("dve", hbuf[:, j, 0:N], t[:, 0:N], Bm[:, 0:N], ALU.mult, [bt, bB], [Bh[j]])
            for c in range(8):
                slot, bs = get_piece((tag, "out", c), [(Wout, 0, NJ, 128 * c, 128, 0)], first)
                sv = slot[:, 0:NJ * 128].rearrange("p (k c) -> p k c", k=NJ)
                Y, bY = bank("mm")
                for k in range(NJ):
                    mm(Y[:, 0:N], sv[:, k, :], hbuf[:, k, 0:N], k == 0, k == NJ - 1, [bs, Bh[k]], bY)
                stt("dve", x[:, c, 0:N], Y[:, 0:N], 0.5 / ALPHA, x[:, c, 0:N], ALU.mult, ALU.add,
                    [bY, Bx[c]], [Bx[c]])
                ln_pre(c, N)
            layer_norm(N, li)

        def proj_res(N, W, src, Bsrc, tag, first):
            for pi in range(2):
                slot, bs = get_piece((tag, pi), [(W, 0, 8, 512 * pi, 512, 0)], first)
                sv = slot[:, :].rearrange("p (k c) -> p k c", k=8)
                for cc in range(4):
                    c = 4 * pi + cc
                    Y, bY = bank("mm")
                    for k in range(8):
                        mm(Y[:, 0:N], sv[:, k, cc * 128:(cc + 1) * 128], src[:, k, 0:N], k == 0, k == 7,
                           [bs, Bsrc[k]], bY)
                    stt("dve", x[:, c, 0:N], Y[:, 0:N], 1.0 / ALPHA, x[:, c, 0:N], ALU.mult, ALU.add,
                        [bY, Bx[c]], [Bx[c]])
                    ln_pre(c, N)

        def mem_phase():
            for s in range(2):
                st_, bst_ = rot("stg", stg, Bstg)
                dma("sp", st_[:, :], mem[s * 128:(s + 1) * 128, :], [], [bst_], "d_stg%d" % (rr["stg"] % 3))
                for kk in range(2):
                    pb, pbuf = bank("mm")
                    for q4 in range(4):
                        k = kk * 4 + q4
                        tr(pb[:, q4 * 128:(q4 + 1) * 128], st_[:, k * 128:(k + 1) * 128], [bst_], pbuf)
                    for q4 in range(4):
                        k = kk * 4 + q4
                        cp("dve" if kk else "act", hbuf[:, k, s * 128:(s + 1) * 128],
                           pb[:, q4 * 128:(q4 + 1) * 128], [pbuf], [Bh[k]])
            if STOP == 5:
                return
            for pi in range(4 if STOP != 6 else 1):
                slot, bs = rot("ws", wslot, Bws)
                src = w_xkv[:, 512 * pi:512 * (pi + 1)].rearrange("(k p) c -> p k c", p=128)
                P.op("pool", lambda e, slot=slot, src=src: e.dma_start(
                    out=slot[:, :].rearrange("p (k c) -> p k c", k=8), in_=src),
                    writes=[bs], dma_sem="d_ws%d" % (rr["ws"] % NSLOT))
                sv = slot[:, :].rearrange("p (k c) -> p k c", k=8)
                if pi < 2:
                    for cc in range(4):
                        j = 4 * pi + cc
                        pb, pbuf = bank("mm")
                        for k in range(8):
                            mm(pb[:, 0:256], sv[:, k, cc * 128:(cc + 1) * 128], hbuf[:, k, 0:256], k == 0, k == 7,
                               [bs, Bh[k]], pbuf)
                        cp("act", memKT[:, j, :], pb[:, 0:256], [pbuf], [Bconst])
                for s in range(2):
                    pb, pbuf = bank("mm")
                    for k in range(8):
                        mm(pb[:, :], hbuf[:, k, s * 128:(s + 1) * 128], sv[:, k, :], k == 0, k == 7,
                           [bs, Bh[k]], pbuf)
                    st_, bst_ = rot("stg", stg, Bstg)
                    cp("dve", st_[:, 0:512], pb[:, :], [pbuf], [bst_])
                    if pi >= 2:
                        cp("pool", memV[:, s, 512 * (pi - 2):512 * (pi - 1)], st_[:, 0:512], [bst_], [Bconst])
                        dma("sp", mv_p[s * 128:(s + 1) * 128, 512 * (pi - 2):512 * (pi - 1)], st_[:, 0:512],
                            [bst_], [], "d_stg%d" % (rr["stg"] % 3))
                    else:
                        dma("sp", mk_p[s * 128:(s + 1) * 128, 512 * pi:512 * (pi + 1)], st_[:, 0:512],
                            [bst_], [], "d_stg%d" % (rr["stg"] % 3))

        def cross_core(c0, c1):
            for hm in range(4):
                ets = []
                for mc in range(2):
                    S, bS = bank("sc")
                    for dc in range(2):
                        mm(S[:, c0:c1], memKT[:, hm * 2 + dc, mc * 128:(mc + 1) * 128], hbuf[:, hm * 2 + dc, c0:c1],
                           dc == 0, dc == 1, [Bconst, Bh[hm * 2 + dc]], bS)
                    et, bet = rot("et", ET, BET)
                    act(et[:, c0:c1], S[:, c0:c1], AF.Exp, [bS], [bet])
                    ets.append((et, bet))
                SM, bSM = bank("pv")
                for mc in range(2):
                    mm(SM[:, c0:c1], ones_b[:], ets[mc][0][:, c0:c1], mc == 0, mc == 1, [Bconst, ets[mc][1]], bSM)
                rc, brc = rot("tmp", tmpf, Btmp)
                P.op("dve", lambda e, rc=rc, SM=SM: e.reciprocal(out=rc[:, c0:c1], in_=SM[:, c0:c1]), reads=[bSM], writes=[brc])
                for dc in range(2):
                    O, bO = bank("pv")
                    for mc in range(2):
                        mm(O[:, c0:c1], memV[:, mc, hm * 256 + dc * 128:hm * 256 + (dc + 1) * 128], ets[mc][0][:, c0:c1],
                           mc == 0, mc == 1, [Bconst, ets[mc][1]], bO)
                    tt("dve", hbuf[:, 8 + hm * 2 + dc, c0:c1], O[:, c0:c1], rc[:, c0:c1], ALU.mult, [bO, brc], [Bh[8 + hm * 2 + dc]])


        srow = sb("srow", [4, 3, 512], F32)
        Bsrow = [Buf("srow%d" % i) for i in range(3)]
        vrow_b = sb("vrow_b", [128, 512], BF16)
        vnrow_b = sb("vnrow_b", [4, 512], BF16)
        dW = sb("dW", [4, 4, 4], BF16)
        w00 = sb("w00", [4, 4], F32)
        dnew = sb("dnew", [128, 4], F32)
        MBall = sb("MBall", [128, 4, 8], BF16)
        MBf = sb("MBf", [128, 4, 8], F32)
        t1c = sb("t1c", [128, 1], F32)
        BSl = sb("BSl", [128, 4, 2], BF16)
        iota_f = sb("iota_f", [128, 1], F32)
        ptb = sb("ptb", [128, 128], I32)
        idx_all = sb("idx_all", [128, 128], I32)
        Bidx = Buf("idx")
        Bsm = Buf("smisc")

        def sample_phase():
            N = NS
            first = True
            dma("sp", iota_f[:], c_iota, [], [Bsm], "d_s1")
            dma("sp", w00[:].unsqueeze(2), bass.AP(sgu_w.tensor, 0, [[0, 4], [128 * 128, 4], [1, 1]]), [], [Bsm], "d_s2", nonc=True)
            dma("sp", dnew[:].unsqueeze(2), bass.AP(dscr_d.tensor, 127, [[0, 128], [383, 4], [1, 1]]), [Bdd], [Bsm], "d_s3", nonc=True)
            for g in range(4):
                ts("dve", dW[:, g, :], ident_f[0:4, 0:4], w00[:, g:g + 1], None, ALU.mult, None, [Bsm, Bconst], [Bsm])
            for b in range(4):
                ts("dve", t1c[:, 0:1], ident_f[:, b:b + 1], -NEG, NEG, ALU.mult, ALU.add, [Bconst, Bsm], [Bsm])
                for c in range(2):
                    ts("dve", MBf[:, b, :].rearrange("p (h c) -> p h c", c=2)[:, :, c], dnew[:, :], ident_f[:, b:b + 1],
                       t1c[:, 0:1], ALU.mult, ALU.add, [Bsm, Bconst], [Bsm])
            cp("dve", MBall[:, :, :], MBf[:, :, :], [Bsm], [Bsm])
            for c in range(2):
                cp("dve", BSl[:, :, c:c + 1], BT[:, :, 128:129], [Bconst], [Bsm])
            if STOP == 10:
                return
            st_, bst_ = rot("stg", stg, Bstg)
            dma("sp", st_[0:4, :], xs, [], [bst_], "d_stg%d" % (rr["stg"] % 3))
            pb, pbuf = bank("mm")
            for k in range(8):
                P.op("pe", lambda e, pb=pb, st_=st_, k=k: e.transpose(pb[:, k * 4:(k + 1) * 4], st_[0:4, k * 128:(k + 1) * 128],
                                                                  ident_f[0:4, 0:4]), reads=[bst_, Bconst], writes=[pbuf])
            P.op("dve", lambda e, pb=pb: e.tensor_copy(out=x[:, :, 0:4], in_=pb[:, 0:32].rearrange("p (k b) -> p k b", k=8)),
                 reads=[pbuf], writes=Bx)
            P.op("pool", lambda e: e.tensor_copy(out=xb[:, :, 0:4], in_=x[:, :, 0:4]), reads=Bx, writes=Bxb)
            ffn(N, w_f1i, w_f1o, 0, first, "f1")
            if STOP == 11:
                return
            slot, bs = get_piece(("mi", 0), [(w_mi, 0, 8, 0, 512, 0)], first)
            sv = slot[:, :].rearrange("p (k c) -> p k c", k=8)
            for h in range(4):
                pb, pbuf = bank("mm")
                for k in range(8):
                    mm(pb[:, 0:N], sv[:, k, h * 128:(h + 1) * 128], xb[:, k, 0:N], k == 0, k == 7, [bs, Bxb[k]], pbuf)
                act(qT[:, h, 0:N], pb[:, 0:N], AF.Copy, [pbuf], [BqT[h]], scale=0.125)
            slot, bs = get_piece(("mi", 1), [(w_mi, 0, 8, 512, 512, 0)], first)
            sv = slot[:, :].rearrange("p (k c) -> p k c", k=8)
            P.op("pool", lambda e: e.memset(KT[:, 15, :, :].rearrange("p h k -> p (h k)"), 0.0), writes=[BKT[15]])
            P.op("pool", lambda e: e.memset(vrow_b[:, :], 0.0), writes=[Bsm])
            for h in range(4):
                pb, pbuf = bank("mm")
                for k in range(8):
                    mm(pb[:, 0:N], sv[:, k, h * 128:(h + 1) * 128], xb[:, k, 0:N], k == 0, k == 7, [bs, Bxb[k]], pbuf)
                cp("dve", KT[:, 15, h, 0:4], pb[:, 0:N], [pbuf], [BKT[15]])
            pb, pbuf = bank("mm")
            for k in range(8):
                mm(pb[0:4, :], xb[:, k, 0:4], sv[:, k, :], k == 0, k == 7, [bs, Bxb[k]], pbuf)
            cp("act", srow[:, 0, :], pb[0:4, :], [pbuf], [Bsrow[0], Bsm])
            dma("sp", k_s, srow[:, 0, :], [Bsrow[0]], [], "d_s4")
            slot, bs = get_piece(("mi", 2), [(w_mi, 0, 8, 1024, 512, 0)], first)
            sv = slot[:, :].rearrange("p (k c) -> p k c", k=8)
            pb, pbuf = bank("mm")
            for k in range(8):
                mm(pb[0:4, :], xb[:, k, 0:4], sv[:, k, :], k == 0, k == 7, [bs, Bxb[k]], pbuf)
            cp("act", srow[:, 1, :], pb[0:4, :], [pbuf], [Bsrow[1]])
            cp("pool", vrow_b[0:4, :], srow[:, 1, :], [Bsrow[1]], [Bsm])
            dma("sp", v_s, srow[:, 1, :], [Bsrow[1]], [], "d_s5")
            slot, bs = get_piece(("mi", 3), [(w_mi, 0, 8, 1536, 512, 0)], first)
            sv = slot[:, :].rearrange("p (k c) -> p k c", k=8)
            for g in range(4):
                pb, pbuf = bank("mm")
                for k in range(8):
                    mm(pb[:, 0:N], sv[:, k, g * 128:(g + 1) * 128], xb[:, k, 0:N], k == 0, k == 7, [bs, Bxb[k]], pbuf)
                act(uT[:, g, 0:N], pb[:, 0:N], AF.Gelu, [pbuf], BuT2[g])
            slot, bs = get_piece(("mi", 4), [(w_mi, 0, 8, 2048, 512, 0)], first)
            sv = slot[:, :].rearrange("p (k c) -> p k c", k=8)
            pb, pbuf = bank("mm")
            for k in range(8):
                mm(pb[0:4, :], xb[:, k, 0:4], sv[:, k, :], k == 0, k == 7, [bs, Bxb[k]], pbuf)
            gv = srow[:, 2, :]
            bgv = Bsrow[2]
            cl, bcl = rot("col", col, Bcol)
            act(gv, pb[0:4, :], AF.Gelu, [pbuf], [bgv])
            P.op("dve", lambda e, cl=cl: e.tensor_reduce(out=cl[0:4, 0:1], in_=srow[:, 2, :], axis=AX.X, op=ALU.add),
                 reads=[bgv], writes=[bcl])
            ts("dve", cl[0:4, 1:2], cl[0:4, 0:1], -1.0 / 512, None, ALU.mult, None, [bcl], [bcl])
            ts("dve", gv, gv, cl[0:4, 1:2], None, ALU.add, None, [bgv, bcl], [bgv])
            junk, bj = rot("tmp", tmpf, Btmp)
            tt("pool", junk[0:4, :], gv, gv, ALU.mult, [bgv], [bj])
            P.op("dve", lambda e, junk=junk, cl=cl: e.tensor_reduce(out=cl[0:4, 2:3], in_=junk[0:4, :], axis=AX.X, op=ALU.add),
                 reads=[bj], writes=[bcl])
            ts("dve", cl[0:4, 3:4], cl[0:4, 2:3], 1.0 / 512, LN_EPS, ALU.mult, ALU.add, [bcl], [bcl])
            act(cl[0:4, 3:4], cl[0:4, 3:4], AF.Sqrt, [bcl], [bcl])
            P.op("dve", lambda e, cl=cl: e.reciprocal(out=cl[0:4, 3:4], in_=cl[0:4, 3:4]), reads=[bcl], writes=[bcl])
            stt("dve", gv, gv, cl[0:4, 3:4], sgug[0:4, :], ALU.mult, ALU.mult, [bgv, bcl, Bconst], [bgv])
            tt("dve", gv, gv, sgub_ln[0:4, :], ALU.add, [bgv, Bconst], [bgv])
            cp("pool", vnrow_b[:, :], gv, [bgv], [Bsm])
            dma("sp", g_s, gv, [bgv], [], "d_s6")
            pb, pbuf = bank("mm")
            for g in range(4):
                mm(pb[:, g * 4:(g + 1) * 4], vnrow_b[0:4, g * 128:(g + 1) * 128], dW[0:4, g, :], True, False, [Bsm], pbuf)
                mm(pb[:, g * 4:(g + 1) * 4], ones_b[0:1, :], sgub_row[0:1, g * 128:g * 128 + 1].to_broadcast([1, 4]),
                   False, True, [Bconst], pbuf)
            P.op("dve", lambda e, pb=pb: e.tensor_tensor(out=catT[:, 4:8, 0:4], in0=pb[:, 0:16].rearrange("p (g t) -> p g t", g=4),
                                                       in1=uT[:, :, 0:4], op=ALU.mult), reads=[pbuf] + BuT, writes=Bcat[4:8])
            if STOP == 12:
                return
            set_attn()
            for b in range(4):
                dma("sp", ptb[:, :], bass.AP(ptab.tensor, b * 128, [[0, 128], [1, 128]]), [], [Bidx], "d_s7", nonc=True)
                ts("dve", idx_all[:, :], ptb[:, :], 128.0, iota_f[:, 0:1], ALU.mult, ALU.add, [Bidx, Bsm], [Bidx])
                O, bO = bank("pv")
                SM, bSM = bank("pv")
                jl = list(range(129))
                if STOP in (14, 15, 16):
                    jl = list(range(8))
                if STOP == 17:
                    jl = list(range(8)) + [127, 128]
                if STOP == 18:
                    jl = list(range(8)) + [127]
                if STOP == 19:
                    jl = list(range(8)) + [128]
                jlast = jl[-1]
                for j in jl:
                    sl = j % 8
                    if j < 128:
                        kst, bkst = rot("stg", stg, Bstg)
                        semn = "d_stg%d" % (rr["stg"] % 3)
                        P.op("pool", lambda e, kst=kst, j=j: e.indirect_dma_start(
                            out=kst[:, :], out_offset=None, in_=cache_kv,
                            in_offset=bass.IndirectOffsetOnAxis(ap=idx_all[:, j:j + 1], axis=0)),
                            reads=[Bidx], writes=[bkst], dma_sem=semn)
                        cp("dve" if j % 2 else "act", Vt[:, sl, :], kst[:, 512:1024], [bkst], [BVt[sl]])
                        pbk, pbkb = bank("mm")
                        for h in range(4):
                            tr(pbk[:, h * 128:(h + 1) * 128], kst[:, h * 128:(h + 1) * 128], [bkst], pbkb)
                        cp("act" if j % 2 else "dve", KT[:, sl, :, :].rearrange("p h k -> p (h k)"), pbk[:, :], [pbkb], [BKT[sl]])
                        ktile, bkt, nkeys, vt_l = sl, BKT[sl], 128, None
                        if STOP in (14, 15):
                            continue
                    else:
                        ktile, bkt, nkeys = 15, BKT[15], 128
                    S, bS = bank("sc")
                    for h in range(4):
                        for c in range(2):
                            hc = h * 2 + c
                            special = (j >= 127)
                            mm(S[0:nkeys, hc:hc + 1], KT[c * 64:(c + 1) * 64, ktile, h, 0:nkeys],
                               qT[c * 64:(c + 1) * 64, h, b:b + 1], hc == 0, (hc == 7) and not special,
                               [bkt, BqT[h]], bS, skip=True)
                    if j == 127:
                        mm(S[:, 0:8], ident_b[:], BSl[:, :, :].rearrange("p h c -> p (h c)"), False, True, [Bconst, Bsm], bS, skip=True)
                    if j == 128:
                        mm(S[:, 0:8], ident_b[:], MBall[:, b, :], False, True, [Bconst, Bsm], bS, skip=True)
                    et, bet = rot("et", ET, BET)
                    act(et[0:nkeys, 0:8], S[0:nkeys, 0:8], AF.Exp, [bS], [bet])
                    for h in range(4):
                        if j < 128:
                            lhs = Vt[:, sl, h * 128:(h + 1) * 128]
                            rd = [BVt[sl], bet]
                        else:
                            lhs = vrow_b[:, h * 128:(h + 1) * 128]
                            rd = [Bsm, bet]
                        mm(O[:, 2 * h:2 * h + 2], lhs, et[0:nkeys, 2 * h:2 * h + 2], (j == 0 and h == 0), (j == jlast and h == 3),
                           rd, bO, skip=True)
                    mm(SM[:, 0:8], ones_b[0:nkeys, :], et[0:nkeys, 0:8], j == 0, j == jlast, [Bconst, bet], bSM, skip=True)
                if STOP in (14, 15):
                    continue
                rc, brc = rot("tmp", tmpf, Btmp)
                P.op("dve", lambda e, rc=rc, SM=SM: e.reciprocal(out=rc[:, 0:8], in_=SM[:, 0:8]), reads=[bSM], writes=[brc])
                tt("dve", rc[:, 0:8], O[:, 0:8], rc[:, 0:8], ALU.mult, [bO, brc], [brc])
                rv = rc[:, 0:8].rearrange("p (h c) -> p h c", c=2)
                stt("dve", rc[:, 8:12], rv[:, :, 1], neglam[:, 0:1], rv[:, :, 0], ALU.mult, ALU.add, [brc, Bconst], [brc])
                sqh, bsqh = rot("et", ET, BET)
                act(sqh[:, 0:4], rc[:, 8:12], AF.Square, [brc], [bsqh])
                pb, pbuf = bank("mm")
                mm(pb[:, 0:4], ones_b[:], sqh[:, 0:4], True, True, [Bconst, bsqh], pbuf)
                ts("dve", rc[:, 16:20], pb[:, 0:4], 1.0 / 128, LN_EPS, ALU.mult, ALU.add, [pbuf, brc], [brc])
                act(rc[:, 16:20], rc[:, 16:20], AF.Sqrt, [brc], [brc])
                P.op("dve", lambda e, rc=rc: e.reciprocal(out=rc[:, 16:20], in_=rc[:, 16:20]), reads=[brc], writes=[brc])
                tt("dve", rc[:, 16:20], rc[:, 16:20], rc[:, 8:12], ALU.mult, [brc], [brc])
                P.op("act", lambda e, rc=rc, b=b: e.activation(out=catT[:, 0:4, b:b + 1], in_=rc[:, 16:20].unsqueeze(2),
                                                              func=AF.Copy, scale=sublg[:, 0:1]),
                     reads=[brc, Bconst], writes=Bcat[0:4])
            if STOP in (13, 14, 15, 16, 17, 18, 19):
                return
            set_dense()
            proj_res(N, w_mo, catT, Bcat, "mo", first)
            layer_norm(N, 1)
            for pi in range(2):
                slot, bs = get_piece(("xq", pi), [(w_xq, 0, 8, 512 * pi, 512, 0)], first)
                sv = slot[:, :].rearrange("p (k c) -> p k c", k=8)
                for cc in range(4):
                    j = 4 * pi + cc
                    pb, pbuf = bank("mm")
                    for k in range(8):
                        mm(pb[:, 0:N], sv[:, k, cc * 128:(cc + 1) * 128], xb[:, k, 0:N], k == 0, k == 7,
                           [bs, Bxb[k]], pbuf)
                    act(hbuf[:, j, 0:N], pb[:, 0:N], AF.Copy, [pbuf], [Bh[j]], scale=1.0 / 16)
            set_attn()
            for b in range(4):
                for mc in range(2):
                    st_, bst_ = rot("stg", stg, Bstg)
                    dma("sp", st_[:, :], cmk[b, mc * 128:(mc + 1) * 128, :], [], [bst_], "d_stg%d" % (rr["stg"] % 3))
                    for kk in range(2):
                        pb, pbuf = bank("mm")
                        for q4 in range(4):
                            k = kk * 4 + q4
                            tr(pb[:, q4 * 128:(q4 + 1) * 128], st_[:, k * 128:(k + 1) * 128], [bst_], pbuf)
                        P.op("dve" if kk else "act", (lambda e, pb=pb, kk=kk, mc=mc: e.tensor_copy(
                            out=memKT[:, kk * 4:(kk + 1) * 4, mc * 128:(mc + 1) * 128],
                            in_=pb[:, :].rearrange("p (q m) -> p q m", q=4))) if kk else
                            (lambda e, pb=pb, kk=kk, mc=mc: e.copy(
                                out=memKT[:, kk * 4:(kk + 1) * 4, mc * 128:(mc + 1) * 128],
                                in_=pb[:, :].rearrange("p (q m) -> p q m", q=4))),
                            reads=[pbuf], writes=[Bconst])
                    st2, bst2 = rot("stg", stg, Bstg)
                    dma("sp", st2[:, :], cmv[b, mc * 128:(mc + 1) * 128, :], [], [bst2], "d_stg%d" % (rr["stg"] % 3))
                    cp("pool", memV[:, mc, :], st2[:, :], [bst2], [Bconst])
                if STOP != 22:
                    cross_core(b, b + 1)
            if STOP in (20, 22):
                return
            set_dense()
            proj_res(N, w_xo, hbuf[:, 8:16, :], Bh[8:16], "xo", first)
            layer_norm(N, 2)
            ffn(N, w_f2i, w_f2o, 3, first, "f2")
            if STOP == 21:
                return
            st_, bst_ = rot("stg", stg, Bstg)
            for kk in range(2):
                pb, pbuf = bank("mm")
                for q4 in range(4):
                    k = kk * 4 + q4
                    mm(pb[0:4, q4 * 128:(q4 + 1) * 128], x[:, k, 0:4], ident_f[:, :], True, True, [Bx[k], Bconst], pbuf)
                cp("dve", st_[0:4, kk * 512:(kk + 1) * 512], pb[0:4, :], [pbuf], [bst_])
            dma("sp", y_s, st_[0:4, :], [bst_], [], "d_stg%d" % (rr["stg"] % 3))

        def prompt_tile(ti, first):
            N = TT
            t0 = ti * TT
            if ti == 4:
                for eng_ in ("act", "dve", "pool", "pe"):
                    P.op(eng_, None, writes=[Bwst[0], Bwst[1]])
            for s in range(4):
                st_, bst_ = rot("stg", stg, Bstg)
                dma("sp", st_[:, :], xp[t0 + s * 128:t0 + (s + 1) * 128, :], [], [bst_], "d_stg%d" % (rr["stg"] % 3))
                for kk in range(2):
                    pb, pbuf = bank("mm")
                    for q4 in range(4):
                        k = kk * 4 + q4
                        tr(pb[:, q4 * 128:(q4 + 1) * 128], st_[:, k * 128:(k + 1) * 128], [bst_], pbuf)
                    for q4 in range(4):
                        k = kk * 4 + q4
                        e1 = "dve" if kk else "act"
                        cp(e1, x[:, k, s * 128:(s + 1) * 128], pb[:, q4 * 128:(q4 + 1) * 128], [pbuf], [Bx[k]])
            for k in range(8):
                cp(("pool", "dve", "act", "pool")[k % 4], xb[:, k, :], x[:, k, :], [Bx[k]], [Bxb[k]])
            ffn(N, w_f1i, w_f1o, 0, first, "f1")
            slot, bs = get_piece(("mi", 0), [(w_mi, 0, 8, 0, 512, 0)], first)
            sv = slot[:, :].rearrange("p (k c) -> p k c", k=8)
            for h in range(4):
                pb, pbuf = bank("mm")
                for k in range(8):
                    mm(pb[:, 0:N], sv[:, k, h * 128:(h + 1) * 128], xb[:, k, 0:N], k == 0, k == 7, [bs, Bxb[k]], pbuf)
                act(qT[:, h, :], pb[:, 0:N], AF.Copy, [pbuf], [BqT[h]], scale=0.125)
            slot, bs = get_piece(("mi", 1), [(w_mi, 0, 8, 512, 512, 0)], first)
            sv = slot[:, :].rearrange("p (k c) -> p k c", k=8)
            for h in range(4):
                pb, pbuf = bank("mm")
                for k in range(8):
                    mm(pb[:, 0:N], sv[:, k, h * 128:(h + 1) * 128], xb[:, k, 0:N], k == 0, k == 7, [bs, Bxb[k]], pbuf)
                for s in range(4):
                    kt = ti * 4 + s
                    cp("act" if h % 2 else "dve", KT[:, kt, h, :], pb[:, s * 128:(s + 1) * 128], [pbuf], [BKT[kt]])
            for s in range(4):
                pb, pbuf = bank("mm")
                for k in range(8):
                    mm(pb[:, :], xb[:, k, s * 128:(s + 1) * 128], sv[:, k, :], k == 0, k == 7, [bs, Bxb[k]], pbuf)
                st_, bst_ = rot("stg", stg, Bstg)
                cp("act", st_[:, 0:512], pb[:, :], [pbuf], [bst_])
                dma("sp", k_p[t0 + s * 128:t0 + (s + 1) * 128, :], st_[:, 0:512], [bst_], [], "d_stg%d" % (rr["stg"] % 3))
            slot, bs = get_piece(("mi", 2), [(w_mi, 0, 8, 1024, 512, 0)], first)
            sv = slot[:, :].rearrange("p (k c) -> p k c", k=8)
            for s in range(4):
                kt = ti * 4 + s
                pb, pbuf = bank("mm")
                for k in range(8):
                    mm(pb[:, :], xb[:, k, s * 128:(s + 1) * 128], sv[:, k, :], k == 0, k == 7, [bs, Bxb[k]], pbuf)
                st_, bst_ = rot("stg", stg, Bstg)
                cp("act", st_[:, 0:512], pb[:, :], [pbuf], [bst_])
                cp("pool", Vt[:, kt, :], st_[:, 0:512], [bst_], [BVt[kt]])
                dma("sp", v_p[t0 + s * 128:t0 + (s + 1) * 128, :], st_[:, 0:512], [bst_], [], "d_stg%d" % (rr["stg"] % 3))
            slot, bs = get_piece(("mi", 3), [(w_mi, 0, 8, 1536, 512, 0)], first)
            sv = slot[:, :].rearrange("p (k c) -> p k c", k=8)
            for g in range(4):
                pb, pbuf = bank("mm")
                for k in range(8):
                    mm(pb[:, 0:N], sv[:, k, g * 128:(g + 1) * 128], xb[:, k, 0:N], k == 0, k == 7, [bs, Bxb[k]], pbuf)
                act(uT[:, g, :], pb[:, 0:N], AF.Gelu, [pbuf], BuT2[g])
            slot, bs = get_piece(("mi", 4), [(w_mi, 0, 8, 2048, 512, 0)], first)
            sv = slot[:, :].rearrange("p (k c) -> p k c", k=8)
            vbs = []
            for s in range(4):
                pb, pbuf = bank("mm")
                for k in range(8):
                    mm(pb[:, :], xb[:, k, s * 128:(s + 1) * 128], sv[:, k, :], k == 0, k == 7, [bs, Bxb[k]], pbuf)
                gv, bgv = tmpf[s], Btmp[s]
                cl, bcl = col[s], Bcol[s]
                act(gv[:, :], pb[:, :], AF.Gelu, [pbuf], [bgv])
                vbs.append((gv, bgv, cl, bcl))
            for s in range(4):
                gv, bgv, cl, bcl = vbs[s]
                P.op("dve", lambda e, gv=gv, cl=cl: e.tensor_reduce(out=cl[:, 0:1], in_=gv[:, :], axis=AX.X, op=ALU.add),
                     reads=[bgv], writes=[bcl])
                ts("dve", cl[:, 1:2], cl[:, 0:1], -1.0 / 512, None, ALU.mult, None, [bcl], [bcl])
                ts("dve", gv[:, :], gv[:, :], cl[:, 1:2], None, ALU.add, None, [bgv, bcl], [bgv])
                tt("pool", oacc[:, :], gv[:, :], gv[:, :], ALU.mult, [bgv], [Boacc])
                P.op("dve", lambda e, cl=cl: e.tensor_reduce(out=cl[:, 2:3], in_=oacc[:, :], axis=AX.X, op=ALU.add),
                     reads=[Boacc], writes=[bcl])
                ts("dve", cl[:, 3:4], cl[:, 2:3], 1.0 / 512, LN_EPS, ALU.mult, ALU.add, [bcl], [bcl])
            for s in range(4):
                gv, bgv, cl, bcl = vbs[s]
                act(cl[:, 3:4], cl[:, 3:4], AF.Sqrt, [bcl], [bcl])
            for s in range(4):
                gv, bgv, cl, bcl = vbs[s]
                P.op("dve", lambda e, cl=cl: e.reciprocal(out=cl[:, 3:4], in_=cl[:, 3:4]), reads=[bcl], writes=[bcl])
                stt("dve", gv[:, :], gv[:, :], cl[:, 3:4], sgug[:, :], ALU.mult, ALU.mult, [bgv, bcl, Bconst], [bgv])
                tt("pool", vn[:, s, :], gv[:, :], sgub_ln[:, :], ALU.add, [bgv, Bconst], [Bvn[s]])
            set_attn()
            nk = 4 * (ti + 1)
            rms_st = {}

            def rms_a(h):
                oa, boa = (oacc, Boacc) if h % 2 == 0 else (oacc2, Boacc2)
                sqh, bsqh = rot("et", ET, BET)
                act(sqh[:, :], oa[:, :], AF.Square, [boa], [bsqh])
                pb, pbuf = bank("mm")
                mm(pb[:, :], ones_b[:], sqh[:, :], True, True, [Bconst, bsqh], pbuf)
                rs_, brs = rot("tmp", tmpf, Btmp)
                ts("dve", rs_[:, :], pb[:, :], 1.0 / 128, LN_EPS, ALU.mult, ALU.add, [pbuf], [brs])
                rms_st[h] = (rs_, brs, oa, boa)

            def rms_b(h):
                rs_, brs, oa, boa = rms_st[h]
                act(rs_[:, :], rs_[:, :], AF.Ln, [brs], [brs])
                act(rs_[:, :], rs_[:, :], AF.Exp, [brs], [brs], scale=-0.5)
                tt("dve", rs_[:, :], rs_[:, :], oa[:, :], ALU.mult, [brs, boa], [brs])

            def rms_c(h):
                rs_, brs, oa, boa = rms_st[h]
                act(catT[:, h, :], rs_[:, :], AF.Copy, [brs, Bconst], [Bcat[h]], scale=sublg[:, 0:1])

            def rms_head(h):
                rms_a(h)
                rms_b(h)
                rms_c(h)

            for h in range(4):
                oa, boa = (oacc, Boacc) if h % 2 == 0 else (oacc2, Boacc2)
                for c in range(2):
                    if c == 1 and h >= 1:
                        rms_a(h - 1)
                    O, bO = bank("pv")
                    SM, bSM = bank("pv")
                    for kt in range(nk):
                        r = kt - 4 * ti
                        q0 = max(r, 0) * 128
                        S, bS = bank("sc")
                        spec = None
                        if kt == 4 * ti - 1:
                            spec = (0, 128, 128)
                        elif r >= 0:
                            ln_ = min(256, 512 - q0)
                            spec = (q0, ln_, 0)
                        mm(S[:, q0:512], KT[c * 64:(c + 1) * 64, kt, h, :], qT[c * 64:(c + 1) * 64, h, q0:512],
                           True, spec is None, [BKT[kt], BqT[h]], bS, skip=True)
                        if spec is not None:
                            cs, ln_, bo = spec
                            mm(S[:, cs:cs + ln_], ident_b[:], BT[:, h, bo:bo + ln_], False, True, [Bconst], bS, skip=True)
                        et, bet = rot("et", ET, BET)
                        act(et[:, q0:512], S[:, q0:512], AF.Exp, [bS, Bconst], [bet], bias=far[:, h:h + 1], scale=1.0)
                        mm(O[:, q0:512], Vt[:, kt, h * 128:(h + 1) * 128], et[:, q0:512], kt == 0, kt == nk - 1,
                           [BVt[kt], bet], bO, skip=True)
                        mm(SM[:, q0:512], ones_b[:], et[:, q0:512], kt == 0, kt == nk - 1, [Bconst, bet], bSM, skip=True)
                        if c == 1 and h >= 1 and kt == 1:
                            rms_b(h - 1)
                        if c == 1 and h >= 1 and kt == 3:
                            rms_c(h - 1)
                    rc, brc = rot("tmp", tmpf, Btmp)
                    P.op("dve", lambda e, rc=rc, SM=SM: e.reciprocal(out=rc[:, :], in_=SM[:, :]), reads=[bSM], writes=[brc])
                    if c == 0:
                        tt("dve", oa[:, :], O[:, :], rc[:, :], ALU.mult, [bO, brc], [boa])
                    else:
                        tt("dve", rc[:, :], O[:, :], rc[:, :], ALU.mult, [bO, brc], [brc])
                        stt("dve", oa[:, :], rc[:, :], neglam[:, 0:1], oa[:, :], ALU.mult, ALU.add,
                            [brc, boa, Bconst], [boa])
            rms_head(3)
            for s in range(4):
                pb, pbuf = bank("mm")
                for g in range(4):
                    mm(pb[:, g * 128:(g + 1) * 128], vn[:, s, g * 128:(g + 1) * 128], trilWT[:, g, :], True, False,
                       [Bvn[s], Bconst], pbuf)
                    mm(pb[:, g * 128:(g + 1) * 128], ones_b[0:1, :], sgub_row[0:1, g * 128:(g + 1) * 128], False, True,
                       [Bconst], pbuf)
                P.op("dve", lambda e, pb=pb, s=s: e.tensor_tensor(
                    out=catT[:, 4:8, s * 128:(s + 1) * 128],
                    in0=pb[:, :].rearrange("p (g t) -> p g t", g=4),
                    in1=uT[:, :, s * 128:(s + 1) * 128], op=ALU.mult),
                    reads=[pbuf] + BuT, writes=Bcat[4:8])
            set_dense()
            proj_res(N, w_mo, catT, Bcat, "mo", first)
            layer_norm(N, 1)
            for pi in range(2):
                slot, bs = get_piece(("xq", pi), [(w_xq, 0, 8, 512 * pi, 512, 0)], first)
                sv = slot[:, :].rearrange("p (k c) -> p k c", k=8)
                for cc in range(4):
                    j = 4 * pi + cc
                    pb, pbuf = bank("mm")
                    for k in range(8):
                        mm(pb[:, 0:N], sv[:, k, cc * 128:(cc + 1) * 128], xb[:, k, 0:N], k == 0, k == 7,
                           [bs, Bxb[k]], pbuf)
                    act(hbuf[:, j, :], pb[:, 0:N], AF.Copy, [pbuf], [Bh[j]], scale=1.0 / 16)
            set_attn()
            cross_core(0, N)
            set_dense()
            proj_res(N, w_xo, hbuf[:, 8:16, :], Bh[8:16], "xo", first)
            layer_norm(N, 2)
            ffn(N, w_f2i, w_f2o, 3, first, "f2")
            for s in range(4):
                st_, bst_ = rot("stg", stg, Bstg)
                for kk in range(2):
                    pb, pbuf = bank("mm")
                    for q4 in range(4):
                        k = kk * 4 + q4
                        tr(pb[:, q4 * 128:(q4 + 1) * 128], x[:, k, s * 128:(s + 1) * 128], [Bx[k]], pbuf)
                    cp("act" if kk else "dve", st_[:, kk * 512:(kk + 1) * 512], pb[:, :], [pbuf], [bst_])
                dma("sp", y_p[t0 + s * 128:t0 + (s + 1) * 128, :], st_[:, :], [bst_], [], "d_stg%d" % (rr["stg"] % 3))


        P.barrier()
        sample_phase()
        if STOP >= 10:
            P.emit(nc); return nc
        P.barrier()
        mem_phase()
        if STOP in (5, 6):
            P.emit(nc); return nc
        for ti in range(NT_RUN):
            prompt_tile(ti, False)
        P.emit(nc)
    return nc


NT_RUN = NT
STOP = 0


def kernel(**inp):
    f32 = np.float32
    consts = _host_consts()
    nc = build_nc()
    lam_in = np.stack([inp["lambda_q1"][0], inp["lambda_k1"][0], inp["lambda_q2"][0], inp["lambda_k2"][0]]).astype(f32)
    shared = {
        "rel_bias": np.ascontiguousarray(inp["rel_bias"], dtype=f32),
        "ln_g": np.ascontiguousarray(inp["ln_g"][0]),
        "ln_b": np.ascontiguousarray(inp["ln_b"][0]),
        "ffn1_w_in": np.ascontiguousarray(inp["ffn1_w_in"][0]),
        "ffn1_w_out": np.ascontiguousarray(inp["ffn1_w_out"][0]),
        "w_mix_in": np.ascontiguousarray(inp["w_mix_in"][0]),
        "w_mix_out": np.ascontiguousarray(inp["w_mix_out"][0]),
        "lam_in": lam_in,
        "subln_g": np.ascontiguousarray(inp["subln_g"][0].reshape(128, 1)),
        "sgu_ln_g": np.ascontiguousarray(inp["sgu_ln_g"][0].reshape(1, 512)),
        "sgu_ln_b": np.ascontiguousarray(inp["sgu_ln_b"][0].reshape(1, 512)),
        "sgu_w": np.ascontiguousarray(inp["sgu_w"][0]),
        "sgu_b": np.ascontiguousarray(inp["sgu_b"][0].reshape(1, 512)),
        "xq_w": np.ascontiguousarray(inp["xq_w"][0]),
        "xkv_w": np.ascontiguousarray(inp["xkv_w"][0]),
        "xo_w": np.ascontiguousarray(inp["xo_w"][0]),
        "ffn2_w_in": np.ascontiguousarray(inp["ffn2_w_in"][0]),
        "ffn2_w_out": np.ascontiguousarray(inp["ffn2_w_out"][0]),
    }
    shared.update(consts)
    kv = np.empty((NPHYS * 128, 1024), dtype=f32)
    kv[:, 0:512] = np.asarray(inp["cache_k"]).reshape(NPHYS * 128, 512)
    kv[:, 512:1024] = np.asarray(inp["cache_v"]).reshape(NPHYS * 128, 512)
    shared["cache_kv"] = kv
    in_maps = []
    for c in range(8):
        m = dict(shared)
        m["xp"] = np.ascontiguousarray(inp["x_prompt"][c])
        m["mem"] = np.ascontiguousarray(inp["mem_prompt"][c])
        m["xs"] = np.ascontiguousarray(inp["x_sample"][NS * c:NS * (c + 1), 0, :])
        m["cmk"] = np.ascontiguousarray(inp["cache_mem_k"][0, NS * c:NS * (c + 1)]).reshape(NS, 256, D)
        m["cmv"] = np.ascontiguousarray(inp["cache_mem_v"][0, NS * c:NS * (c + 1)]).reshape(NS, 256, D)
        m["ptab"] = np.ascontiguousarray(inp["page_table"][NS * c:NS * (c + 1)]).astype(np.int32)
        in_maps.append(m)
    res = run_bass_kernel_spmd(nc, in_maps, core_ids=list(range(8)))
    R = res.results
    y_p = np.stack([R[c]["y_p"] for c in range(8)])
    k_p = np.stack([R[c]["k_p"] for c in range(8)]).reshape(1, 8, SEQ, 4, 128)
    v_p = np.stack([R[c]["v_p"] for c in range(8)]).reshape(1, 8, SEQ, 4, 128)
    mk_p = np.stack([R[c]["mk_p"] for c in range(8)]).reshape(1, 8, 256, 4, 256)
    mv_p = np.stack([R[c]["mv_p"] for c in range(8)]).reshape(1, 8, 256, 4, 256)
    y_s = np.concatenate([R[c]["y_s"] for c in range(8)]).reshape(32, 1, D)
    k_s = np.concatenate([R[c]["k_s"] for c in range(8)]).reshape(1, 32, 1, 4, 128)
    v_s = np.concatenate([R[c]["v_s"] for c in range(8)]).reshape(1, 32, 1, 4, 128)
    g_s = np.concatenate([R[c]["g_s"] for c in range(8)]).reshape(1, 32, 1, 512)
    return (y_p, y_s, k_p, v_p, mk_p, mv_p, k_s, v_s, g_s)
```

```python
import contextlib
import math
import numpy as np
import concourse.bass as bass
import concourse.mybir as mybir
from concourse.bass_utils import run_bass_kernel_spmd

F32 = mybir.dt.float32
BF16 = mybir.dt.bfloat16
I32 = mybir.dt.int32
AF = mybir.ActivationFunctionType
ALU = mybir.AluOpType
AX = mybir.AxisListType

D = 1024
SEQ = 4096
TT = 512
NT = SEQ // TT
DFF = 2816
NJ = DFF // 128
NPHYS = 5120
ALPHA = 2.0 ** 0.25
LN_EPS = 1e-5
EPS_EFF = LN_EPS / (ALPHA * ALPHA)
LAMBDA_INIT = 0.8 - 0.6 * math.exp(0.0)
NEG = -30000.0
NS = 4

ENGS = ("pe", "act", "dve", "pool", "sp")
SAME_ENGINE_SYNC = {"pe": False, "act": True, "dve": True, "pool": True, "sp": False}


class Buf:
    __slots__ = ("name", "w", "rs")

    def __init__(self, name=""):
        self.name = name
        self.w = None
        self.rs = {}


class Op:
    __slots__ = ("eng", "idx", "fn", "waits", "dma", "need_inc", "count")

    def __init__(self, eng, idx, fn):
        self.eng = eng
        self.idx = idx
        self.fn = fn
        self.waits = []
        self.dma = None
        self.need_inc = False
        self.count = None


class Prog:
    def __init__(self):
        self.streams = {e: [] for e in ENGS}
        self.dma_cnt = {}
        self.waited = {e: {} for e in ENGS}

    def op(self, eng, fn, reads=(), writes=(), dma_sem=None):
        st = self.streams[eng]
        o = Op(eng, len(st), fn)
        need = []
        for b in reads:
            if b.w is not None:
                need.append(b.w)
        for b in writes:
            if b.w is not None:
                need.append(b.w)
            need.extend(b.rs.values())
        wd = self.waited[eng]
        best = {}
        for t in need:
            if t[0] == "op":
                p = t[1]
                if p.eng == eng and not SAME_ENGINE_SYNC[eng]:
                    continue
                key = "e_" + p.eng
                if wd.get(key, -1) >= p.idx:
                    continue
                if key not in best or best[key][1].idx < p.idx:
                    best[key] = t
            else:
                _, s, v = t
                if wd.get(s, 0) >= v:
                    continue
                if s not in best or best[s][2] < v:
                    best[s] = t
        for key, t in best.items():
            wd[key] = t[1].idx if t[0] == "op" else t[2]
            if t[0] == "op":
                t[1].need_inc = True
        o.waits = list(best.values())
        if dma_sem is not None:
            self.dma_cnt[dma_sem] = self.dma_cnt.get(dma_sem, 0) + 16
            o.dma = (dma_sem, self.dma_cnt[dma_sem])
            tok = ("dma", dma_sem, self.dma_cnt[dma_sem])
            rkey = dma_sem
        else:
            tok = ("op", o)
            rkey = "e_" + eng
        if fn is not None:
            for b in reads:
                b.rs[rkey] = tok
            for b in writes:
                b.w = tok
                b.rs = {}
        st.append(o)
        return tok

    def barrier(self):
        toks = []
        for e in ENGS:
            for p in reversed(self.streams[e]):
                if p.dma is None and p.fn is not None:
                    toks.append(("op", p))
                    break
        for s, v in self.dma_cnt.items():
            toks.append(("dma", s, v))
        for e in ENGS:
            for t in toks:
                if t[0] == "op" and t[1].eng == e:
                    continue
                b2 = Buf()
                b2.w = t
                self.op(e, None, reads=[b2])

    def emit(self, nc, final_wait_eng="sp"):
        for s, v in list(self.dma_cnt.items()):
            b2 = Buf()
            b2.w = ("dma", s, v)
            self.op(final_wait_eng, None, reads=[b2])
        for e in ENGS:
            c = 0
            for o in self.streams[e]:
                if o.need_inc:
                    c += 1
                    o.count = c
        with contextlib.ExitStack() as es:
            sems = {}
            for e in ENGS:
                sems["e_" + e] = es.enter_context(nc.semaphore("e_" + e))
            for s in self.dma_cnt:
                sems[s] = es.enter_context(nc.semaphore(s))
            block = es.enter_context(nc.Block())

            def run(eng_name):
                def body(eng):
                    for o in self.streams[eng_name]:
                        for t in o.waits:
                            if t[0] == "op":
                                eng.wait_ge(sems["e_" + t[1].eng], t[1].count)
                            else:
                                eng.wait_ge(sems[t[1]], t[2])
                        if o.fn is None:
                            continue
                        ins = o.fn(eng)
                        if o.dma is not None:
                            ins.then_inc(sems[o.dma[0]], 16)
                        elif o.need_inc:
                            ins.then_inc(sems["e_" + eng_name], 1)
                return body

            block.tensor(run("pe"))
            block.scalar(run("act"))
            block.vector(run("dve"))
            block.gpsimd(run("pool"))
            block.sync(run("sp"))


def _bucket_np(n):
    n = np.asarray(n, dtype=np.int64)
    nf = np.maximum(n, 1).astype(np.float32)
    large = 16 + (np.log(nf / np.float32(16)) / np.float32(math.log(8.0)) * np.float32(16)).astype(np.int32)
    large = np.minimum(large, 31)
    return np.where(n < 16, n, large)


def _host_consts():
    ident = np.eye(128, dtype=np.float32)
    tril = np.tril(np.ones((128, 128), dtype=np.float32))
    ohm = np.zeros((32, 383), dtype=np.float32)
    for m in range(383):
        n = m - 127
        if n >= 0:
            ohm[int(_bucket_np(n)), m] = 1.0
    maskrow = np.zeros((4, 383), dtype=np.float32)
    maskrow[:, :127] = NEG
    iota = np.arange(128, dtype=np.float32).reshape(128, 1)
    return {"c_ident": ident, "c_tril": tril, "c_ohm": ohm, "c_maskrow": maskrow, "c_iota": iota}


class Ctx:
    pass


def build_nc():
    nc = bass.Bass("TRN2", target_bir_lowering=False)
    P = Prog()

    def din(name, shape, dt=F32):
        return nc.dram_tensor(name, list(shape), dt, kind="ExternalInput").ap()

    def dout(name, shape, dt=F32):
        return nc.dram_tensor(name, list(shape), dt, kind="ExternalOutput").ap()

    def dscr(name, shape, dt=F32):
        return nc.dram_tensor(name, list(shape), dt, kind="Internal").ap()

    xp = din("xp", [SEQ, D])
    mem = din("mem", [256, D])
    rel_bias = din("rel_bias", [32, 4])
    ln_g = din("ln_g", [4, D])
    ln_b = din("ln_b", [4, D])
    w_f1i = din("ffn1_w_in", [D, 2 * DFF])
    w_f1o = din("ffn1_w_out", [DFF, D])
    w_mi = din("w_mix_in", [D, 2560])
    w_mo = din("w_mix_out", [D, D])
    lam_in = din("lam_in", [4, 64])
    subln_g = din("subln_g", [128, 1])
    sgu_ln_g = din("sgu_ln_g", [1, 512])
    sgu_ln_b = din("sgu_ln_b", [1, 512])
    sgu_w = din("sgu_w", [4, 128, 128])
    sgu_b = din("sgu_b", [1, 512])
    w_xq = din("xq_w", [D, D])
    w_xkv = din("xkv_w", [D, 2 * D])
    w_xo = din("xo_w", [D, D])
    w_f2i = din("ffn2_w_in", [D, 2 * DFF])
    w_f2o = din("ffn2_w_out", [DFF, D])
    c_ident = din("c_ident", [128, 128])
    c_tril = din("c_tril", [128, 128])
    c_ohm = din("c_ohm", [32, 383])
    c_maskrow = din("c_maskrow", [4, 383])
    c_iota = din("c_iota", [128, 1])
    xs = din("xs", [NS, D])
    cache_kv = din("cache_kv", [NPHYS * 128, 1024])
    cmk = din("cmk", [NS, 256, D])
    cmv = din("cmv", [NS, 256, D])
    ptab = din("ptab", [NS, 128], I32)

    y_p = dout("y_p", [SEQ, D])
    k_p = dout("k_p", [SEQ, 512])
    v_p = dout("v_p", [SEQ, 512])
    mk_p = dout("mk_p", [256, D])
    mv_p = dout("mv_p", [256, D])
    y_s = dout("y_s", [NS, D])
    k_s = dout("k_s", [NS, 512])
    v_s = dout("v_s", [NS, 512])
    g_s = dout("g_s", [NS, 512])

    NPIECE = 64
    wscr = dscr("wscr", [NPIECE, 128, 4096], BF16)
    dscr_d = dscr("dscr_d", [4, 383])
    dscr_f = dscr("dscr_f", [4, 128 * 383])
    Bwscr = [Buf("wscr%d" % i) for i in range(NPIECE)]

    with contextlib.ExitStack() as es:
        def sb(name, shape, dt):
            return es.enter_context(nc.sbuf_tensor(name, list(shape), dt))

        ident_f = sb("ident_f", [128, 128], F32)
        ident_b = sb("ident_b", [128, 128], BF16)
        ones_b = sb("ones_b", [128, 128], BF16)
        ones_f = sb("ones_f", [128, 128], F32)
        lng = sb("lng", [128, 4, 8], F32)
        lnb = sb("lnb", [128, 4, 8], F32)
        BT = sb("BT", [128, 4, 256], BF16)
        BTf = sb("BTf", [128, 4, 256], F32)
        far = sb("far", [128, 4], F32)
        neglam = sb("neglam", [128, 1], F32)
        sublg = sb("sublg", [128, 1], F32)
        sgug = sb("sgug", [128, 512], F32)
        sgub_ln = sb("sgub_ln", [128, 512], F32)
        sgub_row = sb("sgub_row", [1, 512], BF16)
        sgub_rowf = sb("sgub_rowf", [1, 512], F32)
        trilWT = sb("trilWT", [128, 4, 128], BF16)
        memKT = sb("memKT", [128, 8, 256], BF16)
        memV = sb("memV", [128, 2, 1024], BF16)
        KT2 = sb("KT", [128, 32 * 512], BF16)
        Vt2 = sb("Vt", [128, 32 * 512], BF16)
        KT = KT2[:, :].rearrange("p (t h k) -> p t h k", t=32, h=4)
        Vt = Vt2[:, :].rearrange("p (t f) -> p t f", t=32)
        BKT = [Buf("KT%d" % i) for i in range(32)]
        BVt = [Buf("Vt%d" % i) for i in range(32)]
        x = sb("x", [128, 8, TT], F32)
        xb = sb("xb", [128, 8, TT], BF16)
        hbuf2 = sb("hbuf", [128, NJ * TT], BF16)
        hbuf = hbuf2[:, :].rearrange("p (k t) -> p k t", k=NJ)
        sq = hbuf2[:, 0:8 * TT].rearrange("p (k t) -> p k t", k=8)
        Bx = [Buf("x%d" % i) for i in range(8)]
        Bxb = [Buf("xb%d" % i) for i in range(8)]
        Bh = [Buf("h%d" % i) for i in range(NJ)]
        Bsq = Bh[0:8]
        NSLOT = 3
        wslot = [sb("wslot%d" % i, [128, 4096], BF16) for i in range(NSLOT)]
        Bws = [Buf("ws%d" % i) for i in range(NSLOT)]
        wstage = [KT2[:, 8192:16384].bitcast(F32), Vt2[:, 8192:16384].bitcast(F32)]
        Bwst = [Buf("wst%d" % i) for i in range(2)]
        tmpf = [sb("tmpf%d" % i, [128, TT], F32) for i in range(4)]
        Btmp = [Buf("tmpf%d" % i) for i in range(4)]
        stg = [sb("stg%d" % i, [128, 1024], F32) for i in range(3)]
        Bstg = [Buf("stg%d" % i) for i in range(3)]
        qT = hbuf2[:, 16 * TT:20 * TT].rearrange("p (k t) -> p k t", k=4)
        BqT = Bh[16:20]
        uT = hbuf2[:, 8 * TT:16 * TT].bitcast(F32).rearrange("p (k t) -> p k t", k=4)
        BuT2 = [[Bh[8 + 2 * g], Bh[9 + 2 * g]] for g in range(4)]
        BuT = Bh[8:16]
        vn = sb("vn", [128, 4, 512], BF16)
        Bvn = [Buf("vn%d" % i) for i in range(4)]
        catT = sq
        Bcat = Bh[0:8]
        ET = [sb("ET%d" % i, [128, TT], BF16) for i in range(4)]
        BET = [Buf("ET%d" % i) for i in range(4)]
        oacc = sb("oacc", [128, TT], F32)
        Boacc = Buf("oacc")
        oacc2 = sb("oacc2", [128, TT], F32)
        Boacc2 = Buf("oacc2")
        col = [sb("col%d" % i, [128, 8], F32) for i in range(4)]
        Bcol = [Buf("col%d" % i) for i in range(4)]
        psum = [es.enter_context(nc.psum_tensor("ps%d" % i, [128, 512], F32)) for i in range(8)]
        Bps = [Buf("ps%d" % i) for i in range(8)]

        rr = {"mm": 0, "sc": 0, "pv": 0, "tmp": 0, "stg": 0, "et": 0, "ws": 0, "wst": 0, "col": 0}
        POOLS = {"mm": [0, 1, 2, 3, 4, 5], "sc": [2, 3], "pv": [4, 5, 6, 7]}

        def set_dense():
            POOLS["mm"] = [0, 1, 2, 3, 4, 5]

        def set_attn():
            POOLS["mm"] = [0, 1]

        def bank(pool):
            lst = POOLS[pool]
            i = lst[rr[pool] % len(lst)]
            rr[pool] += 1
            return psum[i], Bps[i]

        def rot(key, arrs, bufs):
            i = rr[key] % len(arrs)
            rr[key] += 1
            return arrs[i], bufs[i]

        def mm(out, lhsT, rhs, start, stop, reads, wbuf, skip=False):
            if skip:
                P.op("pe", lambda e: e.matmul(out, lhsT=lhsT, rhs=rhs, start=start, stop=stop, skip_group_check=True),
                     reads=reads, writes=[wbuf])
            else:
                P.op("pe", lambda e: e.matmul(out, lhsT=lhsT, rhs=rhs, start=start, stop=stop),
                     reads=reads, writes=[wbuf])

        def tr(out, in_, reads, wbuf):
            P.op("pe", lambda e: e.transpose(out, in_, ident_f[:]), reads=reads + [Bconst], writes=[wbuf])

        def act(out, in_, func, reads, writes, bias=None, scale=1.0):
            if bias is None:
                P.op("act", lambda e: e.activation(out=out, in_=in_, func=func, scale=scale),
                     reads=reads, writes=writes)
            else:
                P.op("act", lambda e: e.activation(out=out, in_=in_, func=func, bias=bias, scale=scale),
                     reads=reads, writes=writes)

        def tt(eng, out, in0, in1, op, reads, writes):
            P.op(eng, lambda e: e.tensor_tensor(out=out, in0=in0, in1=in1, op=op), reads=reads, writes=writes)

        def ts(eng, out, in0, s1, s2, op0, op1, reads, writes):
            if s2 is None:
                P.op(eng, lambda e: e.tensor_scalar(out=out, in0=in0, scalar1=s1, scalar2=None, op0=op0),
                     reads=reads, writes=writes)
            else:
                P.op(eng, lambda e: e.tensor_scalar(out=out, in0=in0, scalar1=s1, scalar2=s2, op0=op0, op1=op1),
                     reads=reads, writes=writes)

        def stt(eng, out, in0, scalar, in1, op0, op1, reads, writes):
            P.op(eng, lambda e: e.scalar_tensor_tensor(out=out, in0=in0, scalar=scalar, in1=in1, op0=op0, op1=op1),
                 reads=reads, writes=writes)

        def cp(eng, out, in_, reads, writes):
            if eng == "act":
                P.op("act", lambda e: e.copy(out=out, in_=in_), reads=reads, writes=writes)
            else:
                P.op(eng, lambda e: e.tensor_copy(out=out, in_=in_), reads=reads, writes=writes)

        def dma(eng, out, in_, reads, writes, sem, nonc=False):
            if nonc:
                P.op(eng, lambda e: e.dma_start(out=out, in_=in_, allow_slow_non_contiguous=True),
                     reads=reads, writes=writes, dma_sem=sem)
            else:
                P.op(eng, lambda e: e.dma_start(out=out, in_=in_), reads=reads, writes=writes, dma_sem=sem)

        Bconst = Buf("const")

        dma("sp", ident_f[:], c_ident, [], [Bconst], "d_c1")
        cp("dve", ident_b[:], ident_f[:], [Bconst], [Bconst])
        P.op("pool", lambda e: e.memset(ones_b[:], 1.0), writes=[Bconst])
        P.op("pool", lambda e: e.memset(ones_f[:], 1.0), writes=[Bconst])
        dma("sp", lng[:], ln_g.rearrange("i (k p) -> p i k", p=128), [], [Bconst], "d_c2", nonc=True)
        dma("sp", lnb[:], ln_b.rearrange("i (k p) -> p i k", p=128), [], [Bconst], "d_c3", nonc=True)
        dma("sp", sublg[:], subln_g, [], [Bconst], "d_c4", nonc=True)
        ts("dve", sublg[:], sublg[:], 1.0 - LAMBDA_INIT, None, ALU.mult, None, [Bconst], [Bconst])
        dma("sp", sgug[:], sgu_ln_g.to_broadcast([128, 512]), [], [Bconst], "d_c5", nonc=True)
        dma("sp", sgub_ln[:], sgu_ln_b.to_broadcast([128, 512]), [], [Bconst], "d_c6", nonc=True)
        dma("sp", sgub_rowf[:], sgu_b, [], [Bconst], "d_c7")
        cp("dve", sgub_row[:], sgub_rowf[:], [Bconst], [Bconst])
        lam_t = sb("lam_t", [128, 4, 64], F32)
        lam_s = sb("lam_s", [128, 4], F32)
        dma("sp", lam_t[:], bass.AP(lam_in.tensor, 0, [[0, 128], [64, 4], [1, 64]]), [], [Bconst], "d_c8", nonc=True)
        tt("dve", lam_t[:, 0, :], lam_t[:, 0, :], lam_t[:, 1, :], ALU.mult, [Bconst], [Bconst])
        tt("dve", lam_t[:, 2, :], lam_t[:, 2, :], lam_t[:, 3, :], ALU.mult, [Bconst], [Bconst])
        P.op("dve", lambda e: e.tensor_reduce(out=lam_s[:, 0:1], in_=lam_t[:, 0, :], axis=AX.X, op=ALU.add),
             reads=[Bconst], writes=[Bconst])
        P.op("dve", lambda e: e.tensor_reduce(out=lam_s[:, 1:2], in_=lam_t[:, 2, :], axis=AX.X, op=ALU.add),
             reads=[Bconst], writes=[Bconst])
        act(lam_s[:, 2:4], lam_s[:, 0:2], AF.Exp, [Bconst], [Bconst])
        tt("dve", neglam[:], lam_s[:, 3:4], lam_s[:, 2:3], ALU.subtract, [Bconst], [Bconst])
        ts("dve", neglam[:], neglam[:], -LAMBDA_INIT, None, ALU.add, None, [Bconst], [Bconst])

        if STOP == 1:
            P.emit(nc); return nc
        tab = sb("tab", [32, 4], F32)
        ohm = sb("ohm", [32, 383], F32)
        mrow = sb("mrow", [4, 383], F32)
        dvec = sb("dvec", [4, 383], F32)
        dma("sp", tab[:], rel_bias, [], [Bconst], "d_c9")
        dma("sp", ohm[:], c_ohm, [], [Bconst], "d_c10")
        dma("sp", mrow[:], c_maskrow, [], [Bconst], "d_c11")
        pb, pbuf = bank("mm")
        mm(pb[0:4, 0:383], tab[:], ohm[:], True, True, [Bconst], pbuf)
        cp("dve", dvec[:], pb[0:4, 0:383], [pbuf], [Bconst])
        ts("dve", dvec[:], dvec[:], dvec[:, 382:383], None, ALU.subtract, None, [Bconst], [Bconst])
        tt("dve", dvec[:], dvec[:], mrow[:], ALU.add, [Bconst], [Bconst])
        Bdd = Buf("dscr_d")
        Bdf = Buf("dscr_f")
        dma("sp", dscr_d, dvec[:], [Bconst], [Bdd], "d_c12")
        for h in range(4):
            dma("sp", bass.AP(dscr_f.tensor, h * 128 * 383, [[383, 128], [1, 383]]),
                bass.AP(dscr_d.tensor, h * 383, [[0, 128], [1, 383]]), [Bdd], [Bdf], "d_c13", nonc=True)
        for h in range(4):
            dma("sp", BTf[:, h, :], bass.AP(dscr_f.tensor, h * 128 * 383 + 127, [[382, 128], [1, 256]]),
                [Bdf], [Bconst], "d_c14", nonc=True)
        cp("dve", BT[:], BTf[:], [Bconst], [Bconst])
        if STOP == 2:
            P.emit(nc); return nc
        dma("sp", far[:], bass.AP(rel_bias.tensor, 31 * 4, [[0, 128], [1, 4]]), [], [Bconst], "d_c15", nonc=True)
        wtmp = sb("wtmp", [128, 4, 128], F32)
        trm = sb("trm", [128, 128], F32)
        dma("sp", wtmp[:], sgu_w.rearrange("g t s -> t g s"), [], [Bconst], "d_c16", nonc=True)
        dma("sp", trm[:], c_tril, [], [Bconst], "d_c17")
        for g in range(4):
            tt("dve", wtmp[:, g, :], wtmp[:, g, :], trm[:], ALU.mult, [Bconst], [Bconst])
            pb, pbuf = bank("mm")
            tr(pb[:, 0:128], wtmp[:, g, :], [Bconst], pbuf)
            cp("dve", trilWT[:, g, :], pb[:, 0:128], [pbuf], [Bconst])

        if STOP == 3:
            P.emit(nc); return nc
        piece_id = {}

        def get_piece(key, blocks, first):
            if key not in piece_id:
                piece_id[key] = len(piece_id)
            pid = piece_id[key]
            slot, bslot = rot("ws", wslot, Bws)
            tot = sum(nk * ncols for (_, _, nk, _, ncols, _) in blocks)
            if first:
                st, bst = rot("wst", wstage, Bwst)
                for (W, r0, nk, c0, ncols, off) in blocks:
                    src = W[r0:r0 + nk * 128, c0:c0 + ncols].rearrange("(k p) c -> p k c", p=128)
                    dst = st[:, off:off + nk * ncols].rearrange("p (k c) -> p k c", c=ncols)
                    dma("sp", dst, src, [], [bst], "d_wst%d" % (rr["wst"] % 2))
                ceng = ("act", "dve", "pool")[pid % 3]
                cp(ceng, slot[:, 0:tot], st[:, 0:tot], [bst], [bslot])
                dma("sp", wscr[pid][:, 0:tot], slot[:, 0:tot], [bslot], [Bwscr[pid]], "d_wsc%d" % (rr["ws"] % NSLOT))
            else:
                dma("sp", slot[:, 0:tot], wscr[pid][:, 0:tot], [Bwscr[pid]], [bslot], "d_ws%d" % (rr["ws"] % NSLOT))
            return slot, bslot

        ln_state = {}

        def ln_begin():
            ln_state["pend"] = None
            ln_state["cnt"] = 0

        def ln_stats(k, N, last):
            sqs, bsq_ = ln_state["sq%d" % k]
            mm(psum[6][:, 0:N], ones_b[:], xb[:, k, 0:N], k == 0, last, [Bconst, Bxb[k]], Bps[6])
            mm(psum[7][:, 0:N], ones_b[:], sqs[:, 0:N], k == 0, last, [Bconst, bsq_], Bps[7])

        def ln_pre(k, N):
            cp("pool", xb[:, k, 0:N], x[:, k, 0:N], [Bx[k]], [Bxb[k]])
            sqs, bsq_ = rot("et", ET, BET)
            act(sqs[:, 0:N], x[:, k, 0:N], AF.Square, [Bx[k]], [bsq_])
            ln_state["sq%d" % k] = (sqs, bsq_)
            if k >= 1:
                ln_stats(k - 1, N, False)

        def layer_norm(N, li):
            ln_stats(7, N, True)
            s1, b1 = psum[6], Bps[6]
            s2, b2 = psum[7], Bps[7]
            mean, bm = rot("tmp", tmpf, Btmp)
            msq, bq = rot("tmp", tmpf, Btmp)
            rstd, br = rot("tmp", tmpf, Btmp)
            ts("dve", mean[:, 0:N], s1[:, 0:N], 1.0 / D, None, ALU.mult, None, [b1], [bm])
            tt("dve", msq[:, 0:N], mean[:, 0:N], mean[:, 0:N], ALU.mult, [bm], [bq])
            stt("dve", rstd[:, 0:N], s2[:, 0:N], 1.0 / D, msq[:, 0:N], ALU.mult, ALU.subtract, [b2, bq], [br])
            ts("dve", rstd[:, 0:N], rstd[:, 0:N], EPS_EFF, None, ALU.add, None, [br], [br])
            act(rstd[:, 0:N], rstd[:, 0:N], AF.Ln, [br], [br])
            act(rstd[:, 0:N], rstd[:, 0:N], AF.Exp, [br], [br], scale=-0.5)
            for k in range(8):
                eng = "dve"
                tt(eng, x[:, k, 0:N], x[:, k, 0:N], mean[:, 0:N], ALU.subtract, [Bx[k], bm], [Bx[k]])
                tt(eng, x[:, k, 0:N], x[:, k, 0:N], rstd[:, 0:N], ALU.mult, [Bx[k], br], [Bx[k]])
                act(xb[:, k, 0:N], x[:, k, 0:N], AF.Identity, [Bx[k], Bconst], [Bxb[k]],
                    bias=lnb[:, li, k:k + 1], scale=lng[:, li, k:k + 1])
                ts("pool", x[:, k, 0:N], x[:, k, 0:N], lng[:, li, k:k + 1], lnb[:, li, k:k + 1], ALU.mult, ALU.add,
                   [Bx[k], Bconst], [Bx[k]])

        def ffn(N, Win, Wout, li, first, tag):
            for pi in range(11):
                slot, bs = get_piece((tag, "in", pi),
                                     [(Win, 0, 8, 256 * pi, 256, 0), (Win, 0, 8, DFF + 256 * pi, 256, 2048)], first)
                sv = slot[:, :].rearrange("p (a k c) -> p a k c", a=2, k=8)
                for jj in range(2):
                    j = 2 * pi + jj
                    A, bA = bank("mm")
                    for k in range(8):
                        mm(A[:, 0:N], sv[:, 0, k, jj * 128:(jj + 1) * 128], xb[:, k, 0:N], k == 0, k == 7,
                           [bs, Bxb[k]], bA)
                    Bm, bB = bank("mm")
                    for k in range(8):
                        mm(Bm[:, 0:N], sv[:, 1, k, jj * 128:(jj + 1) * 128], xb[:, k, 0:N], k == 0, k == 7,
                           [bs, Bxb[k]], bB)
                    t, bt = rot("tmp", tmpf, Btmp)
                    act(t[:, 0:N], A[:, 0:N], AF.Silu, [bA], [bt])
                    tt("dve", hbuf[:, j, 0:N], t[:, 0:N], Bm[:, 0:N], ALU.mult, [bt, bB], [Bh[j]])
            for c in range(8):
                slot, bs = get_piece((tag, "out", c), [(Wout, 0, NJ, 128 * c, 128, 0)], first)
                sv = slot[:, 0:NJ * 128].rearrange("p (k c) -> p k c", k=NJ)
                Y, bY = bank("mm")
                for k in range(NJ):
                    mm(Y[:, 0:N], sv[:, k, :], hbuf[:, k, 0:N], k == 0, k == NJ - 1, [bs, Bh[k]], bY)
                stt("dve", x[:, c, 0:N], Y[:, 0:N], 0.5 / ALPHA, x[:, c, 0:N], ALU.mult, ALU.add,
                    [bY, Bx[c]], [Bx[c]])
                ln_pre(c, N)
            layer_norm(N, li)

        def proj_res(N, W, src, Bsrc, tag, first):
            for pi in range(2):
                slot, bs = get_piece((tag, pi), [(W, 0, 8, 512 * pi, 512, 0)], first)
                sv = slot[:, :].rearrange("p (k c) -> p k c", k=8)
                for cc in range(4):
                    c = 4 * pi + cc
                    Y, bY = bank("mm")
                    for k in range(8):
                        mm(Y[:, 0:N], sv[:, k, cc * 128:(cc + 1) * 128], src[:, k, 0:N], k == 0, k == 7,
                           [bs, Bsrc[k]], bY)
                    stt("dve", x[:, c, 0:N], Y[:, 0:N], 1.0 / ALPHA, x[:, c, 0:N], ALU.mult, ALU.add,
                        [bY, Bx[c]], [Bx[c]])
                    ln_pre(c, N)

        def mem_phase():
            for s in range(2):
                st_, bst_ = rot("stg", stg, Bstg)
                dma("sp", st_[:, :], mem[s * 128:(s + 1) * 128, :], [], [bst_], "d_stg%d" % (rr["stg"] % 3))
                for kk in range(2):
                    pb, pbuf = bank("mm")
                    for q4 in range(4):
                        k = kk * 4 + q4
                        tr(pb[:, q4 * 128:(q4 + 1) * 128], st_[:, k * 128:(k + 1) * 128], [bst_], pbuf)
                    for q4 in range(4):
                        k = kk * 4 + q4
                        cp("dve" if kk else "act", hbuf[:, k, s * 128:(s + 1) * 128],
                           pb[:, q4 * 128:(q4 + 1) * 128], [pbuf], [Bh[k]])
            if STOP == 5:
                return
            for pi in range(4 if STOP != 6 else 1):
                slot, bs = rot("ws", wslot, Bws)
                src = w_xkv[:, 512 * pi:512 * (pi + 1)].rearrange("(k p) c -> p k c", p=128)
                P.op("pool", lambda e, slot=slot, src=src: e.dma_start(
                    out=slot[:, :].rearrange("p (k c) -> p k c", k=8), in_=src),
                    writes=[bs], dma_sem="d_ws%d" % (rr["ws"] % NSLOT))
                sv = slot[:, :].rearrange("p (k c) -> p k c", k=8)
                if pi < 2:
                    for cc in range(4):
                        j = 4 * pi + cc
                        pb, pbuf = bank("mm")
                        for k in range(8):
                            mm(pb[:, 0:256], sv[:, k, cc * 128:(cc + 1) * 128], hbuf[:, k, 0:256], k == 0, k == 7,
                               [bs, Bh[k]], pbuf)
                        cp("act", memKT[:, j, :], pb[:, 0:256], [pbuf], [Bconst])
                for s in range(2):
                    pb, pbuf = bank("mm")
                    for k in range(8):
                        mm(pb[:, :], hbuf[:, k, s * 128:(s + 1) * 128], sv[:, k, :], k == 0, k == 7,
                           [bs, Bh[k]], pbuf)
                    st_, bst_ = rot("stg", stg, Bstg)
                    cp("dve", st_[:, 0:512], pb[:, :], [pbuf], [bst_])
                    if pi >= 2:
                        cp("pool", memV[:, s, 512 * (pi - 2):512 * (pi - 1)], st_[:, 0:512], [bst_], [Bconst])
                        dma("sp", mv_p[s * 128:(s + 1) * 128, 512 * (pi - 2):512 * (pi - 1)], st_[:, 0:512],
                            [bst_], [], "d_stg%d" % (rr["stg"] % 3))
                    else:
                        dma("sp", mk_p[s * 128:(s + 1) * 128, 512 * pi:512 * (pi + 1)], st_[:, 0:512],
                            [bst_], [], "d_stg%d" % (rr["stg"] % 3))

        def cross_core(c0, c1):
            for hm in range(4):
                ets = []
                for mc in range(2):
                    S, bS = bank("sc")
                    for dc in range(2):
                        mm(S[:, c0:c1], memKT[:, hm * 2 + dc, mc * 128:(mc + 1) * 128], hbuf[:, hm * 2 + dc, c0:c1],
                           dc == 0, dc == 1, [Bconst, Bh[hm * 2 + dc]], bS)
                    et, bet = rot("et", ET, BET)
                    act(et[:, c0:c1], S[:, c0:c1], AF.Exp, [bS], [bet])
                    ets.append((et, bet))
                SM, bSM = bank("pv")
                for mc in range(2):
                    mm(SM[:, c0:c1], ones_b[:], ets[mc][0][:, c0:c1], mc == 0, mc == 1, [Bconst, ets[mc][1]], bSM)
                rc, brc = rot("tmp", tmpf, Btmp)
                P.op("dve", lambda e, rc=rc, SM=SM: e.reciprocal(out=rc[:, c0:c1], in_=SM[:, c0:c1]), reads=[bSM], writes=[brc])
                for dc in range(2):
                    O, bO = bank("pv")
                    for mc in range(2):
                        mm(O[:, c0:c1], memV[:, mc, hm * 256 + dc * 128:hm * 256 + (dc + 1) * 128], ets[mc][0][:, c0:c1],
                           mc == 0, mc == 1, [Bconst, ets[mc][1]], bO)
                    tt("dve", hbuf[:, 8 + hm * 2 + dc, c0:c1], O[:, c0:c1], rc[:, c0:c1], ALU.mult, [bO, brc], [Bh[8 + hm * 2 + dc]])


        srow = sb("srow", [4, 3, 512], F32)
        Bsrow = [Buf("srow%d" % i) for i in range(3)]
        vrow_b = sb("vrow_b", [128, 512], BF16)
        vnrow_b = sb("vnrow_b", [4, 512], BF16)
        dW = sb("dW", [4, 4, 4], BF16)
        w00 = sb("w00", [4, 4], F32)
        dnew = sb("dnew", [128, 4], F32)
        MBall = sb("MBall", [128, 4, 8], BF16)
        MBf = sb("MBf", [128, 4, 8], F32)
        t1c = sb("t1c", [128, 1], F32)
        BSl = sb("BSl", [128, 4, 2], BF16)
        iota_f = sb("iota_f", [128, 1], F32)
        ptb = sb("ptb", [128, 128], I32)
        idx_all = sb("idx_all", [128, 128], I32)
        Bidx = Buf("idx")
        Bsm = Buf("smisc")

        def sample_phase():
            N = NS
            first = True
            dma("sp", iota_f[:], c_iota, [], [Bsm], "d_s1")
            dma("sp", w00[:].unsqueeze(2), bass.AP(sgu_w.tensor, 0, [[0, 4], [128 * 128, 4], [1, 1]]), [], [Bsm], "d_s2", nonc=True)
            dma("sp", dnew[:].unsqueeze(2), bass.AP(dscr_d.tensor, 127, [[0, 128], [383, 4], [1, 1]]), [Bdd], [Bsm], "d_s3", nonc=True)
            for g in range(4):
                ts("dve", dW[:, g, :], ident_f[0:4, 0:4], w00[:, g:g + 1], None, ALU.mult, None, [Bsm, Bconst], [Bsm])
            for b in range(4):
                ts("dve", t1c[:, 0:1], ident_f[:, b:b + 1], -NEG, NEG, ALU.mult, ALU.add, [Bconst, Bsm], [Bsm])
                for c in range(2):
                    ts("dve", MBf[:, b, :].rearrange("p (h c) -> p h c", c=2)[:, :, c], dnew[:, :], ident_f[:, b:b + 1],
                       t1c[:, 0:1], ALU.mult, ALU.add, [Bsm, Bconst], [Bsm])
            cp("dve", MBall[:, :, :], MBf[:, :, :], [Bsm], [Bsm])
            for c in range(2):
                cp("dve", BSl[:, :, c:c + 1], BT[:, :, 128:129], [Bconst], [Bsm])
            if STOP == 10:
                return
            st_, bst_ = rot("stg", stg, Bstg)
            dma("sp", st_[0:4, :], xs, [], [bst_], "d_stg%d" % (rr["stg"] % 3))
            pb, pbuf = bank("mm")
            for k in range(8):
                P.op("pe", lambda e, pb=pb, st_=st_, k=k: e.transpose(pb[:, k * 4:(k + 1) * 4], st_[0:4, k * 128:(k + 1) * 128],
                                                                  ident_f[0:4, 0:4]), reads=[bst_, Bconst], writes=[pbuf])
            P.op("dve", lambda e, pb=pb: e.tensor_copy(out=x[:, :, 0:4], in_=pb[:, 0:32].rearrange("p (k b) -> p k b", k=8)),
                 reads=[pbuf], writes=Bx)
            P.op("pool", lambda e: e.tensor_copy(out=xb[:, :, 0:4], in_=x[:, :, 0:4]), reads=Bx, writes=Bxb)
            ffn(N, w_f1i, w_f1o, 0, first, "f1")
            if STOP == 11:
                return
            slot, bs = get_piece(("mi", 0), [(w_mi, 0, 8, 0, 512, 0)], first)
            sv = slot[:, :].rearrange("p (k c) -> p k c", k=8)
            for h in range(4):
                pb, pbuf = bank("mm")
                for k in range(8):
                    mm(pb[:, 0:N], sv[:, k, h * 128:(h + 1) * 128], xb[:, k, 0:N], k == 0, k == 7, [bs, Bxb[k]], pbuf)
                act(qT[:, h, 0:N], pb[:, 0:N], AF.Copy, [pbuf], [BqT[h]], scale=0.125)
            slot, bs = get_piece(("mi", 1), [(w_mi, 0, 8, 512, 512, 0)], first)
            sv = slot[:, :].rearrange("p (k c) -> p k c", k=8)
            P.op("pool", lambda e: e.memset(KT[:, 15, :, :].rearrange("p h k -> p (h k)"), 0.0), writes=[BKT[15]])
            P.op("pool", lambda e: e.memset(vrow_b[:, :], 0.0), writes=[Bsm])
            for h in range(4):
                pb, pbuf = bank("mm")
                for k in range(8):
                    mm(pb[:, 0:N], sv[:, k, h * 128:(h + 1) * 128], xb[:, k, 0:N], k == 0, k == 7, [bs, Bxb[k]], pbuf)
                cp("dve", KT[:, 15, h, 0:4], pb[:, 0:N], [pbuf], [BKT[15]])
            pb, pbuf = bank("mm")
            for k in range(8):
                mm(pb[0:4, :], xb[:, k, 0:4], sv[:, k, :], k == 0, k == 7, [bs, Bxb[k]], pbuf)
            cp("act", srow[:, 0, :], pb[0:4, :], [pbuf], [Bsrow[0], Bsm])
            dma("sp", k_s, srow[:, 0, :], [Bsrow[0]], [], "d_s4")
            slot, bs = get_piece(("mi", 2), [(w_mi, 0, 8, 1024, 512, 0)], first)
            sv = slot[:, :].rearrange("p (k c) -> p k c", k=8)
            pb, pbuf = bank("mm")
            for k in range(8):
                mm(pb[0:4, :], xb[:, k, 0:4], sv[:, k, :], k == 0, k == 7, [bs, Bxb[k]], pbuf)
            cp("act", srow[:, 1, :], pb[0:4, :], [pbuf], [Bsrow[1]])
            cp("pool", vrow_b[0:4, :], srow[:, 1, :], [Bsrow[1]], [Bsm])
            dma("sp", v_s, srow[:, 1, :], [Bsrow[1]], [], "d_s5")
            slot, bs = get_piece(("mi", 3), [(w_mi, 0, 8, 1536, 512, 0)], first)
            sv = slot[:, :].rearrange("p (k c) -> p k c", k=8)
            for g in range(4):
                pb, pbuf = bank("mm")
                for k in range(8):
                    mm(pb[:, 0:N], sv[:, k, g * 128:(g + 1) * 128], xb[:, k, 0:N], k == 0, k == 7, [bs, Bxb[k]], pbuf)
                act(uT[:, g, 0:N], pb[:, 0:N], AF.Gelu, [pbuf], BuT2[g])
            slot, bs = get_piece(("mi", 4), [(w_mi, 0, 8, 2048, 512, 0)], first)
            sv = slot[:, :].rearrange("p (k c) -> p k c", k=8)
            pb, pbuf = bank("mm")
            for k in range(8):
                mm(pb[0:4, :], xb[:, k, 0:4], sv[:, k, :], k == 0, k == 7, [bs, Bxb[k]], pbuf)
            gv = srow[:, 2, :]
            bgv = Bsrow[2]
            cl, bcl = rot("col", col, Bcol)
            act(gv, pb[0:4, :], AF.Gelu, [pbuf], [bgv])
            P.op("dve", lambda e, cl=cl: e.tensor_reduce(out=cl[0:4, 0:1], in_=srow[:, 2, :], axis=AX.X, op=ALU.add),
                 reads=[bgv], writes=[bcl])
            ts("dve", cl[0:4, 1:2], cl[0:4, 0:1], -1.0 / 512, None, ALU.mult, None, [bcl], [bcl])
            ts("dve", gv, gv, cl[0:4, 1:2], None, ALU.add, None, [bgv, bcl], [bgv])
            junk, bj = rot("tmp", tmpf, Btmp)
            tt("pool", junk[0:4, :], gv, gv, ALU.mult, [bgv], [bj])
            P.op("dve", lambda e, junk=junk, cl=cl: e.tensor_reduce(out=cl[0:4, 2:3], in_=junk[0:4, :], axis=AX.X, op=ALU.add),
                 reads=[bj], writes=[bcl])
            ts("dve", cl[0:4, 3:4], cl[0:4, 2:3], 1.0 / 512, LN_EPS, ALU.mult, ALU.add, [bcl], [bcl])
            act(cl[0:4, 3:4], cl[0:4, 3:4], AF.Sqrt, [bcl], [bcl])
            P.op("dve", lambda e, cl=cl: e.reciprocal(out=cl[0:4, 3:4], in_=cl[0:4, 3:4]), reads=[bcl], writes=[bcl])
            stt("dve", gv, gv, cl[0:4, 3:4], sgug[0:4, :], ALU.mult, ALU.mult, [bgv, bcl, Bconst], [bgv])
            tt("dve", gv, gv, sgub_ln[0:4, :], ALU.add, [bgv, Bconst], [bgv])
            cp("pool", vnrow_b[:, :], gv, [bgv], [Bsm])
            dma("sp", g_s, gv, [bgv], [], "d_s6")
            pb, pbuf = bank("mm")
            for g in range(4):
                mm(pb[:, g * 4:(g + 1) * 4], vnrow_b[0:4, g * 128:(g + 1) * 128], dW[0:4, g, :], True, False, [Bsm], pbuf)
                mm(pb[:, g * 4:(g + 1) * 4], ones_b[0:1, :], sgub_row[0:1, g * 128:g * 128 + 1].to_broadcast([1, 4]),
                   False, True, [Bconst], pbuf)
            P.op("dve", lambda e, pb=pb: e.tensor_tensor(out=catT[:, 4:8, 0:4], in0=pb[:, 0:16].rearrange("p (g t) -> p g t", g=4),
                                                       in1=uT[:, :, 0:4], op=ALU.mult), reads=[pbuf] + BuT, writes=Bcat[4:8])
            if STOP == 12:
                return
            set_attn()
            for b in range(4):
                dma("sp", ptb[:, :], bass.AP(ptab.tensor, b * 128, [[0, 128], [1, 128]]), [], [Bidx], "d_s7", nonc=True)
                ts("dve", idx_all[:, :], ptb[:, :], 128.0, iota_f[:, 0:1], ALU.mult, ALU.add, [Bidx, Bsm], [Bidx])
                O, bO = bank("pv")
                SM, bSM = bank("pv")
                jl = list(range(129))
                if STOP in (14, 15, 16):
                    jl = list(range(8))
                if STOP == 17:
                    jl = list(range(8)) + [127, 128]
                if STOP == 18:
                    jl = list(range(8)) + [127]
                if STOP == 19:
                    jl = list(range(8)) + [128]
                jlast = jl[-1]
                for j in jl:
                    sl = j % 8
                    if j < 128:
                        kst, bkst = rot("stg", stg, Bstg)
                        semn = "d_stg%d" % (rr["stg"] % 3)
                        P.op("pool", lambda e, kst=kst, j=j: e.indirect_dma_start(
                            out=kst[:, :], out_offset=None, in_=cache_kv,
                            in_offset=bass.IndirectOffsetOnAxis(ap=idx_all[:, j:j + 1], axis=0)),
                            reads=[Bidx], writes=[bkst], dma_sem=semn)
                        cp("dve" if j % 2 else "act", Vt[:, sl, :], kst[:, 512:1024], [bkst], [BVt[sl]])
                        pbk, pbkb = bank("mm")
                        for h in range(4):
                            tr(pbk[:, h * 128:(h + 1) * 128], kst[:, h * 128:(h + 1) * 128], [bkst], pbkb)
                        cp("act" if j % 2 else "dve", KT[:, sl, :, :].rearrange("p h k -> p (h k)"), pbk[:, :], [pbkb], [BKT[sl]])
                        ktile, bkt, nkeys, vt_l = sl, BKT[sl], 128, None
                        if STOP in (14, 15):
                            continue
                    else:
                        ktile, bkt, nkeys = 15, BKT[15], 128
                    S, bS = bank("sc")
                    for h in range(4):
                        for c in range(2):
                            hc = h * 2 + c
                            special = (j >= 127)
                            mm(S[0:nkeys, hc:hc + 1], KT[c * 64:(c + 1) * 64, ktile, h, 0:nkeys],
                               qT[c * 64:(c + 1) * 64, h, b:b + 1], hc == 0, (hc == 7) and not special,
                               [bkt, BqT[h]], bS, skip=True)
                    if j == 127:
                        mm(S[:, 0:8], ident_b[:], BSl[:, :, :].rearrange("p h c -> p (h c)"), False, True, [Bconst, Bsm], bS, skip=True)
                    if j == 128:
                        mm(S[:, 0:8], ident_b[:], MBall[:, b, :], False, True, [Bconst, Bsm], bS, skip=True)
                    et, bet = rot("et", ET, BET)
                    act(et[0:nkeys, 0:8], S[0:nkeys, 0:8], AF.Exp, [bS], [bet])
                    for h in range(4):
                        if j < 128:
                            lhs = Vt[:, sl, h * 128:(h + 1) * 128]
                            rd = [BVt[sl], bet]
                        else:
                            lhs = vrow_b[:, h * 128:(h + 1) * 128]
                            rd = [Bsm, bet]
                        mm(O[:, 2 * h:2 * h + 2], lhs, et[0:nkeys, 2 * h:2 * h + 2], (j == 0 and h == 0), (j == jlast and h == 3),
                           rd, bO, skip=True)
                    mm(SM[:, 0:8], ones_b[0:nkeys, :], et[0:nkeys, 0:8], j == 0, j == jlast, [Bconst, bet], bSM, skip=True)
                if STOP in (14, 15):
                    continue
                rc, brc = rot("tmp", tmpf, Btmp)
                P.op("dve", lambda e, rc=rc, SM=SM: e.reciprocal(out=rc[:, 0:8], in_=SM[:, 0:8]), reads=[bSM], writes=[brc])
                tt("dve", rc[:, 0:8], O[:, 0:8], rc[:, 0:8], ALU.mult, [bO, brc], [brc])
                rv = rc[:, 0:8].rearrange("p (h c) -> p h c", c=2)
                stt("dve", rc[:, 8:12], rv[:, :, 1], neglam[:, 0:1], rv[:, :, 0], ALU.mult, ALU.add, [brc, Bconst], [brc])
                sqh, bsqh = rot("et", ET, BET)
                act(sqh[:, 0:4], rc[:, 8:12], AF.Square, [brc], [bsqh])
                pb, pbuf = bank("mm")
                mm(pb[:, 0:4], ones_b[:], sqh[:, 0:4], True, True, [Bconst, bsqh], pbuf)
                ts("dve", rc[:, 16:20], pb[:, 0:4], 1.0 / 128, LN_EPS, ALU.mult, ALU.add, [pbuf, brc], [brc])
                act(rc[:, 16:20], rc[:, 16:20], AF.Sqrt, [brc], [brc])
                P.op("dve", lambda e, rc=rc: e.reciprocal(out=rc[:, 16:20], in_=rc[:, 16:20]), reads=[brc], writes=[brc])
                tt("dve", rc[:, 16:20], rc[:, 16:20], rc[:, 8:12], ALU.mult, [brc], [brc])
                P.op("act", lambda e, rc=rc, b=b: e.activation(out=catT[:, 0:4, b:b + 1], in_=rc[:, 16:20].unsqueeze(2),
                                                              func=AF.Copy, scale=sublg[:, 0:1]),
                     reads=[brc, Bconst], writes=Bcat[0:4])
            if STOP in (13, 14, 15, 16, 17, 18, 19):
                return
            set_dense()
            proj_res(N, w_mo, catT, Bcat, "mo", first)
            layer_norm(N, 1)
            for pi in range(2):
                slot, bs = get_piece(("xq", pi), [(w_xq, 0, 8, 512 * pi, 512, 0)], first)
                sv = slot[:, :].rearrange("p (k c) -> p k c", k=8)
                for cc in range(4):
                    j = 4 * pi + cc
                    pb, pbuf = bank("mm")
                    for k in range(8):
                        mm(pb[:, 0:N], sv[:, k, cc * 128:(cc + 1) * 128], xb[:, k, 0:N], k == 0, k == 7,
                           [bs, Bxb[k]], pbuf)
                    act(hbuf[:, j, 0:N], pb[:, 0:N], AF.Copy, [pbuf], [Bh[j]], scale=1.0 / 16)
            set_attn()
            for b in range(4):
                for mc in range(2):
                    st_, bst_ = rot("stg", stg, Bstg)
                    dma("sp", st_[:, :], cmk[b, mc * 128:(mc + 1) * 128, :], [], [bst_], "d_stg%d" % (rr["stg"] % 3))
                    for kk in range(2):
                        pb, pbuf = bank("mm")
                        for q4 in range(4):
                            k = kk * 4 + q4
                            tr(pb[:, q4 * 128:(q4 + 1) * 128], st_[:, k * 128:(k + 1) * 128], [bst_], pbuf)
                        P.op("dve" if kk else "act", (lambda e, pb=pb, kk=kk, mc=mc: e.tensor_copy(
                            out=memKT[:, kk * 4:(kk + 1) * 4, mc * 128:(mc + 1) * 128],
                            in_=pb[:, :].rearrange("p (q m) -> p q m", q=4))) if kk else
                            (lambda e, pb=pb, kk=kk, mc=mc: e.copy(
                                out=memKT[:, kk * 4:(kk + 1) * 4, mc * 128:(mc + 1) * 128],
                                in_=pb[:, :].rearrange("p (q m) -> p q m", q=4))),
                            reads=[pbuf], writes=[Bconst])
                    st2, bst2 = rot("stg", stg, Bstg)
                    dma("sp", st2[:, :], cmv[b, mc * 128:(mc + 1) * 128, :], [], [bst2], "d_stg%d" % (rr["stg"] % 3))
                    cp("pool", memV[:, mc, :], st2[:, :], [bst2], [Bconst])
                if STOP != 22:
                    cross_core(b, b + 1)
            if STOP in (20, 22):
                return
            set_dense()
            proj_res(N, w_xo, hbuf[:, 8:16, :], Bh[8:16], "xo", first)
            layer_norm(N, 2)
            ffn(N, w_f2i, w_f2o, 3, first, "f2")
            if STOP == 21:
                return
            st_, bst_ = rot("stg", stg, Bstg)
            for kk in range(2):
                pb, pbuf = bank("mm")
                for q4 in range(4):
                    k = kk * 4 + q4
                    mm(pb[0:4, q4 * 128:(q4 + 1) * 128], x[:, k, 0:4], ident_f[:, :], True, True, [Bx[k], Bconst], pbuf)
                cp("dve", st_[0:4, kk * 512:(kk + 1) * 512], pb[0:4, :], [pbuf], [bst_])
            dma("sp", y_s, st_[0:4, :], [bst_], [], "d_stg%d" % (rr["stg"] % 3))

        def prompt_tile(ti, first):
            N = TT
            t0 = ti * TT
            if ti == 4:
                for eng_ in ("act", "dve", "pool", "pe"):
                    P.op(eng_, None, writes=[Bwst[0], Bwst[1]])
            for s in range(4):
                st_, bst_ = rot("stg", stg, Bstg)
                dma("sp", st_[:, :], xp[t0 + s * 128:t0 + (s + 1) * 128, :], [], [bst_], "d_stg%d" % (rr["stg"] % 3))
                for kk in range(2):
                    pb, pbuf = bank("mm")
                    for q4 in range(4):
                        k = kk * 4 + q4
                        tr(pb[:, q4 * 128:(q4 + 1) * 128], st_[:, k * 128:(k + 1) * 128], [bst_], pbuf)
                    for q4 in range(4):
                        k = kk * 4 + q4
                        e1 = "dve" if kk else "act"
                        cp(e1, x[:, k, s * 128:(s + 1) * 128], pb[:, q4 * 128:(q4 + 1) * 128], [pbuf], [Bx[k]])
            for k in range(8):
                cp(("pool", "dve", "act", "pool")[k % 4], xb[:, k, :], x[:, k, :], [Bx[k]], [Bxb[k]])
            ffn(N, w_f1i, w_f1o, 0, first, "f1")
            slot, bs = get_piece(("mi", 0), [(w_mi, 0, 8, 0, 512, 0)], first)
            sv = slot[:, :].rearrange("p (k c) -> p k c", k=8)
            for h in range(4):
                pb, pbuf = bank("mm")
                for k in range(8):
                    mm(pb[:, 0:N], sv[:, k, h * 128:(h + 1) * 128], xb[:, k, 0:N], k == 0, k == 7, [bs, Bxb[k]], pbuf)
                act(qT[:, h, :], pb[:, 0:N], AF.Copy, [pbuf], [BqT[h]], scale=0.125)
            slot, bs = get_piece(("mi", 1), [(w_mi, 0, 8, 512, 512, 0)], first)
            sv = slot[:, :].rearrange("p (k c) -> p k c", k=8)
            for h in range(4):
                pb, pbuf = bank("mm")
                for k in range(8):
                    mm(pb[:, 0:N], sv[:, k, h * 128:(h + 1) * 128], xb[:, k, 0:N], k == 0, k == 7, [bs, Bxb[k]], pbuf)
                for s in range(4):
                    kt = ti * 4 + s
                    cp("act" if h % 2 else "dve", KT[:, kt, h, :], pb[:, s * 128:(s + 1) * 128], [pbuf], [BKT[kt]])
            for s in range(4):
                pb, pbuf = bank("mm")
                for k in range(8):
                    mm(pb[:, :], xb[:, k, s * 128:(s + 1) * 128], sv[:, k, :], k == 0, k == 7, [bs, Bxb[k]], pbuf)
                st_, bst_ = rot("stg", stg, Bstg)
                cp("act", st_[:, 0:512], pb[:, :], [pbuf], [bst_])
                dma("sp", k_p[t0 + s * 128:t0 + (s + 1) * 128, :], st_[:, 0:512], [bst_], [], "d_stg%d" % (rr["stg"] % 3))
            slot, bs = get_piece(("mi", 2), [(w_mi, 0, 8, 1024, 512, 0)], first)
            sv = slot[:, :].rearrange("p (k c) -> p k c", k=8)
            for s in range(4):
                kt = ti * 4 + s
                pb, pbuf = bank("mm")
                for k in range(8):
                    mm(pb[:, :], xb[:, k, s * 128:(s + 1) * 128], sv[:, k, :], k == 0, k == 7, [bs, Bxb[k]], pbuf)
                st_, bst_ = rot("stg", stg, Bstg)
                cp("act", st_[:, 0:512], pb[:, :], [pbuf], [bst_])
                cp("pool", Vt[:, kt, :], st_[:, 0:512], [bst_], [BVt[kt]])
                dma("sp", v_p[t0 + s * 128:t0 + (s + 1) * 128, :], st_[:, 0:512], [bst_], [], "d_stg%d" % (rr["stg"] % 3))
            slot, bs = get_piece(("mi", 3), [(w_mi, 0, 8, 1536, 512, 0)], first)
            sv = slot[:, :].rearrange("p (k c) -> p k c", k=8)
            for g in range(4):
                pb, pbuf = bank("mm")
                for k in range(8):
                    mm(pb[:, 0:N], sv[:, k, g * 128:(g + 1) * 128], xb[:, k, 0:N], k == 0, k == 7, [bs, Bxb[k]], pbuf)
                act(uT[:, g, :], pb[:, 0:N], AF.Gelu, [pbuf], BuT2[g])
            slot, bs = get_piece(("mi", 4), [(w_mi, 0, 8, 2048, 512, 0)], first)
            sv = slot[:, :].rearrange("p (k c) -> p k c", k=8)
            vbs = []
            for s in range(4):
                pb, pbuf = bank("mm")
                for k in range(8):
                    mm(pb[:, :], xb[:, k, s * 128:(s + 1) * 128], sv[:, k, :], k == 0, k == 7, [bs, Bxb[k]], pbuf)
                gv, bgv = tmpf[s], Btmp[s]
                cl, bcl = col[s], Bcol[s]
                act(gv[:, :], pb[:, :], AF.Gelu, [pbuf], [bgv])
                vbs.append((gv, bgv, cl, bcl))
            for s in range(4):
                gv, bgv, cl, bcl = vbs[s]
                P.op("dve", lambda e, gv=gv, cl=cl: e.tensor_reduce(out=cl[:, 0:1], in_=gv[:, :], axis=AX.X, op=ALU.add),
                     reads=[bgv], writes=[bcl])
                ts("dve", cl[:, 1:2], cl[:, 0:1], -1.0 / 512, None, ALU.mult, None, [bcl], [bcl])
                ts("dve", gv[:, :], gv[:, :], cl[:, 1:2], None, ALU.add, None, [bgv, bcl], [bgv])
                tt("pool", oacc[:, :], gv[:, :], gv[:, :], ALU.mult, [bgv], [Boacc])
                P.op("dve", lambda e, cl=cl: e.tensor_reduce(out=cl[:, 2:3], in_=oacc[:, :], axis=AX.X, op=ALU.add),
                     reads=[Boacc], writes=[bcl])
                ts("dve", cl[:, 3:4], cl[:, 2:3], 1.0 / 512, LN_EPS, ALU.mult, ALU.add, [bcl], [bcl])
            for s in range(4):
                gv, bgv, cl, bcl = vbs[s]
                act(cl[:, 3:4], cl[:, 3:4], AF.Sqrt, [bcl], [bcl])
            for s in range(4):
                gv, bgv, cl, bcl = vbs[s]
                P.op("dve", lambda e, cl=cl: e.reciprocal(out=cl[:, 3:4], in_=cl[:, 3:4]), reads=[bcl], writes=[bcl])
                stt("dve", gv[:, :], gv[:, :], cl[:, 3:4], sgug[:, :], ALU.mult, ALU.mult, [bgv, bcl, Bconst], [bgv])
                tt("pool", vn[:, s, :], gv[:, :], sgub_ln[:, :], ALU.add, [bgv, Bconst], [Bvn[s]])
            set_attn()
            nk = 4 * (ti + 1)
            rms_st = {}

            def rms_a(h):
                oa, boa = (oacc, Boacc) if h % 2 == 0 else (oacc2, Boacc2)
                sqh, bsqh = rot("et", ET, BET)
                act(sqh[:, :], oa[:, :], AF.Square, [boa], [bsqh])
                pb, pbuf = bank("mm")
                mm(pb[:, :], ones_b[:], sqh[:, :], True, True, [Bconst, bsqh], pbuf)
                rs_, brs = rot("tmp", tmpf, Btmp)
                ts("dve", rs_[:, :], pb[:, :], 1.0 / 128, LN_EPS, ALU.mult, ALU.add, [pbuf], [brs])
                rms_st[h] = (rs_, brs, oa, boa)

            def rms_b(h):
                rs_, brs, oa, boa = rms_st[h]
                act(rs_[:, :], rs_[:, :], AF.Ln, [brs], [brs])
                act(rs_[:, :], rs_[:, :], AF.Exp, [brs], [brs], scale=-0.5)
                tt("dve", rs_[:, :], rs_[:, :], oa[:, :], ALU.mult, [brs, boa], [brs])

            def rms_c(h):
                rs_, brs, oa, boa = rms_st[h]
                act(catT[:, h, :], rs_[:, :], AF.Copy, [brs, Bconst], [Bcat[h]], scale=sublg[:, 0:1])

            def rms_head(h):
                rms_a(h)
                rms_b(h)
                rms_c(h)

            for h in range(4):
                oa, boa = (oacc, Boacc) if h % 2 == 0 else (oacc2, Boacc2)
                for c in range(2):
                    if c == 1 and h >= 1:
                        rms_a(h - 1)
                    O, bO = bank("pv")
                    SM, bSM = bank("pv")
                    for kt in range(nk):
                        r = kt - 4 * ti
                        q0 = max(r, 0) * 128
                        S, bS = bank("sc")
                        spec = None
                        if kt == 4 * ti - 1:
                            spec = (0, 128, 128)
                        elif r >= 0:
                            ln_ = min(256, 512 - q0)
                            spec = (q0, ln_, 0)
                        mm(S[:, q0:512], KT[c * 64:(c + 1) * 64, kt, h, :], qT[c * 64:(c + 1) * 64, h, q0:512],
                           True, spec is None, [BKT[kt], BqT[h]], bS, skip=True)
                        if spec is not None:
                            cs, ln_, bo = spec
                            mm(S[:, cs:cs + ln_], ident_b[:], BT[:, h, bo:bo + ln_], False, True, [Bconst], bS, skip=True)
                        et, bet = rot("et", ET, BET)
                        act(et[:, q0:512], S[:, q0:512], AF.Exp, [bS, Bconst], [bet], bias=far[:, h:h + 1], scale=1.0)
                        mm(O[:, q0:512], Vt[:, kt, h * 128:(h + 1) * 128], et[:, q0:512], kt == 0, kt == nk - 1,
                           [BVt[kt], bet], bO, skip=True)
                        mm(SM[:, q0:512], ones_b[:], et[:, q0:512], kt == 0, kt == nk - 1, [Bconst, bet], bSM, skip=True)
                        if c == 1 and h >= 1 and kt == 1:
                            rms_b(h - 1)
                        if c == 1 and h >= 1 and kt == 3:
                            rms_c(h - 1)
                    rc, brc = rot("tmp", tmpf, Btmp)
                    P.op("dve", lambda e, rc=rc, SM=SM: e.reciprocal(out=rc[:, :], in_=SM[:, :]), reads=[bSM], writes=[brc])
                    if c == 0:
                        tt("dve", oa[:, :], O[:, :], rc[:, :], ALU.mult, [bO, brc], [boa])
                    else:
                        tt("dve", rc[:, :], O[:, :], rc[:, :], ALU.mult, [bO, brc], [brc])
                        stt("dve", oa[:, :], rc[:, :], neglam[:, 0:1], oa[:, :], ALU.mult, ALU.add,
                            [brc, boa, Bconst], [boa])
            rms_head(3)
            for s in range(4):
                pb, pbuf = bank("mm")
                for g in range(4):
                    mm(pb[:, g * 128:(g + 1) * 128], vn[:, s, g * 128:(g + 1) * 128], trilWT[:, g, :], True, False,
                       [Bvn[s], Bconst], pbuf)
                    mm(pb[:, g * 128:(g + 1) * 128], ones_b[0:1, :], sgub_row[0:1, g * 128:(g + 1) * 128], False, True,
                       [Bconst], pbuf)
                P.op("dve", lambda e, pb=pb, s=s: e.tensor_tensor(
                    out=catT[:, 4:8, s * 128:(s + 1) * 128],
                    in0=pb[:, :].rearrange("p (g t) -> p g t", g=4),
                    in1=uT[:, :, s * 128:(s + 1) * 128], op=ALU.mult),
                    reads=[pbuf] + BuT, writes=Bcat[4:8])
            set_dense()
            proj_res(N, w_mo, catT, Bcat, "mo", first)
            layer_norm(N, 1)
            for pi in range(2):
                slot, bs = get_piece(("xq", pi), [(w_xq, 0, 8, 512 * pi, 512, 0)], first)
                sv = slot[:, :].rearrange("p (k c) -> p k c", k=8)
                for cc in range(4):
                    j = 4 * pi + cc
                    pb, pbuf = bank("mm")
                    for k in range(8):
                        mm(pb[:, 0:N], sv[:, k, cc * 128:(cc + 1) * 128], xb[:, k, 0:N], k == 0, k == 7,
                           [bs, Bxb[k]], pbuf)
                    act(hbuf[:, j, :], pb[:, 0:N], AF.Copy, [pbuf], [Bh[j]], scale=1.0 / 16)
            set_attn()
            cross_core(0, N)
            set_dense()
            proj_res(N, w_xo, hbuf[:, 8:16, :], Bh[8:16], "xo", first)
            layer_norm(N, 2)
            ffn(N, w_f2i, w_f2o, 3, first, "f2")
            for s in range(4):
                st_, bst_ = rot("stg", stg, Bstg)
                for kk in range(2):
                    pb, pbuf = bank("mm")
                    for q4 in range(4):
                        k = kk * 4 + q4
                        tr(pb[:, q4 * 128:(q4 + 1) * 128], x[:, k, s * 128:(s + 1) * 128], [Bx[k]], pbuf)
                    cp("act" if kk else "dve", st_[:, kk * 512:(kk + 1) * 512], pb[:, :], [pbuf], [bst_])
                dma("sp", y_p[t0 + s * 128:t0 + (s + 1) * 128, :], st_[:, :], [bst_], [], "d_stg%d" % (rr["stg"] % 3))


        P.barrier()
        sample_phase()
        if STOP >= 10:
            P.emit(nc); return nc
        P.barrier()
        mem_phase()
        if STOP in (5, 6):
            P.emit(nc); return nc
        for ti in range(NT_RUN):
            prompt_tile(ti, False)
        P.emit(nc)
    return nc


NT_RUN = NT
STOP = 0


def kernel(**inp):
    f32 = np.float32
    consts = _host_consts()
    nc = build_nc()
    lam_in = np.stack([inp["lambda_q1"][0], inp["lambda_k1"][0], inp["lambda_q2"][0], inp["lambda_k2"][0]]).astype(f32)
    shared = {
        "rel_bias": np.ascontiguousarray(inp["rel_bias"], dtype=f32),
        "ln_g": np.ascontiguousarray(inp["ln_g"][0]),
        "ln_b": np.ascontiguousarray(inp["ln_b"][0]),
        "ffn1_w_in": np.ascontiguousarray(inp["ffn1_w_in"][0]),
        "ffn1_w_out": np.ascontiguousarray(inp["ffn1_w_out"][0]),
        "w_mix_in": np.ascontiguousarray(inp["w_mix_in"][0]),
        "w_mix_out": np.ascontiguousarray(inp["w_mix_out"][0]),
        "lam_in": lam_in,
        "subln_g": np.ascontiguousarray(inp["subln_g"][0].reshape(128, 1)),
        "sgu_ln_g": np.ascontiguousarray(inp["sgu_ln_g"][0].reshape(1, 512)),
        "sgu_ln_b": np.ascontiguousarray(inp["sgu_ln_b"][0].reshape(1, 512)),
        "sgu_w": np.ascontiguousarray(inp["sgu_w"][0]),
        "sgu_b": np.ascontiguousarray(inp["sgu_b"][0].reshape(1, 512)),
        "xq_w": np.ascontiguousarray(inp["xq_w"][0]),
        "xkv_w": np.ascontiguousarray(inp["xkv_w"][0]),
        "xo_w": np.ascontiguousarray(inp["xo_w"][0]),
        "ffn2_w_in": np.ascontiguousarray(inp["ffn2_w_in"][0]),
        "ffn2_w_out": np.ascontiguousarray(inp["ffn2_w_out"][0]),
    }
    shared.update(consts)
    kv = np.empty((NPHYS * 128, 1024), dtype=f32)
    kv[:, 0:512] = np.asarray(inp["cache_k"]).reshape(NPHYS * 128, 512)
    kv[:, 512:1024] = np.asarray(inp["cache_v"]).reshape(NPHYS * 128, 512)
    shared["cache_kv"] = kv
    in_maps = []
    for c in range(8):
        m = dict(shared)
        m["xp"] = np.ascontiguousarray(inp["x_prompt"][c])
        m["mem"] = np.ascontiguousarray(inp["mem_prompt"][c])
        m["xs"] = np.ascontiguousarray(inp["x_sample"][NS * c:NS * (c + 1), 0, :])
        m["cmk"] = np.ascontiguousarray(inp["cache_mem_k"][0, NS * c:NS * (c + 1)]).reshape(NS, 256, D)
        m["cmv"] = np.ascontiguousarray(inp["cache_mem_v"][0, NS * c:NS * (c + 1)]).reshape(NS, 256, D)
        m["ptab"] = np.ascontiguousarray(inp["page_table"][NS * c:NS * (c + 1)]).astype(np.int32)
        in_maps.append(m)
    res = run_bass_kernel_spmd(nc, in_maps, core_ids=list(range(8)))
    R = res.results
    y_p = np.stack([R[c]["y_p"] for c in range(8)])
    k_p = np.stack([R[c]["k_p"] for c in range(8)]).reshape(1, 8, SEQ, 4, 128)
    v_p = np.stack([R[c]["v_p"] for c in range(8)]).reshape(1, 8, SEQ, 4, 128)
    mk_p = np.stack([R[c]["mk_p"] for c in range(8)]).reshape(1, 8, 256, 4, 256)
    mv_p = np.stack([R[c]["mv_p"] for c in range(8)]).reshape(1, 8, 256, 4, 256)
    y_s = np.concatenate([R[c]["y_s"] for c in range(8)]).reshape(32, 1, D)
    k_s = np.concatenate([R[c]["k_s"] for c in range(8)]).reshape(1, 32, 1, 4, 128)
    v_s = np.concatenate([R[c]["v_s"] for c in range(8)]).reshape(1, 32, 1, 4, 128)
    g_s = np.concatenate([R[c]["g_s"] for c in range(8)]).reshape(1, 32, 1, 512)
    return (y_p, y_s, k_p, v_p, mk_p, mv_p, k_s, v_s, g_s)
```
